# Optimizing a Trainium2 kernel written in Bass

```python
import math, functools
import jax, jax.numpy as jnp
from jax import lax
import numpy as np


D_MODEL = 2048
BATCH = 4
SEQ = 2048
DEPTH = 1
DEC_BATCH = 128
DEC_SEQ = 4
PAST_LEN = 8192
PAGE_SIZE = 128

HEAD_DIM = 64
ATTN_WIDTH = D_MODEL // 2
N_HEADS = ATTN_WIDTH // HEAD_DIM
N_KV_HEADS = max(1, N_HEADS // 8)
GROUP = N_HEADS // N_KV_HEADS
WINDOW = 128
BLOCK = 128
GMLP_WIDTH = D_MODEL - ATTN_WIDTH
GMLP_HEADS = 8
GMLP_HEAD_DIM = GMLP_WIDTH // GMLP_HEADS
CHUNK = 128
D_FF = 4 * D_MODEL
NUM_BUCKETS = 32
MAX_DISTANCE = 128
EPS = 1e-6

Q_COLS = N_HEADS * HEAD_DIM
KV_COLS = N_KV_HEADS * HEAD_DIM
IN_COLS = Q_COLS + 2 * KV_COLS + 2 * GMLP_WIDTH
SPLITS = (Q_COLS, Q_COLS + KV_COLS, Q_COLS + 2 * KV_COLS, Q_COLS + 2 * KV_COLS + GMLP_WIDTH)

kernel_name = 'hymba_swa_sink_gmlp_decoder_step'


def rmsnorm(x, gain):
    x32 = x.astype(jnp.float32)
    y = x32 * lax.rsqrt(jnp.mean(x32 * x32, axis=-1, keepdims=True) + EPS)
    return (y * gain.astype(jnp.float32)).astype(x.dtype)


def adaln_params(c, w_ada, b_ada):
    mod = jax.nn.silu(c) @ w_ada + b_ada
    return jnp.split(mod[:, None, :], 6, axis=-1)


def t5_bucket(dist):
    n = jnp.maximum(dist, 0)
    max_exact = NUM_BUCKETS // 2
    nf = jnp.maximum(n, 1).astype(jnp.float32)
    large = max_exact + (jnp.log(nf / max_exact) / math.log(MAX_DISTANCE / max_exact)
                         * (NUM_BUCKETS - max_exact)).astype(jnp.int32)
    large = jnp.minimum(large, NUM_BUCKETS - 1)
    return jnp.where(n < max_exact, n, large)


def rel_bias(dist, rel_table):
    b = rel_table[t5_bucket(dist)].astype(jnp.float32)
    b = jnp.transpose(b, (2, 0, 1))
    return b.reshape(N_KV_HEADS, GROUP, dist.shape[0], dist.shape[1])


def sink_attend(q, k, v, bias, valid, sinks):
    scale = HEAD_DIM ** -0.5
    logits = jnp.einsum('...qkgd,...jkd->...kgqj', q, k).astype(jnp.float32) * scale + bias
    logits = jnp.where(valid, logits, -1e30)
    sink = sinks.astype(jnp.float32).reshape(N_KV_HEADS, GROUP)[:, :, None, None]
    m = jnp.maximum(jnp.max(logits, axis=-1, keepdims=True), sink)
    p = jnp.exp(logits - m)
    probs = p / (jnp.sum(p, axis=-1, keepdims=True) + jnp.exp(sink - m))
    return jnp.einsum('...kgqj,...jkd->...qkgd', probs.astype(v.dtype), v)


def attention_prompt(q, k, v, rel_table, sinks):
    B, S = q.shape[:2]
    nb = S // BLOCK
    qb = q.reshape(B, nb, BLOCK, N_KV_HEADS, GROUP, HEAD_DIM)
    pad = ((0, 0), (BLOCK, 0), (0, 0), (0, 0))
    k_prev = jnp.pad(k, pad)[:, :S].reshape(B, nb, BLOCK, N_KV_HEADS, HEAD_DIM)
    v_prev = jnp.pad(v, pad)[:, :S].reshape(B, nb, BLOCK, N_KV_HEADS, HEAD_DIM)
    kb = jnp.concatenate([k_prev, k.reshape(B, nb, BLOCK, N_KV_HEADS, HEAD_DIM)], axis=2)
    vb = jnp.concatenate([v_prev, v.reshape(B, nb, BLOCK, N_KV_HEADS, HEAD_DIM)], axis=2)
    qi = jnp.arange(BLOCK)[:, None]
    kj = jnp.arange(2 * BLOCK)[None, :]
    dist = BLOCK + qi - kj
    key_pos = jnp.arange(nb)[:, None, None] * BLOCK - BLOCK + kj[None]
    valid = (dist >= 0) & (dist < WINDOW) & (key_pos >= 0)
    out = sink_attend(qb, kb, vb, rel_bias(dist, rel_table), valid[:, None, None], sinks)
    return out.reshape(B, S, ATTN_WIDTH)


def attention_sample(q, k, v, cache_k, cache_v, rel_table, sinks):
    DB, T = q.shape[:2]
    W = cache_k.shape[1]
    kk = jnp.concatenate([cache_k, k], axis=1)
    vv = jnp.concatenate([cache_v, v], axis=1)
    dist = W + jnp.arange(T)[:, None] - jnp.arange(W + T)[None, :]
    valid = (dist >= 0) & (dist < WINDOW)
    out = sink_attend(q, kk, vv, rel_bias(dist, rel_table), valid, sinks)
    return out.reshape(DB, T, ATTN_WIDTH)


def gmlp_spatial_gate(u, v, w_s, b_s, v_gain):
    B, S, _ = u.shape
    L = min(S, CHUNK)
    n = S // L
    u = u.reshape(B, n, L, GMLP_HEADS, GMLP_HEAD_DIM)
    v = rmsnorm(v.reshape(B, n, L, GMLP_HEADS, GMLP_HEAD_DIM), v_gain)
    w = jnp.tril(w_s[:, :L, :L])
    mixed = jnp.einsum('hij,bnjhc->bnihc', w, v) + jnp.transpose(b_s[:, :L])[None, None, :, :, None]
    out = (u * mixed).reshape(B, S, GMLP_WIDTH)
    return out, v.reshape(B, S, GMLP_HEADS, GMLP_HEAD_DIM)


def trunk_layer(x, c, attend, w_ada, b_ada, g_pre_mix, w_in, gmlp_v_gain, gmlp_w_s, gmlp_b_s,
                g_attn_out, g_gmlp_out, w_out, g_post_mix, g_pre_ff, w_ff1, w_ff2, g_post_ff):
    B, S, _ = x.shape
    sh_m, sc_m, gt_m, sh_f, sc_f, gt_f = adaln_params(c, w_ada, b_ada)
    h = rmsnorm(x, g_pre_mix) * (1 + sc_m) + sh_m
    q, k, v, gu, gv = jnp.split(h @ w_in, SPLITS, axis=-1)
    q = q.reshape(B, S, N_KV_HEADS, GROUP, HEAD_DIM)
    k = k.reshape(B, S, N_KV_HEADS, HEAD_DIM)
    v = v.reshape(B, S, N_KV_HEADS, HEAD_DIM)
    attn_out = attend(q, k, v)
    gmlp_out, v_rows = gmlp_spatial_gate(jax.nn.gelu(gu), jax.nn.gelu(gv), gmlp_w_s, gmlp_b_s, gmlp_v_gain)
    merged = jnp.concatenate([rmsnorm(attn_out, g_attn_out), rmsnorm(gmlp_out, g_gmlp_out)], axis=-1)
    x = x + gt_m * rmsnorm(merged @ w_out, g_post_mix)
    h = rmsnorm(x, g_pre_ff) * (1 + sc_f) + sh_f
    f = jnp.square(jax.nn.relu(h @ w_ff1)) @ w_ff2
    x = x + gt_f * rmsnorm(f, g_post_ff)
    return x, k, v, v_rows


def setup_inputs(seed: int = 0) -> dict:
    key = jax.random.key(seed)
    ks = jax.random.split(key, 24)
    f32 = jnp.float32

    def nrm(k, shape, scale):
        return jax.random.normal(k, shape, f32) * scale

    cw = min(WINDOW, PAST_LEN)
    d_s = D_MODEL ** -0.5
    return {
        'x_prompt': nrm(ks[0], (BATCH, SEQ, D_MODEL), 1.0),
        'x_sample': nrm(ks[1], (DEC_BATCH, DEC_SEQ, D_MODEL), 1.0),
        'cache_k': nrm(ks[2], (DEPTH, DEC_BATCH, cw, N_KV_HEADS, HEAD_DIM), 1.0),
        'cache_v': nrm(ks[3], (DEPTH, DEC_BATCH, cw, N_KV_HEADS, HEAD_DIM), 1.0),
        'c_prompt': nrm(ks[4], (BATCH, D_MODEL), 1.0),
        'c_sample': nrm(ks[5], (DEC_BATCH, D_MODEL), 1.0),
        'rel_bias_table': nrm(ks[6], (NUM_BUCKETS, N_HEADS), 0.5),
        'w_ada': nrm(ks[7], (DEPTH, D_MODEL, 6 * D_MODEL), 0.5 * d_s),
        'b_ada': nrm(ks[8], (DEPTH, 6 * D_MODEL), 0.02),
        'g_pre_mix': 1.0 + nrm(ks[9], (DEPTH, D_MODEL), 0.05),
        'w_in': nrm(ks[10], (DEPTH, D_MODEL, IN_COLS), d_s),
        'attn_sinks': nrm(ks[11], (DEPTH, N_HEADS), 1.0),
        'gmlp_v_gain': 1.0 + nrm(ks[12], (DEPTH, GMLP_HEADS, GMLP_HEAD_DIM), 0.05),
        'gmlp_w_s': nrm(ks[13], (DEPTH, GMLP_HEADS, CHUNK, CHUNK), CHUNK ** -0.5),
        'gmlp_b_s': 1.0 + nrm(ks[14], (DEPTH, GMLP_HEADS, CHUNK), 0.05),
        'g_attn_out': 1.0 + nrm(ks[15], (DEPTH, ATTN_WIDTH), 0.05),
        'g_gmlp_out': 1.0 + nrm(ks[16], (DEPTH, GMLP_WIDTH), 0.05),
        'w_out': nrm(ks[17], (DEPTH, D_MODEL, D_MODEL), d_s),
        'g_post_mix': 1.0 + nrm(ks[18], (DEPTH, D_MODEL), 0.05),
        'g_pre_ff': 1.0 + nrm(ks[19], (DEPTH, D_MODEL), 0.05),
        'w_ff1': nrm(ks[20], (DEPTH, D_MODEL, D_FF), d_s),
        'w_ff2': nrm(ks[21], (DEPTH, D_FF, D_MODEL), D_FF ** -0.5),
        'g_post_ff': 1.0 + nrm(ks[22], (DEPTH, D_MODEL), 0.05),
    }


def reference(x_prompt, x_sample, cache_k, cache_v, c_prompt, c_sample, rel_bias_table, w_ada, b_ada,
              g_pre_mix, w_in, attn_sinks, gmlp_v_gain, gmlp_w_s, gmlp_b_s, g_attn_out, g_gmlp_out,
              w_out, g_post_mix, g_pre_ff, w_ff1, w_ff2, g_post_ff):
    y_p, y_s = x_prompt, x_sample
    kp_rows, vp_rows, ks_rows, vs_rows, gv_rows = [], [], [], [], []
    cw_prompt = min(WINDOW, x_prompt.shape[1])
    for l in range(DEPTH):
        weights = (w_ada[l], b_ada[l], g_pre_mix[l], w_in[l], gmlp_v_gain[l], gmlp_w_s[l], gmlp_b_s[l],
                   g_attn_out[l], g_gmlp_out[l], w_out[l], g_post_mix[l], g_pre_ff[l], w_ff1[l], w_ff2[l],
                   g_post_ff[l])
        attend_p = functools.partial(attention_prompt, rel_table=rel_bias_table, sinks=attn_sinks[l])
        attend_s = functools.partial(attention_sample, cache_k=cache_k[l], cache_v=cache_v[l],
                                     rel_table=rel_bias_table, sinks=attn_sinks[l])
        y_p, k_p, v_p, _ = trunk_layer(y_p, c_prompt, attend_p, *weights)
        y_s, k_s, v_s, gv_s = trunk_layer(y_s, c_sample, attend_s, *weights)
        kp_rows.append(k_p[:, k_p.shape[1] - cw_prompt:])
        vp_rows.append(v_p[:, v_p.shape[1] - cw_prompt:])
        ks_rows.append(k_s)
        vs_rows.append(v_s)
        gv_rows.append(gv_s)
    new_k_prompt = jnp.stack(kp_rows, axis=0)
    new_v_prompt = jnp.stack(vp_rows, axis=0)
    new_k_sample = jnp.stack(ks_rows, axis=0)
    new_v_sample = jnp.stack(vs_rows, axis=0)
    gmlp_v_sample = jnp.stack(gv_rows, axis=0)
    return (y_p, y_s, new_k_prompt, new_v_prompt, new_k_sample, new_v_sample, gmlp_v_sample)
```

```python
import bisect
import math
import numpy as np
import concourse.bass as bass
import concourse.mybir as mybir
from concourse.bass_utils import run_bass_kernel_spmd

F32 = mybir.dt.float32
BF16 = mybir.dt.bfloat16
AF = mybir.ActivationFunctionType
ALU = mybir.AluOpType
AX = mybir.AxisListType

NCORES = 8
D = 2048
TP, TS, TH = 1024, 64, 128
TM = TP + TS
TT = TM + TH
GROUPS = [(0, 512), (512, 512), (1024, 64)]
EPS = 1e-6
NEG = -30000.0
UNITW = 4096


class Buf:
    __slots__ = ("name", "last_write", "readers", "dsem", "dcount", "excl")

    def __init__(self, name, excl=False):
        self.name = name
        self.excl = excl
        self.last_write = None
        self.readers = []
        self.dsem = None
        self.dcount = 0


class _Op:
    __slots__ = ("fn", "waits", "inc", "is_dma", "dma_sem")

    def __init__(self, fn, is_dma=False, dma_sem=None):
        self.fn = fn
        self.waits = []
        self.inc = None
        self.is_dma = is_dma
        self.dma_sem = dma_sem


class _Eng:
    def __init__(self, name, sem):
        self.name = name
        self.sem = sem
        self.count = 0
        self.ops = []
        self.inc_seqs = []
        self.inc_vals = []
        self.waited = {}
        self.last_compute = -1


class FW:
    SAME_ENGINE_SYNC = True

    def __init__(self, nc):
        self.nc = nc
        self.engs = {}
        for n in ("pe", "act", "dve", "pool", "sp"):
            self.engs[n] = _Eng(n, nc.alloc_semaphore("s_" + n))
        self.nsem = 5

    def _resolve(self, t):
        if t[0] == "d":
            return t[1], t[2]
        E = self.engs[t[1]]
        seq = t[2]
        i = bisect.bisect_left(E.inc_seqs, seq)
        if i < len(E.inc_seqs):
            return E.sem, E.inc_vals[i]
        s = E.last_compute
        assert s >= seq
        E.count += 1
        E.ops[s].inc = E.count
        E.inc_seqs.append(s)
        E.inc_vals.append(E.count)
        return E.sem, E.count

    def _need(self, E, op, t, kind):
        if t is None:
            return
        if t[0] == "e" and t[1] == E.name:
            if E.name == "pe":
                return
            if kind == "rar" or not self.SAME_ENGINE_SYNC:
                return
            if kind == "waw" and getattr(self, "nowaw", False):
                return
        sem, val = self._resolve(t)
        key = id(sem)
        if E.waited.get(key, 0) >= val:
            return
        E.waited[key] = val
        op.waits.append((sem, val))

    def _deps(self, E, op, reads, writes):
        for b in reads:
            self._need(E, op, b.last_write, "raw")
            if b.excl:
                for r in b.readers:
                    self._need(E, op, r, "rar")
        for b in writes:
            self._need(E, op, b.last_write, "waw")
            for r in b.readers:
                self._need(E, op, r, "war")

    def _commit(self, t, reads, writes):
        for b in writes:
            b.last_write = t
            b.readers = []
        for b in reads:
            if t[0] == "e":
                b.readers = [r for r in b.readers if not (r[0] == "e" and r[1] == t[1])]
            else:
                b.readers = [r for r in b.readers if not (r[0] == "d" and r[1] is t[1])]
            b.readers.append(t)

    def op(self, eng, fn, reads=(), writes=(), nowaw=False):
        E = self.engs[eng]
        o = _Op(fn)
        self.nowaw = nowaw
        self._deps(E, o, reads, writes)
        self.nowaw = False
        E.ops.append(o)
        seq = len(E.ops) - 1
        E.last_compute = seq
        self._commit(("e", eng, seq), reads, writes)

    def dma(self, eng, fn, reads=(), writes=(), sembuf=None, par=False):
        E = self.engs[eng]
        sb = sembuf or (writes[0] if writes else reads[0])
        if sb.dsem is None:
            sb.dsem = self.nc.alloc_semaphore("d%d" % self.nsem)
            self.nsem += 1
        o = _Op(fn, True, sb.dsem)
        saved = []
        if par:
            for b in writes:
                lw = b.last_write
                if lw is not None and lw[0] == "d" and lw[1] is sb.dsem:
                    saved.append((b, lw))
                    b.last_write = None
        self._deps(E, o, reads, writes)
        for b, lw in saved:
            b.last_write = lw
        E.ops.append(o)
        sb.dcount += 16
        t = ("d", sb.dsem, sb.dcount)
        self._commit(t, reads, writes)
        return t

    def final_wait(self, eng, tickets):
        E = self.engs[eng]
        o = _Op(None)
        for t in tickets:
            self._need(E, o, t, "raw")
        E.ops.append(o)

    def emit(self):
        nc = self.nc
        hw = {"pe": "tensor", "act": "scalar", "dve": "vector", "pool": "gpsimd", "sp": "sync"}
        with nc.Block() as block:
            for n, E in self.engs.items():
                if not E.ops:
                    continue

                def body(e, E=E):
                    for o in E.ops:
                        for sem, val in o.waits:
                            e.wait_ge(sem, val)
                        if o.fn is None:
                            continue
                        ins = o.fn(e)
                        if o.is_dma:
                            ins.then_inc(o.dma_sem, 16)
                        elif o.inc is not None:
                            ins.then_inc(E.sem, 1)

                getattr(block, hw[n])(body)


class T:
    def __init__(self, name, ap, s, e, nbufs):
        self.name = name
        self.ap = ap
        self.s = s
        self.e = e
        self.bufs = [Buf("%s.%d" % (name, i)) for i in range(nbufs)]

    @property
    def b(self):
        return self.bufs[0]


class Arena:
    def __init__(self, big_ap, nwords):
        self.big = big_ap
        self.free_list = [(0, nwords)]
        self.retired = []
        self.peak = 0
        self.live = {}

    def alloc(self, name, shape, dtype, nbufs=1):
        n = 1
        for d_ in shape:
            n *= d_
        esz = 2 if dtype == BF16 else 4
        nw = (n * esz + 3) // 4
        nw = (nw + 7) // 8 * 8
        top = nw < 3000
        order = range(len(self.free_list) - 1, -1, -1) if top else range(len(self.free_list))
        for i in order:
            s, e = self.free_list[i]
            if e - s >= nw:
                break
        else:
            raise RuntimeError("arena out of SBUF for %s (%d words); free=%s live=%s" % (
                name, nw, self.free_list, sorted((v, k) for k, v in self.live.items())))
        if e - s == nw:
            self.free_list.pop(i)
        elif top:
            self.free_list[i] = (s, e - nw)
            s = e - nw
        else:
            self.free_list[i] = (s + nw, e)
        e = s + nw
        self.peak = max(self.peak, e)
        ap = self.big[:, s:e]
        if dtype == BF16:
            ap = ap.bitcast(BF16)
        ap = ap[:, 0:n]
        if len(shape) == 2:
            ap = ap.rearrange("p (a b) -> p a b", a=shape[0])
        elif len(shape) == 3:
            ap = ap.rearrange("p (a b c) -> p a b c", a=shape[0], b=shape[1])
        t = T(name, ap, s, e, nbufs)
        self.live[name] = (s, e)
        tick = []
        keep = []
        for (rs, re, tk) in self.retired:
            if rs < e and s < re:
                tick.extend(tk)
                if not (s <= rs and re <= e):
                    keep.append((rs, re, tk))
            else:
                keep.append((rs, re, tk))
        self.retired = keep
        for b in t.bufs:
            b.readers = list(tick)
        return t

    def free(self, t):
        self.live.pop(t.name, None)
        tk = []
        for b in t.bufs:
            if b.last_write is not None:
                tk.append(b.last_write)
            tk.extend(b.readers)
        seen = set()
        tk2 = []
        for x in tk:
            k = (x[0], id(x[1]) if x[0] == "d" else x[1], x[2])
            if k not in seen:
                seen.add(k)
                tk2.append(x)
        self.retired.append((t.s, t.e, tk2))
        fl = self.free_list + [(t.s, t.e)]
        fl.sort()
        merged = []
        for s, e in fl:
            if merged and merged[-1][1] == s:
                merged[-1] = (merged[-1][0], e)
            else:
                merged.append((s, e))
        self.free_list = merged


def build_program(stop=None):
    import os
    stop = int(os.environ.get("K_STOP", "99")) if stop is None else stop
    nc = bass.Bass("TRN2", target_bir_lowering=False)
    fw = FW(nc)

    def finish():
        fw.final_wait("sp", out_tickets)
        fw.emit()
        return nc

    def din(name, shape):
        return nc.dram_tensor(name, list(shape), F32, kind="ExternalInput").ap()

    def dout(name, shape):
        return nc.dram_tensor(name, list(shape), F32, kind="ExternalOutput").ap()

    xT = din("xT", [D, TT])
    cT_d = din("cT", [128, 16 * 17])
    ident_d = din("ident", [128, 128])
    tri_d = din("tri", [128, 128])
    blkmask_d = din("blkmask", [64, 64])
    oh_d = din("oh", [32, 128])
    blk0_d = din("blk0", [128, 1])
    tbl_d = din("tblP", [32, 16])
    sink_d = din("sinksP", [1, 16])
    gT_d = din("gT", [128, 64])
    gA_d = din("gA", [128, 8])
    gG_d = din("gG", [128, 8])
    badaT_d = din("badaT", [128, 96])
    vgain_d = din("vgain", [1, 1024])
    bs_d = din("bs", [1, 1024])
    wsT_d = din("wsT", [128, 8, 128])
    ckT_d = din("ckT", [128, 16 * 128])
    cv_d = din("cv2", [128, 16 * 128])
    wada_d = din("wada_u", [48, 128, UNITW])
    wfm_d = din("wfm_u", [9, 128, UNITW])
    wtm_d = din("wtm_u", [5, 128, UNITW])
    wout_d = din("wout_u", [8, 128, UNITW])
    wff1_d = din("wff1_u", [32, 128, UNITW])
    wff2_d = din("wff2_u", [32, 128, UNITW])

    yT = dout("yT", [D, TM])
    kT_out = dout("kT_out", [128, 192])
    v_out = dout("v_out", [192, 128])
    gv_out = dout("gv_out", [64, 1024])

    a_dr_t = nc.dram_tensor("a_scr", [16, 383], F32)
    a_dr = a_dr_t.ap()
    x1_dr = nc.dram_tensor("x1_scr", [D, TM], F32).ap()
    a_dr_buf = Buf("a_dr")
    x1_dr_bufs = [Buf("x1dr%d" % i) for i in range(16)]
    out_tickets = []

    NW = 52800
    big = nc.alloc_sbuf_tensor("big", [128, NW], F32)
    A = Arena(big.ap(), NW)
    psum = [nc.alloc_psum_tensor("ps%d" % i, [128, 512], F32) for i in range(8)]
    psb = [Buf("psb%d" % i, excl=True) for i in range(8)]
    rr = {"bank": 0, "slot": 0}

    def nextbank(allowed=None):
        allowed = allowed or range(8)
        while True:
            i = rr["bank"] % 8
            rr["bank"] += 1
            if i in allowed:
                return psum[i].ap(), psb[i]

    def mm(out, lhsT, rhs, start, stop, reads, writes, **kw):
        fw.op("pe", lambda e: e.matmul(out, lhsT, rhs, start=start, stop=stop, **kw), reads, writes)

    def tr(out, in_, ident, reads, writes):
        fw.op("pe", lambda e: e.transpose(out, in_, ident), reads, writes)

    def act(out, in_, func, reads, writes, bias=None, scale=None, accum=None, nowaw=False):
        kw = {}
        if bias is not None:
            kw["bias"] = bias
        if scale is not None:
            kw["scale"] = scale
        if accum is not None:
            kw["accum_out"] = accum
        fw.op("act", lambda e: e.activation(out, in_, func, **kw), reads, writes, nowaw=nowaw)

    def tt(eng, out, in0, in1, op, reads, writes, nowaw=False):
        fw.op(eng, lambda e: e.tensor_tensor(out, in0, in1, op), reads, writes, nowaw=nowaw)

    def ts(eng, out, in0, s1, s2, op0, op1, reads, writes):
        if op1 is None:
            fw.op(eng, lambda e: e.tensor_scalar(out, in0, s1, None, op0), reads, writes)
        else:
            fw.op(eng, lambda e: e.tensor_scalar(out, in0, s1, s2, op0, op1), reads, writes)

    def stt(out, in0, scalar, in1, op0, op1, reads, writes):
        fw.op("dve", lambda e: e.scalar_tensor_tensor(out, in0, scalar, in1, op0, op1), reads, writes)

    def cp(eng, out, in_, reads, writes):
        fw.op(eng, lambda e: e.tensor_copy(out, in_), reads, writes)

    def ld(out, in_, writes, reads=(), eng="sp", sembuf=None, par=False, **kw):
        return fw.dma(eng, lambda e: e.dma_start(out=out, in_=in_, **kw), reads, writes, sembuf=sembuf, par=par)

    def rsqrt_inplace(ap, buf, reads_extra=()):
        act(ap, ap, AF.Ln, [buf], [buf])
        act(ap, ap, AF.Exp, [buf], [buf], scale=-0.5)

    ring = [A.alloc("ring%d" % i, [UNITW], BF16) for i in range(2)]
    ring_extra = []

    def load_unit(dram_unit_ap):
        i = rr["slot"] % len(ring)
        rr["slot"] += 1
        t = ring[i]
        fw.dma("pool", lambda e: e.dma_start(out=t.ap, in_=dram_unit_ap, max_dma_last_dim=4096), (), [t.b])
        return t

    class Stream:
        def __init__(self, units):
            self.units = list(units)
            self.next = 0
            self.ready = []
            self.limit = len(self.units)

        def prefetch(self, n=1):
            while n > 0 and self.next < min(self.limit, len(self.units)):
                self.ready.append(load_unit(self.units[self.next]))
                self.next += 1
                n -= 1

        def get(self):
            if not self.ready:
                self.prefetch(1)
            return self.ready.pop(0)

    ident_f = A.alloc("ident_f", [128], F32)
    ident_b = A.alloc("ident_b", [128], BF16)
    ones_b = A.alloc("ones_b", [128], BF16)
    modT = A.alloc("modT", [96, 17], F32, nbufs=48)
    gT = A.alloc("gT", [64], F32)
    gA = A.alloc("gA", [8], F32)
    gG = A.alloc("gG", [8], F32)
    badaT = A.alloc("badaT", [96], F32)
    aM = A.alloc("aM", [16, 17], F32, nbufs=8)
    cm = A.alloc("cm", [16, 17], F32)
    a2 = A.alloc("a2", [16, 17], F32)
    cf = A.alloc("cf", [16, 17], F32)
    ld(ident_f.ap, ident_d, [ident_f.b])
    ld(gT.ap, gT_d, [gT.b])
    ld(gA.ap, gA_d, [gA.b])
    ld(gG.ap, gG_d, [gG.b])
    ld(badaT.ap, badaT_d, [badaT.b])
    cp("dve", ident_b.ap, ident_f.ap, [ident_f.b], [ident_b.b])
    fw.op("dve", lambda e: e.memset(ones_b.ap, 1.0), (), [ones_b.b])

    XT = A.alloc("XT", [16, TT], F32, nbufs=16)
    for kc in range(16):
        ld(XT.ap[:, kc, :], xT[kc * 128:(kc + 1) * 128, :], [XT.bufs[kc]])

    cT = A.alloc("cT", [16, 17], F32)
    siluT = A.alloc("siluT", [16, 17], BF16)
    ring_extra.extend(A.alloc("ringx%d" % i, [UNITW], BF16) for i in range(2))
    ring.extend(ring_extra)
    ld(cT.ap, cT_d.rearrange("p (a b) -> p a b", a=16), [cT.b])
    act(siluT.ap, cT.ap, AF.Silu, [cT.b], [siluT.b])

    ada_order = [u for i in range(8) for u in (i, 8 + i)] + list(range(16, 48))
    ada_stream = Stream([wada_d[u] for u in ada_order])
    ada_state = {"i": 0}

    def ada_unit(bank):
        u = ada_order[ada_state["i"]]
        ada_state["i"] += 1
        pt, pb = psum[bank].ap(), psb[bank]
        slot = ada_stream.get()
        w = slot.ap.rearrange("p (k j c) -> p k j c", k=16, j=2)
        for j in range(2):
            for kc in range(16):
                mm(pt[:, j * 17:(j + 1) * 17], w[:, kc, j, :], siluT.ap[:, kc, :],
                   kc == 0, kc == 15, [slot.b, siluT.b], [pb])
        ada_stream.prefetch(1)
        tt("dve", modT.ap[:, 2 * u:2 * u + 2, :], pt[:, 0:34].rearrange("p (a b) -> p a b", a=2),
           badaT.ap[:, 2 * u:2 * u + 2].unsqueeze(2).broadcast_to([128, 2, 17]), ALU.add,
           [pb, badaT.b], [modT.bufs[u]])
        if u == 23:
            tt("dve", cm.ap, seg_ap(2), gbc(1), ALU.mult, modT.bufs[16:24] + [gT.b], [cm.b])
        if u == 39:
            stt(a2.ap, seg_ap(4), 1.0, gbc(2), ALU.add, ALU.mult, modT.bufs[32:40] + [gT.b], [a2.b])
        if u == 47:
            tt("dve", cf.ap, seg_ap(5), gbc(3), ALU.mult, modT.bufs[40:48] + [gT.b], [cf.b])

    def seg_ap(seg):
        return modT.ap[:, seg * 16:(seg + 1) * 16, :]

    def gbc(i):
        return gT.ap[:, i * 16:(i + 1) * 16].unsqueeze(2).broadcast_to([128, 16, 17])

    def modulate_kc(dst, dst_buf, src, src_buf, rb, a_t, a_buf, sh_seg, kc, ncols_main, halo, idx):
        tm = tmp2[idx % 2]
        w = ncols_main + (TH if halo else 0)
        shb = modT.bufs[sh_seg * 8 + kc // 2]
        tt("dve", tm.ap[:, 0:w], src[:, 0:w], rb.ap[:, 0:w], ALU.mult, [src_buf, rb.b], [tm.b])
        shp = modT.ap[:, sh_seg * 16 + kc, :]
        rd = [tm.b, a_buf, shb]
        act(dst.ap[:, kc, 0:TP], tm.ap[:, 0:TP], AF.Identity, rd, [dst_buf],
            bias=shp[:, 0:1], scale=a_t.ap[:, kc, 0:1])
        if halo:
            act(dst.ap[:, kc, TM:TT], tm.ap[:, TM:TT], AF.Identity, rd, [dst_buf],
                bias=shp[:, 0:1], scale=a_t.ap[:, kc, 0:1])
        tt("dve", tmps.ap.rearrange("p (b t) -> p b t", t=4), tm.ap[:, TP:TM].rearrange("p (b t) -> p b t", t=4),
           a_t.ap[:, kc, 1:17].unsqueeze(2).broadcast_to([128, 16, 4]), ALU.mult, [tm.b, a_buf], [tmps.b])
        tt("dve", dst.ap[:, kc, TP:TM].rearrange("p (b t) -> p b t", t=4), tmps.ap.rearrange("p (b t) -> p b t", t=4),
           shp[:, 1:17].unsqueeze(2).broadcast_to([128, 16, 4]), ALU.add, [tmps.b, shb], [dst_buf])

    HT = A.alloc("HT", [16, TT], BF16, nbufs=16)
    tmp2 = [A.alloc("tmp%d" % i, [TT], F32) for i in range(2)]
    tmps = A.alloc("tmps", [64], F32)
    ring_start = [A.alloc("ringy%d" % i, [UNITW], BF16) for i in range(2)]
    ring.extend(ring_start)
    ada_stream.limit = 16
    ada_stream.prefetch(5)
    def ada_pair_mod(i_):
        stt(aM.ap[:, 2 * i_:2 * i_ + 2, :], modT.ap[:, 16 + 2 * i_:18 + 2 * i_, :], 1.0,
            gT.ap[:, 2 * i_:2 * i_ + 2].unsqueeze(2).broadcast_to([128, 2, 17]), ALU.add, ALU.mult,
            [modT.bufs[8 + i_], gT.b], [aM.bufs[i_]])
        for kc in (2 * i_, 2 * i_ + 1):
            modulate_kc(HT, HT.bufs[kc], XT.ap[:, kc, :], XT.bufs[kc], RBC, aM, aM.bufs[i_], 0, kc, TM, True, kc)

    NPRE = 3
    for i_ in range(NPRE):
        ada_unit(6)
        ada_unit(7)
    def sumsq_banks():
        return [nextbank() for _ in range(3)]

    SSG = [(0, 512), (512, 512), (1024, 192)]
    RBC = A.alloc("RBC", [TT], F32)
    sq2 = [A.alloc("sq%d" % i, [TT], BF16) for i in range(2)]
    ssb = [(psum[i].ap(), psb[i]) for i in range(3)]
    for kc in range(16):
        sq = sq2[kc % 2]
        act(sq.ap, XT.ap[:, kc, :], AF.Square, [XT.bufs[kc]], [sq.b])
        for gi, (o, n) in enumerate(SSG):
            mm(ssb[gi][0][:, 0:n], ones_b.ap, sq.ap[:, o:o + n], kc == 0, kc == 15, [ones_b.b, sq.b], [ssb[gi][1]])
    for gi, (o, n) in enumerate(SSG):
        ts("dve", RBC.ap[:, o:o + n], ssb[gi][0][:, 0:n], 1.0 / D, EPS, ALU.mult, ALU.add, [ssb[gi][1]], [RBC.b])
    rsqrt_inplace(RBC.ap, RBC.b)

    for i_ in range(NPRE):
        ada_pair_mod(i_)
    for i_ in range(NPRE, 8):
        ada_unit(6)
        ada_unit(7)
        ada_pair_mod(i_)
    for t_ in ring_start:
        ring.remove(t_)
        A.free(t_)
    wi_stream = Stream([wfm_d[u] for u in range(9)] + [wtm_d[u] for u in range(5)])
    wi_stream.prefetch(3)
    wsT_f = A.alloc("wsT_f", [8, 128], F32)
    tri = A.alloc("tri", [128], F32)
    WsT = A.alloc("WsT", [8, 128], BF16)
    ld(wsT_f.ap, wsT_d, [wsT_f.b])
    ld(tri.ap, tri_d, [tri.b])
    tt("dve", WsT.ap, wsT_f.ap, tri.ap.unsqueeze(1).broadcast_to([128, 8, 128]), ALU.mult,
       [wsT_f.b, tri.b], [WsT.b])
    mrep = A.alloc("mrep", [8, 4], F32)
    blkm = A.alloc("blkm", [64], F32)
    Wblk = A.alloc("Wblk", [8, 64], BF16)
    ld(blkm.ap[0:64, :], blkmask_d, [blkm.b])
    for b_ in range(16):
        ld(mrep.ap[b_ * 4:(b_ + 1) * 4, :, :], wsT_d[0:4, :, 0:4], [mrep.b], par=True)
    tt("dve", Wblk.ap[0:64].rearrange("p h (b i) -> p h b i", b=16),
       mrep.ap[0:64].unsqueeze(2).broadcast_to([64, 8, 16, 4]),
       blkm.ap[0:64].rearrange("p (b i) -> p b i", b=16).unsqueeze(1).broadcast_to([64, 8, 16, 4]),
       ALU.mult, [mrep.b, blkm.b], [Wblk.b])
    bsr = A.alloc("bsr", [8, 128], F32)
    bs_hi = A.alloc("bs_hi", [8, 128], BF16)
    bs_lo = A.alloc("bs_lo", [8, 128], BF16)
    bs_t = A.alloc("bs_t", [8, 128], F32)
    bsS_hi = A.alloc("bsS_hi", [8, 64], BF16)
    bsS_lo = A.alloc("bsS_lo", [8, 64], BF16)
    ones_r = A.alloc("ones_r", [128], BF16)
    ld(bsr.ap[0:1], bs_d.rearrange("o (h i) -> o h i", h=8), [bsr.b])
    fw.op("dve", lambda e: e.memset(ones_r.ap[0:1, :], 1.0), (), [ones_r.b])
    cp("dve", bs_hi.ap[0:1], bsr.ap[0:1], [bsr.b], [bs_hi.b])
    cp("dve", bs_t.ap[0:1], bs_hi.ap[0:1], [bs_hi.b], [bs_t.b])
    tt("dve", bs_lo.ap[0:1], bsr.ap[0:1], bs_t.ap[0:1], ALU.subtract, [bsr.b, bs_t.b], [bs_lo.b])
    cp("dve", bsS_hi.ap[0:1].rearrange("p h (b i) -> p h b i", b=16),
       bs_hi.ap[0:1, :, 0:4].unsqueeze(2).broadcast_to([1, 8, 16, 4]), [bs_hi.b], [bsS_hi.b])
    cp("dve", bsS_lo.ap[0:1].rearrange("p h (b i) -> p h b i", b=16),
       bs_lo.ap[0:1, :, 0:4].unsqueeze(2).broadcast_to([1, 8, 16, 4]), [bs_lo.b], [bsS_lo.b])
    vgain = A.alloc("vgain", [1024], F32)
    ld(vgain.ap, vgain_d.partition_broadcast(128), [vgain.b])
    for t_ in (wsT_f, tri, mrep, blkm, bsr, bs_t):
        A.free(t_)

    A.free(XT)
    for t_ in (tmp2[0], tmp2[1], tmps):
        A.free(t_)
    A.free(sq2[0])
    A.free(sq2[1])
    KcT = A.alloc("KcT", [16, 128], BF16)
    Vc = A.alloc("Vc", [16, 128], BF16)
    fw.dma("pool", lambda e: e.dma_start(out=KcT.ap.rearrange("p a b -> p (a b)"), in_=ckT_d, max_dma_last_dim=4096), (), [KcT.b])
    fw.dma("pool", lambda e: e.dma_start(out=Vc.ap.rearrange("p a b -> p (a b)"), in_=cv_d, max_dma_last_dim=4096), (), [Vc.b])


    if stop == 1:
        return finish()
    tbl = A.alloc("tbl", [16], F32)
    oh = A.alloc("oh", [128], F32)
    a_sb = A.alloc("a_sb", [383], F32)
    sinkP = A.alloc("sinkP", [16], F32)
    nsinkP = A.alloc("nsinkP", [16], F32)
    sinkS = A.alloc("sinkS", [2], F32)
    nsinkS = A.alloc("nsinkS", [2], F32)
    blk0 = A.alloc("blk0", [1], F32)
    biasP = A.alloc("biasP", [16, 256], F32)
    biasS = A.alloc("biasS", [2, 132], F32)
    ld(tbl.ap[0:32, :], tbl_d, [tbl.b])
    ld(oh.ap[0:32, :], oh_d, [oh.b])
    ld(blk0.ap, blk0_d, [blk0.b])
    ld(sinkP.ap, sink_d.partition_broadcast(128), [sinkP.b])
    for g in range(8):
        src = bass.AP(tensor=sink_d.tensor, offset=2 * g, ap=[[0, 4], [1, 2]])
        ld(sinkS.ap[g * 4:(g + 1) * 4, :], src, [sinkS.b], par=True)
    ts("dve", nsinkP.ap, sinkP.ap, -1.0, None, ALU.mult, None, [sinkP.b], [nsinkP.b])
    ts("dve", nsinkS.ap[0:32, :], sinkS.ap[0:32, :], -1.0, None, ALU.mult, None, [sinkS.b], [nsinkS.b])
    pt, pb = nextbank()
    mm(pt[0:16, 0:128], tbl.ap[0:32, :], oh.ap[0:32, :], True, True, [tbl.b, oh.b], [pb])
    fw.op("dve", lambda e: e.memset(a_sb.ap[0:16, :], NEG), (), [a_sb.b])
    cp("dve", a_sb.ap[0:16, 127:255], pt[0:16, 0:128], [pb], [a_sb.b])
    ld(a_dr, a_sb.ap[0:16, :], [a_dr_buf], [a_sb.b])
    T1 = A.alloc("T1", [16, 256], F32)
    S1 = A.alloc("S1", [2, 128], F32)
    S1n = A.alloc("S1n", [2, 4], F32)
    for h_ in range(16):
        ld(T1.ap[:, h_, :], bass.AP(tensor=a_dr_t, offset=383 * h_, ap=[[1, 128], [1, 256]]), [T1.b], [a_dr_buf], par=True)
    for g in range(8):
        ld(S1.ap[g * 4:(g + 1) * 4, :, :],
           bass.AP(tensor=a_dr_t, offset=2 * g * 383 + 128, ap=[[1, 4], [383, 2], [1, 128]]), [S1.b], [a_dr_buf], par=True)
        ld(S1n.ap[g * 4:(g + 1) * 4, :, :],
           bass.AP(tensor=a_dr_t, offset=2 * g * 383 + 124, ap=[[1, 4], [383, 2], [1, 4]]), [S1n.b], [a_dr_buf], par=True)
    cp("dve", biasP.ap, T1.ap[:, :, ::-1], [T1.b], [biasP.b])
    cp("dve", biasS.ap[0:32, :, 0:128], S1.ap[0:32, :, ::-1], [S1.b], [biasS.b])
    cp("dve", biasS.ap[0:32, :, 128:132], S1n.ap[0:32, :, ::-1], [S1n.b], [biasS.b])
    for t_ in (T1, S1, S1n, a_sb, tbl, oh):
        A.free(t_)
    if stop == 3:
        return finish()
    QT = A.alloc("QT", [8, TM], BF16, nbufs=8)
    KT = A.alloc("KT", [TT], BF16)
    GUT = A.alloc("GUT", [8, TM], BF16, nbufs=8)
    KOUT = A.alloc("KOUT", [192], F32)
    fm_chunks = [("q", c) for c in range(8)] + [("k", 0)] + [("gu", h) for h in range(8)]
    for u in range(9):
        slot = wi_stream.get()
        wi_stream.prefetch(1)
        w = slot.ap.rearrange("p (k j c) -> p k j c", k=16, j=2)
        for j in range(2):
            ci = 2 * u + j
            if ci >= len(fm_chunks):
                break
            kind, idx = fm_chunks[ci]
            groups = GROUPS + ([(TM, TH)] if kind == "k" else [])
            for (o, n) in groups:
                pt, pb = nextbank()
                for kc in range(16):
                    mm(pt[:, 0:n], w[:, kc, j, :], HT.ap[:, kc, o:o + n], kc == 0, kc == 15,
                       [slot.b, HT.bufs[kc]], [pb])
                if kind == "q":
                    act(QT.ap[:, idx, o:o + n], pt[:, 0:n], AF.Identity, [pb], [QT.bufs[idx]], scale=0.125)
                elif kind == "k":
                    act(KT.ap[:, o:o + n], pt[:, 0:n], AF.Identity, [pb], [KT.b])
                    if o == 512:
                        cp("dve", KOUT.ap[:, 0:128], pt[:, 384:512], [pb], [KOUT.b])
                    if o == 1024:
                        cp("dve", KOUT.ap[:, 128:192], pt[:, 0:64], [pb], [KOUT.b])
                else:
                    act(GUT.ap[:, idx, o:o + n], pt[:, 0:n], AF.Gelu_apprx_tanh, [pb], [GUT.bufs[idx]])
    out_tickets.append(ld(kT_out, KOUT.ap, [Buf("kT_out")], [KOUT.b], sembuf=KOUT.b))

    if stop == 4:
        return finish()
    TILES = [(t_ * 128, 128) for t_ in range(8)] + [(TP, TS)]
    GV = A.alloc("GV", [9, 1024], BF16, nbufs=9)
    V = A.alloc("V", [10, 128], BF16, nbufs=10)
    GVOUT = A.alloc("GVOUT", [1024], F32)
    VOUT = A.alloc("VOUT", [2, 128], F32)
    gtmp = [A.alloc("gtmp%d" % i, [256], F32) for i in range(2)]
    sqt2 = [A.alloc("sqt%d" % i, [256], F32) for i in range(2)]
    gn2 = [A.alloc("gn%d" % i, [256], F32) for i in range(2)]
    ssv2 = [A.alloc("ssv%d" % i, [2], F32) for i in range(2)]
    mhalf = A.alloc("mhalf", [2], F32)
    fw.op("pool", lambda e: e.memset(mhalf.ap, -0.5), (), [mhalf.b])
    it = 0
    for u in range(4):
        slot = wi_stream.get()
        wi_stream.prefetch(1)
        w = slot.ap.rearrange("p (k c) -> p k c", k=16)
        for ti, (o, n) in enumerate(TILES):
            pt, pb = nextbank()
            for kc in range(16):
                mm(pt[0:n, 0:256], HT.ap[:, kc, o:o + n], w[:, kc, :], kc == 0, kc == 15,
                   [slot.b, HT.bufs[kc]], [pb])
            g_, sqt, gn, ssv = gtmp[it % 2], sqt2[it % 2], gn2[it % 2], ssv2[it % 2]
            it += 1
            act(g_.ap[0:n], pt[0:n, 0:256], AF.Gelu_apprx_tanh, [pb], [g_.b])
            for j_ in range(2):
                act(sqt.ap[0:n, j_ * 128:(j_ + 1) * 128], g_.ap[0:n, j_ * 128:(j_ + 1) * 128], AF.Square, [g_.b], [sqt.b, ssv.b],
                    accum=ssv.ap[0:n, j_:j_ + 1])
            ts("dve", ssv.ap[0:n], ssv.ap[0:n], 1.0 / 128, EPS, ALU.mult, ALU.add, [ssv.b], [ssv.b])
            tt("pool", ssv.ap[0:n], ssv.ap[0:n], mhalf.ap[0:n], ALU.pow, [ssv.b, mhalf.b], [ssv.b])
            tt("dve", gn.ap[0:n].rearrange("p (h c) -> p h c", h=2), g_.ap[0:n].rearrange("p (h c) -> p h c", h=2),
               ssv.ap[0:n].unsqueeze(2).broadcast_to([n, 2, 128]), ALU.mult, [g_.b, ssv.b], [gn.b])
            cols = slice(u * 256, (u + 1) * 256)
            tt("dve", GV.ap[0:n, ti, cols], gn.ap[0:n], vgain.ap[0:n, cols], ALU.mult, [gn.b, vgain.b], [GV.bufs[ti]])
            if ti == 8:
                tt("dve", GVOUT.ap[0:n, cols], gn.ap[0:n], vgain.ap[0:n, cols], ALU.mult, [gn.b, vgain.b], [GVOUT.b])
    out_tickets.append(ld(gv_out, GVOUT.ap[0:64, :], [Buf("gv_out")], [GVOUT.b], sembuf=GVOUT.b))
    slot = wi_stream.get()
    w = slot.ap.rearrange("p (k c) -> p k c", k=16)
    for ti, (o, n) in enumerate(TILES + [(TM, TH)]):
        pt, pb = nextbank()
        for kc in range(16):
            mm(pt[0:n, 0:128], HT.ap[:, kc, o:o + n], w[:, kc, 0:128], kc == 0, kc == 15,
               [slot.b, HT.bufs[kc]], [pb])
        act(V.ap[0:n, ti, :], pt[0:n, 0:128], AF.Identity, [pb], [V.bufs[ti]])
        if ti == 7:
            cp("dve", VOUT.ap[:, 0, :], pt[:, 0:128], [pb], [VOUT.b])
        if ti == 8:
            cp("dve", VOUT.ap[0:64, 1, :], pt[0:64, 0:128], [pb], [VOUT.b])
    vo_b = Buf("v_out")
    out_tickets.append(ld(v_out[0:128, :], VOUT.ap[:, 0, :], [vo_b], [VOUT.b], sembuf=VOUT.b))
    out_tickets.append(ld(v_out[128:192, :], VOUT.ap[0:64, 1, :], [vo_b], [VOUT.b], sembuf=VOUT.b))
    for t_ in (gtmp[0], gtmp[1], sqt2[0], sqt2[1], gn2[0], gn2[1], ssv2[0], ssv2[1], mhalf, vgain, RBC, KOUT, VOUT, GVOUT):
        A.free(t_)
    A.free(HT)

    if stop == 5:
        return finish()
    ada_stream.limit = 48
    ada_stream.prefetch(3)

    MT = A.alloc("MT", [16, TM], BF16, nbufs=16)
    RA = A.alloc("RA", [TM], F32)
    S2 = [A.alloc("S2_%d" % i, [4, 256], F32) for i in range(2)]
    Pb = [A.alloc("P_%d" % i, [4, 256], BF16) for i in range(2)]
    PTs = [A.alloc("PT_%d" % i, [8, 128], BF16) for i in range(2)]
    SQA = A.alloc("SQA", [256], BF16)
    st = A.alloc("st", [32], F32)
    def att_1a(qb, m, idx):
        qo = qb * 128
        sbanks = [(psum[(idx % 2) * 2 + i].ap(), psb[(idx % 2) * 2 + i]) for i in range(2)]
        s2 = S2[idx % 2]
        for s in range(4):
            c, half = 2 * m + s // 2, s % 2
            bank, bb = sbanks[half]
            cb = (s // 2) * 256
            hp = slice(half * 64, (half + 1) * 64)
            lh = QT.ap[hp, c, qo:qo + 128]
            if qb >= 1:
                mm(bank[:, cb:cb + 256], lh, KT.ap[hp, qo - 128:qo + 128], True, True, [QT.bufs[c], KT.b], [bb])
            else:
                mm(bank[:, cb:cb + 128], lh, KT.ap[hp, TM:TT], True, True, [QT.bufs[c], KT.b], [bb])
                mm(bank[:, cb + 128:cb + 256], lh, KT.ap[hp, 0:128], True, True, [QT.bufs[c], KT.b], [bb])
        for i in range(2):
            tt("dve", s2.ap[:, i:4:2, :], sbanks[i][0].rearrange("p (a b) -> p a b", a=2),
               biasP.ap[:, 4 * m + i:4 * m + 4:2, :], ALU.add, [sbanks[i][1], biasP.b], [s2.b], nowaw=True)
        if qb == 0:
            ts("dve", s2.ap[:, :, 0:128], s2.ap[:, :, 0:128], blk0.ap[:, 0:1], None, ALU.add, None, [s2.b, blk0.b], [s2.b])
        stb = st2[idx % 3]
        nmx = stb.ap[:, 0:4]
        fw.op("dve", lambda e, s2=s2, nmx=nmx: e.tensor_reduce(nmx, s2.ap, AX.X, ALU.max, negate=True), [s2.b], [stb.bufs[0]])
        tt("dve", nmx, nmx, nsinkP.ap[:, 4 * m:4 * m + 4], ALU.min, [stb.bufs[0], nsinkP.b], [stb.bufs[0]])

    def att_1b(qb, m, idx):
        s2, pb_, stb = S2[idx % 2], Pb[idx % 3], st2[idx % 3]
        for s in range(4):
            act(pb_.ap[:, s, :], s2.ap[:, s, :], AF.Exp, [s2.b, stb.bufs[0]], [pb_.b, stb.bufs[1]],
                bias=stb.ap[:, s:s + 1], accum=stb.ap[:, 4 + s:5 + s], nowaw=True)

    def att_1c(qb, m, idx):
        pb_, stb = Pb[idx % 3], st2[idx % 3]
        nmx, rs, es, den = (stb.ap[:, 0:4], stb.ap[:, 4:8], stb.ap[:, 8:12], stb.ap[:, 12:16])
        b0, b1, b2 = stb.bufs
        tt("pool", es, sinkP.ap[:, 4 * m:4 * m + 4], nmx, ALU.add, [sinkP.b, b0], [b2])
        act(es, es, AF.Exp, [b2], [b2])
        tt("pool", den, rs, es, ALU.add, [b1, b2], [b2])
        fw.op("dve", lambda e, den=den: e.reciprocal(den, den), [b2], [b2])
        for s_ in range(4):
            act(pb_.ap[:, s_, :], pb_.ap[:, s_, :], AF.Identity, [pb_.b, b2], [pb_.b], scale=den[:, s_:s_ + 1], nowaw=True)

    def att_stage2(qb, m, idx):
        qo = qb * 128
        pb_, ptt = Pb[idx % 3], PTs[idx % 2]
        ssA_t, ssA_b = psum[6].ap(), psb[6]
        ptb_t, ptb_b = psum[4].ap().bitcast(BF16), psb[4]
        for s in range(4):
            for kb in range(2):
                tr(ptb_t[:, (s * 2 + kb) * 128:(s * 2 + kb + 1) * 128], pb_.ap[:, s, kb * 128:(kb + 1) * 128],
                   ident_b.ap, [pb_.b, ident_b.b], [ptb_b])
        cp("dve", ptt.ap.rearrange("p a b -> p (a b)"), ptb_t, [ptb_b], [ptt.b])
        pv_t, pv_b = psum[5].ap(), psb[5]
        for s in range(4):
            c, half = 2 * m + s // 2, s % 2
            hp = slice(half * 64, (half + 1) * 64)
            for kb in range(2):
                kt_i = (9 if qb == 0 else qb - 1) if kb == 0 else qb
                mm(pv_t[hp, (s // 2) * 128:(s // 2 + 1) * 128], V.ap[:, kt_i, hp], ptt.ap[:, s * 2 + kb, :],
                   kb == 0, kb == 1, [V.bufs[kt_i], ptt.b], [pv_b])
        for j in range(2):
            c = 2 * m + j
            ts("dve", MT.ap[:, c, qo:qo + 128], pv_t[:, j * 128:(j + 1) * 128], gA.ap[:, c:c + 1], None, ALU.mult, None,
               [pv_b, gA.b], [MT.bufs[c]])
        sqa = SQA2[idx % 2]
        act(sqa.ap, pv_t[:, 0:256], AF.Square, [pv_b], [sqa.b])

        def fin():
            for j in range(2):
                mm(ssA_t[:, 0:128], ones_b.ap, sqa.ap[:, j * 128:(j + 1) * 128], m == 0 and j == 0, m == 3 and j == 1,
                   [ones_b.b, sqa.b], [ssA_b])
            if m == 3:
                ts("dve", RA.ap[:, qo:qo + 128], ssA_t[:, 0:128], 1.0 / 1024, EPS, ALU.mult, ALU.add, [ssA_b], [RA.b])
        return fin

    st2 = [A.alloc("st%d" % i, [16], F32, nbufs=3) for i in range(3)]
    mone = A.alloc("mone", [4], F32)
    fw.op("pool", lambda e: e.memset(mone.ap, -1.0), (), [mone.b])
    Pb.append(A.alloc("P_2", [4, 256], BF16))
    SQA2 = [SQA, A.alloc("SQAb", [256], BF16)]
    groups_ = [(qb, m) for qb in range(8) for m in range(4)]
    NG_ = len(groups_)
    pend = None
    for t in range(NG_ + 3):
        if t < NG_:
            att_1a(groups_[t][0], groups_[t][1], t)
        if 0 <= t - 1 < NG_:
            att_1b(groups_[t - 1][0], groups_[t - 1][1], t - 1)
        if 0 <= t - 2 < NG_:
            att_1c(groups_[t - 2][0], groups_[t - 2][1], t - 2)
        if 0 <= t - 3 < NG_:
            fin = att_stage2(groups_[t - 3][0], groups_[t - 3][1], t - 3)
            if pend is not None:
                pend()
            pend = fin
            if (t - 3) % 2 == 0:
                ada_unit(7)
    pend()
    for t_ in (mone, st2[0], st2[1], st2[2], SQA2[1], S2[0], S2[1], Pb[0], Pb[1], Pb[2], PTs[0], PTs[1], SQA):
        A.free(t_)

    if stop == 6:
        return finish()
    Vn = A.alloc("Vn", [16, 128], BF16)
    for b_ in range(16):
        ld(Vn.ap[0:4, b_, :], V.ap[b_ * 4:(b_ + 1) * 4, 8, :], [Vn.b], [V.bufs[8]], par=True)
    QS = A.alloc("QS", [16, 32], BF16)
    cp("dve", QS.ap.rearrange("p b (c t) -> p b c t", c=8), QT.ap[:, :, TP:TM].rearrange("p c (b t) -> p b c t", t=4),
       QT.bufs, [QS.b])
    SS2 = A.alloc("SS2", [4, 132], F32)
    Ps = A.alloc("Ps", [4, 132], BF16)
    PTS = A.alloc("PTS", [4, 32], BF16)
    PTN = A.alloc("PTN", [4, 32], BF16)
    os_t, os_b = psum[7].ap(), psb[7]
    SS2s = [SS2, A.alloc("SS2b", [4, 132], F32)]
    Pss = [Ps, A.alloc("Psb", [4, 132], BF16)]
    sts = [st, A.alloc("stb", [32], F32)]
    def samp_a(grp):
        SS2, Ps, st = SS2s[grp % 2], Pss[grp % 2], sts[grp % 2]
        sbanks = [(psum[(grp % 2) * 2 + i].ap(), psb[(grp % 2) * 2 + i]) for i in range(2)]
        for l in range(4):
            b_, kvh = 2 * grp + l // 2, l % 2
            bank, bb = sbanks[kvh]
            base = (l // 2) * 256
            hp = slice(kvh * 64, (kvh + 1) * 64)
            lh = QS.ap[hp, b_, :]
            mm(bank[0:32, base:base + 128], lh, KcT.ap[hp, b_, :], True, True, [QS.b, KcT.b], [bb])
            mm(bank[0:32, base + 128:base + 132], lh, KT.ap[hp, TP + b_ * 4:TP + b_ * 4 + 4], True, True,
               [QS.b, KT.b], [bb])
        for i in range(2):
            tt("dve", SS2.ap[0:32, i:4:2, :], sbanks[i][0][0:32, :].rearrange("p (a b) -> p a b", a=2)[:, :, 0:132],
               biasS.ap[0:32, i, :].unsqueeze(1).broadcast_to([32, 2, 132]), ALU.add, [sbanks[i][1], biasS.b], [SS2.b])
        nmx, rs, es, den = st.ap[0:32, 0:4], st.ap[0:32, 4:8], st.ap[0:32, 8:12], st.ap[0:32, 12:16]
        fw.op("dve", lambda e, nmx=nmx: e.tensor_reduce(nmx, SS2.ap[0:32], AX.X, ALU.max, negate=True), [SS2.b], [st.b])
        tt("dve", nmx.rearrange("p (a b) -> p a b", a=2), nmx.rearrange("p (a b) -> p a b", a=2),
           nsinkS.ap[0:32].unsqueeze(1).broadcast_to([32, 2, 2]), ALU.min, [st.b, nsinkS.b], [st.b])
        for l in range(4):
            act(Ps.ap[0:32, l, :], SS2.ap[0:32, l, :], AF.Exp, [SS2.b, st.b], [Ps.b, st.b],
                bias=st.ap[0:32, l:l + 1], accum=st.ap[0:32, 4 + l:5 + l])
        tt("dve", es.rearrange("p (a b) -> p a b", a=2), nmx.rearrange("p (a b) -> p a b", a=2),
           sinkS.ap[0:32].unsqueeze(1).broadcast_to([32, 2, 2]), ALU.add, [sinkS.b, st.b], [st.b])
        act(es, es, AF.Exp, [st.b], [st.b])
        tt("dve", den, rs, es, ALU.add, [st.b], [st.b])
        fw.op("dve", lambda e, den=den: e.reciprocal(den, den), [st.b], [st.b])
        tt("dve", Ps.ap[0:32], Ps.ap[0:32], den.unsqueeze(2).broadcast_to([32, 4, 132]), ALU.mult, [Ps.b, st.b], [Ps.b])
    def samp_b(grp):
        Ps = Pss[grp % 2]
        ptb_t, ptb_b = psum[4].ap().bitcast(BF16), psb[4]
        for l in range(4):
            tr(ptb_t[:, l * 32:(l + 1) * 32], Ps.ap[0:32, l, 0:128], ident_b.ap[0:32, 0:32], [Ps.b, ident_b.b], [ptb_b])
            tr(ptb_t[0:4, 128 + l * 32:128 + (l + 1) * 32], Ps.ap[0:32, l, 128:132], ident_b.ap[0:32, 0:32],
               [Ps.b, ident_b.b], [ptb_b])
        act(PTS.ap.rearrange("p a b -> p (a b)"), ptb_t[:, 0:128], AF.Identity, [ptb_b], [PTS.b])
        cp("dve", PTN.ap[0:4].rearrange("p a b -> p (a b)"), ptb_t[0:4, 128:256], [ptb_b], [PTN.b])
        for l in range(4):
            b_, kvh = 2 * grp + l // 2, l % 2
            hp = slice(kvh * 64, (kvh + 1) * 64)
            o_ap = os_t[hp, b_ * 32:(b_ + 1) * 32]
            mm(o_ap, Vc.ap[:, b_, hp], PTS.ap[:, l, :], True, False, [Vc.b, PTS.b], [os_b])
            mm(o_ap, Vn.ap[0:4, b_, hp], PTN.ap[0:4, l, :], False, True, [Vn.b, PTN.b], [os_b])
    samp_a(0)
    for grp in range(8):
        if grp + 1 < 8:
            samp_a(grp + 1)
        samp_b(grp)
        ada_unit(5)
    for t_ in (SS2s[1], Pss[1], sts[1]):
        A.free(t_)
    tt("dve", MT.ap[:, 0:8, TP:TM].rearrange("p c (b t) -> p b c t", t=4),
       os_t.rearrange("p (b c t) -> p b c t", b=16, c=8),
       gA.ap.unsqueeze(1).unsqueeze(3).broadcast_to([128, 16, 8, 4]), ALU.mult, [os_b, gA.b], MT.bufs[0:8])
    SQS = A.alloc("SQS", [512], BF16)
    act(SQS.ap.rearrange("p (c b t) -> p b c t", c=8, b=16), os_t.rearrange("p (b c t) -> p b c t", b=16, c=8),
        AF.Square, [os_b], [SQS.b])
    ssA_t, ssA_b = psum[6].ap(), psb[6]
    for c in range(8):
        mm(ssA_t[:, 0:64], ones_b.ap, SQS.ap[:, c * 64:(c + 1) * 64], c == 0, c == 7, [ones_b.b, SQS.b], [ssA_b])
    ts("dve", RA.ap[:, TP:TM], ssA_t[:, 0:64], 1.0 / 1024, EPS, ALU.mult, ALU.add, [ssA_b], [RA.b])
    rsqrt_inplace(RA.ap, RA.b)
    for t_ in (st, Vn, QS, SS2, Ps, PTS, PTN, SQS, KcT, Vc,
               biasP, biasS, QT, KT, V, sinkP, nsinkP, sinkS, nsinkS, blk0):
        A.free(t_)

    if stop == 7:
        return finish()
    RG = A.alloc("RG", [TM], F32)
    GO = [A.alloc("GO%d" % i, [4, 128], F32) for i in range(2)]
    SQG = [A.alloc("SQG%d" % i, [4, 128], BF16) for i in range(3)]
    it = 0
    pendgq = []
    for ti, (o, n) in enumerate(TILES):
        ssG_t, ssG_b = psum[6].ap(), psb[6]
        for hh in range(2):
            pt, pb = nextbank(allowed=(0, 1, 2, 3))
            for h4 in range(4):
                h = hh * 4 + h4
                out = pt[:, h4 * 128:h4 * 128 + n]
                if ti < 8:
                    rhs, rhi, rlo = WsT.ap[:, h, :], bs_hi.ap[0:1, h, :], bs_lo.ap[0:1, h, :]
                    rb_ = [WsT.b]
                else:
                    rhs, rhi, rlo = Wblk.ap[0:64, h, :], bsS_hi.ap[0:1, h, :], bsS_lo.ap[0:1, h, :]
                    rb_ = [Wblk.b]
                mm(out, GV.ap[0:n, ti, h * 128:(h + 1) * 128], rhs, True, False, [GV.bufs[ti]] + rb_, [pb])
                mm(out, ones_r.ap[0:1, :], rhi, False, False, [ones_r.b, bs_hi.b, bsS_hi.b], [pb])
                mm(out, ones_r.ap[0:1, :], rlo, False, True, [ones_r.b, bs_lo.b, bsS_lo.b], [pb])
            go, sqg = GO[it % 2], SQG[it % 3]
            it += 1
            hs = slice(hh * 4, hh * 4 + 4)
            tt("dve", go.ap[:, :, 0:n], pt.rearrange("p (a b) -> p a b", a=4)[:, :, 0:n], GUT.ap[:, hs, o:o + n],
               ALU.mult, [pb] + GUT.bufs[hh * 4:hh * 4 + 4], [go.b])
            tt("dve", MT.ap[:, 8 + hh * 4:12 + hh * 4, o:o + n], go.ap[:, :, 0:n],
               gG.ap[:, hs].unsqueeze(2).broadcast_to([128, 4, n]), ALU.mult, [go.b, gG.b], MT.bufs[8 + hh * 4:12 + hh * 4])
            act(sqg.ap[:, :, 0:n], go.ap[:, :, 0:n], AF.Square, [go.b], [sqg.b])
            if len(pendgq) >= 2:
                pendgq.pop(0)()

            def pendg(sqg=sqg, n=n, hh=hh, o=o):
                for h4 in range(4):
                    mm(ssG_t[:, 0:n], ones_b.ap, sqg.ap[:, h4, 0:n], hh == 0 and h4 == 0, hh == 1 and h4 == 3,
                       [ones_b.b, sqg.b], [ssG_b])
                if hh == 1:
                    ts("dve", RG.ap[:, o:o + n], ssG_t[:, 0:n], 1.0 / 1024, EPS, ALU.mult, ALU.add, [ssG_b], [RG.b])
            pendgq.append(pendg)
            if hh == 1 and ti < 8:
                ada_unit(5)
    while pendgq:
        pendgq.pop(0)()
    rsqrt_inplace(RG.ap, RG.b)
    for t_ in (cT, siluT, GO[0], GO[1], SQG[0], SQG[1], SQG[2], GUT, GV, WsT, Wblk, bs_hi, bs_lo, bsS_hi, bsS_lo, ones_r):
        A.free(t_)

    if stop == 8:
        return finish()
    OT = A.alloc("OT", [16, TM], F32, nbufs=16)
    ta = [A.alloc("ta%d" % i, [512], F32) for i in range(2)]
    tb = [A.alloc("tb%d" % i, [512], F32) for i in range(2)]
    sqo = [A.alloc("sqo%d" % i, [512], BF16) for i in range(3)]
    ssO = [(psum[5].ap(), psb[5]), (psum[6].ap(), psb[6]), (psum[7].ap(), psb[7])]
    it = 0
    wo_stream = Stream([wout_d[u] for u in range(8)])
    wo_stream.prefetch(3)
    pendq = []
    for u in range(8):
        slot = wo_stream.get()
        wo_stream.prefetch(1)
        w = slot.ap.rearrange("p (k j c) -> p k j c", k=16, j=2)
        for j in range(2):
            m = 2 * u + j
            for gi, (o, n) in enumerate(GROUPS):
                p1, b1 = nextbank(allowed=(0, 1, 2, 3, 4))
                p2, b2 = nextbank(allowed=(0, 1, 2, 3, 4))
                for kc in range(8):
                    mm(p1[:, 0:n], w[:, kc, j, :], MT.ap[:, kc, o:o + n], kc == 0, kc == 7, [slot.b, MT.bufs[kc]], [b1])
                for kc in range(8, 16):
                    mm(p2[:, 0:n], w[:, kc, j, :], MT.ap[:, kc, o:o + n], kc == 8, kc == 15, [slot.b, MT.bufs[kc]], [b2])
                a_, b__, q_ = ta[it % 2], tb[it % 2], sqo[it % 3]
                it += 1
                tt("dve", a_.ap[:, 0:n], p1[:, 0:n], RA.ap[:, o:o + n], ALU.mult, [b1, RA.b], [a_.b])
                tt("dve", b__.ap[:, 0:n], p2[:, 0:n], RG.ap[:, o:o + n], ALU.mult, [b2, RG.b], [b__.b])
                tt("pool", OT.ap[:, m, o:o + n], a_.ap[:, 0:n], b__.ap[:, 0:n], ALU.add, [a_.b, b__.b], [OT.bufs[m]])
                act(q_.ap[:, 0:n], OT.ap[:, m, o:o + n], AF.Square, [OT.bufs[m]], [q_.b])
                if len(pendq) >= 2:
                    pendq.pop(0)()

                def pend_(q_=q_, gi=gi, n=n, m=m):
                    mm(ssO[gi][0][:, 0:n], ones_b.ap, q_.ap[:, 0:n], m == 0, m == 15, [ones_b.b, q_.b], [ssO[gi][1]])
                pendq.append(pend_)
    while pendq:
        pendq.pop(0)()
    if os.environ.get("K_DBG"):
        dbg_mt = nc.dram_tensor("dbg_mt", [128, 16 * TM], BF16, kind="ExternalOutput").ap()
        dbg_ra = nc.dram_tensor("dbg_ra", [128, TM], F32, kind="ExternalOutput").ap()
        dbg_rg = nc.dram_tensor("dbg_rg", [128, TM], F32, kind="ExternalOutput").ap()
        dbg_cm = nc.dram_tensor("dbg_cm", [128, 16 * 17], F32, kind="ExternalOutput").ap()
        dbg_ot = nc.dram_tensor("dbg_ot", [128, 16 * TM], F32, kind="ExternalOutput").ap()
        db = Buf("dbg")
        out_tickets.append(ld(dbg_mt, MT.ap.rearrange("p a b -> p (a b)"), [db], MT.bufs, sembuf=db))
        out_tickets.append(ld(dbg_ra, RA.ap, [db], [RA.b], sembuf=db))
        out_tickets.append(ld(dbg_rg, RG.ap, [db], [RG.b], sembuf=db))
        out_tickets.append(ld(dbg_cm, cm.ap.rearrange("p a b -> p (a b)"), [db], [cm.b], sembuf=db))
        out_tickets.append(ld(dbg_ot, OT.ap.rearrange("p a b -> p (a b)"), [db], OT.bufs, sembuf=db))
    for t_ in (ta[0], ta[1], tb[0], tb[1], sqo[0], sqo[1], sqo[2], RA, RG):
        A.free(t_)
    A.free(MT)
    RB2 = A.alloc("RB2", [TM], F32)
    for gi, (o, n) in enumerate(GROUPS):
        ts("dve", RB2.ap[:, o:o + n], ssO[gi][0][:, 0:n], 1.0 / D, EPS, ALU.mult, ALU.add, [ssO[gi][1]], [RB2.b])
    rsqrt_inplace(RB2.ap, RB2.b)

    if stop == 9:
        return finish()
    assert ada_state["i"] == 48

    def residual(src, src_bufs, rb, c_t, x_src_fn, x_src_bufs_fn, dst_fn, dst_bufs_fn, after_fn):
        xc = [A.alloc("xc%d" % i, [TM], F32) for i in range(3)]
        tm2 = [A.alloc("rtm%d" % i, [TM], F32) for i in range(2)]
        tsm = A.alloc("rts", [64], F32)
        for kc in range(16):
            x_, t_ = xc[kc % 3], tm2[kc % 2]
            ld(x_.ap, x_src_fn(kc), [x_.b], x_src_bufs_fn(kc))
            tt("dve", t_.ap, src.ap[:, kc, :], rb.ap, ALU.mult, [src_bufs[kc], rb.b], [t_.b])
            dst, dbufs = dst_fn(kc), dst_bufs_fn(kc)
            stt(dst[:, 0:TP], t_.ap[:, 0:TP], c_t.ap[:, kc, 0:1], x_.ap[:, 0:TP], ALU.mult, ALU.add,
                [t_.b, c_t.b, x_.b], dbufs)
            tt("dve", tsm.ap.rearrange("p (b t) -> p b t", t=4), t_.ap[:, TP:TM].rearrange("p (b t) -> p b t", t=4),
               c_t.ap[:, kc, 1:17].unsqueeze(2).broadcast_to([128, 16, 4]), ALU.mult, [t_.b, c_t.b], [tsm.b])
            tt("dve", dst[:, TP:TM], tsm.ap, x_.ap[:, TP:TM], ALU.add, [tsm.b, x_.b], dbufs)
            after_fn(kc)
        for t_ in xc + tm2 + [tsm]:
            A.free(t_)

    sq2 = [A.alloc("sqx%d" % i, [TM], BF16) for i in range(2)]
    ss1 = sumsq_banks()

    def after_x1(kc):
        sq = sq2[kc % 2]
        act(sq.ap, OT.ap[:, kc, :], AF.Square, [OT.bufs[kc]], [sq.b])
        for gi, (o, n) in enumerate(GROUPS):
            mm(ss1[gi][0][:, 0:n], ones_b.ap, sq.ap[:, o:o + n], kc == 0, kc == 15, [ones_b.b, sq.b], [ss1[gi][1]])
        ld(x1_dr[kc * 128:(kc + 1) * 128, :], OT.ap[:, kc, :], [x1_dr_bufs[kc]], [OT.bufs[kc]], eng="act",
           sembuf=x1_dr_bufs[kc])

    residual(OT, OT.bufs, RB2, cm, lambda kc: xT[kc * 128:(kc + 1) * 128, 0:TM], lambda kc: (),
             lambda kc: OT.ap[:, kc, :], lambda kc: [OT.bufs[kc]], after_x1)
    RB3 = A.alloc("RB3", [TM], F32)
    for gi, (o, n) in enumerate(GROUPS):
        ts("dve", RB3.ap[:, o:o + n], ss1[gi][0][:, 0:n], 1.0 / D, EPS, ALU.mult, ALU.add, [ss1[gi][1]], [RB3.b])
    rsqrt_inplace(RB3.ap, RB3.b)
    H2T = A.alloc("H2T", [16, TM], BF16, nbufs=16)
    tmp2 = [A.alloc("tmpB%d" % i, [TT], F32) for i in range(2)]
    tmps = A.alloc("tmpsB", [64], F32)
    for kc in range(16):
        modulate_kc(H2T, H2T.bufs[kc], OT.ap[:, kc, :], OT.bufs[kc], RB3, a2, a2.b, 3, kc, TM, False, kc)
    for t_ in (tmp2[0], tmp2[1], tmps, RB2, RB3, sq2[0], sq2[1]):
        A.free(t_)
    A.free(OT)

    if stop == 10:
        return finish()
    for t_ in ring_extra:
        ring.remove(t_)
        A.free(t_)
    ACC = A.alloc("ACC", [16, TM], F32, nbufs=16)
    HID = A.alloc("HID", [16, TM], BF16, nbufs=16)
    W2G = A.alloc("W2G", [8, 2048], BF16, nbufs=4)
    rl = [A.alloc("rl%d" % i, [512], F32) for i in range(2)]
    state = {"it": 0}

    w1_stream = Stream([wff1_d[u] for u in range(32)])

    def ffn1(g):
        for uu in range(4):
            slot = w1_stream.get()
            w = slot.ap.rearrange("p (k j c) -> p k j c", k=16, j=2)
            for j in range(2):
                hs = (g * 8 + uu * 2 + j) % 16
                for (o, n) in GROUPS:
                    pt, pb = nextbank()
                    for kc in range(16):
                        mm(pt[:, 0:n], w[:, kc, j, :], H2T.ap[:, kc, o:o + n], kc == 0, kc == 15,
                           [slot.b, H2T.bufs[kc]], [pb])
                    r_ = rl[state["it"] % 2]
                    state["it"] += 1
                    act(r_.ap[:, 0:n], pt[:, 0:n], AF.Relu, [pb], [r_.b])
                    tt("dve", HID.ap[:, hs, o:o + n], r_.ap[:, 0:n], r_.ap[:, 0:n], ALU.mult, [r_.b], [HID.bufs[hs]])

    def load_w2(g):
        for uu in range(4):
            dst = w2cur["t"].ap[:, 2 * uu:2 * uu + 2, :].rearrange("p a b -> p (a b)")
            fw.dma("pool", lambda e, uu=uu, g=g, dst=dst: e.dma_start(
                out=dst, in_=wff2_d[g * 4 + uu], max_dma_last_dim=4096), (), [w2cur["t"].bufs[uu]])

    ffin = {"on": False}
    pendf = []

    def ffn2(g):
        final = ffin["on"]
        for m in range(16):
            for (o, n) in GROUPS:
                pt, pb = nextbank(allowed=(0, 1, 2, 3, 4)) if final else nextbank()
                for k8 in range(8):
                    hs = (g * 8 + k8) % 16
                    mm(pt[:, 0:n], w2cur["t"].ap[:, k8, m * 128:(m + 1) * 128], HID.ap[:, hs, o:o + n], k8 == 0, k8 == 7,
                       [w2cur["t"].bufs[k8 // 2], HID.bufs[hs]], [pb])
                if g == 0:
                    act(ACC.ap[:, m, o:o + n], pt[:, 0:n], AF.Identity, [pb], [ACC.bufs[m]])
                else:
                    tt("dve", ACC.ap[:, m, o:o + n], pt[:, 0:n], ACC.ap[:, m, o:o + n], ALU.add, [pb, ACC.bufs[m]], [ACC.bufs[m]])
            if final:
                sq = sqf[m % 3]
                act(sq.ap, ACC.ap[:, m, :], AF.Square, [ACC.bufs[m]], [sq.b])
                if len(pendf) >= 2:
                    pendf.pop(0)()

                def pf(sq=sq, m=m):
                    for gi, (o, n) in enumerate(GROUPS):
                        mm(ssF[gi][0][:, 0:n], ones_b.ap, sq.ap[:, o:o + n], m == 0, m == 15, [ones_b.b, sq.b], [ssF[gi][1]])
                pendf.append(pf)
        while pendf:
            pendf.pop(0)()

    w1_stream.prefetch(2)
    ffn1(0)
    w2cur = {"t": W2G}
    for g in range(8):
        if g < 7:
            load_w2(g)
        if g + 1 < 8:
            ffn1(g + 1)
            w1_stream.prefetch(2)
            if g + 1 == 7:
                A.free(H2T)
                W2G7 = A.alloc("W2G7", [8, 2048], BF16, nbufs=4)
        if g == 7:
            w2cur["t"] = W2G7
            load_w2(7)
            A.free(W2G)
            sqf = [A.alloc("sqf%d" % i, [TM], BF16) for i in range(3)]
            ssF = [(psum[5 + i].ap(), psb[5 + i]) for i in range(3)]
            ffin["on"] = True
        ffn2(g)
    for t_ in (HID, W2G7, rl[0], rl[1]):
        A.free(t_)

    if stop == 11:
        return finish()
    RB4 = A.alloc("RB4", [TM], F32)
    for gi, (o, n) in enumerate(GROUPS):
        ts("dve", RB4.ap[:, o:o + n], ssF[gi][0][:, 0:n], 1.0 / D, EPS, ALU.mult, ALU.add, [ssF[gi][1]], [RB4.b])
    rsqrt_inplace(RB4.ap, RB4.b)
    yo = [A.alloc("yo%d" % i, [TM], F32) for i in range(3)]
    y_b = Buf("yT")

    def after_y(kc):
        out_tickets.append(ld(yT[kc * 128:(kc + 1) * 128, :], yo[kc % 3].ap, [y_b], [yo[kc % 3].b], eng="act",
                              sembuf=yo[kc % 3].b))

    residual(ACC, ACC.bufs, RB4, cf, lambda kc: x1_dr[kc * 128:(kc + 1) * 128, :], lambda kc: [x1_dr_bufs[kc]],
             lambda kc: yo[kc % 3].ap, lambda kc: [yo[kc % 3].b], after_y)

    return finish()


def _t5_bucket_np(n):
    n = np.maximum(n, 0)
    nf = np.maximum(n, 1).astype(np.float32)
    large = 16 + (np.log(nf / np.float32(16)) / np.float32(math.log(8.0)) * np.float32(16)).astype(np.int32)
    large = np.minimum(large, 31)
    return np.where(n < 16, n, large)


def _units_kjc(w, nu):
    return np.ascontiguousarray(w.reshape(16, 128, nu, 2, 128).transpose(2, 1, 0, 3, 4).reshape(nu, 128, UNITW))


_PROGRAM = {}


def kernel(x_prompt, x_sample, cache_k, cache_v, c_prompt, c_sample, rel_bias_table, w_ada, b_ada,
           g_pre_mix, w_in, attn_sinks, gmlp_v_gain, gmlp_w_s, gmlp_b_s, g_attn_out, g_gmlp_out,
           w_out, g_post_mix, g_pre_ff, w_ff1, w_ff2, g_post_ff):
    f32 = np.float32
    x_prompt = np.asarray(x_prompt, f32)
    x_sample = np.asarray(x_sample, f32)
    cache_k = np.asarray(cache_k, f32)
    cache_v = np.asarray(cache_v, f32)
    c_prompt = np.asarray(c_prompt, f32)
    c_sample = np.asarray(c_sample, f32)
    w_in0 = np.asarray(w_in, f32)[0]

    wada_u = _units_kjc(np.asarray(w_ada, f32)[0], 48)
    qcols = []
    for c in range(8):
        qcols += list(range(c * 64, (c + 1) * 64)) + list(range((8 + c) * 64, (9 + c) * 64))
    fm_cols = qcols + list(range(1024, 1152)) + list(range(1280, 2304))
    wfm = np.zeros((2048, 18 * 128), f32)
    wfm[:, :17 * 128] = w_in0[:, fm_cols]
    wfm_u = _units_kjc(wfm, 9)
    wtm = np.zeros((2048, 5 * 256), f32)
    wtm[:, 0:1024] = w_in0[:, 2304:3328]
    wtm[:, 1024:1152] = w_in0[:, 1152:1280]
    wtm_u = np.ascontiguousarray(wtm.reshape(16, 128, 5, 256).transpose(2, 1, 0, 3).reshape(5, 128, UNITW))
    perm = []
    for kc in range(16):
        for p in range(128):
            if kc < 8:
                perm.append(kc * 64 + p if p < 64 else (8 + kc) * 64 + (p - 64))
            else:
                perm.append(1024 + (kc - 8) * 128 + p)
    perm = np.array(perm)
    wout_u = _units_kjc(np.asarray(w_out, f32)[0][perm, :], 8)
    wff1_u = _units_kjc(np.asarray(w_ff1, f32)[0], 32)
    wff2_u = np.ascontiguousarray(np.asarray(w_ff2, f32)[0].reshape(32, 2, 128, 2048).transpose(0, 2, 1, 3).reshape(32, 128, UNITW))

    def fm16(g):
        return np.asarray(g, f32).reshape(16, 128).T

    gT = np.ascontiguousarray(np.concatenate([fm16(g_pre_mix[0]), fm16(g_post_mix[0]), fm16(g_pre_ff[0]), fm16(g_post_ff[0])], axis=1))
    ga = np.asarray(g_attn_out, f32)[0]
    gA = np.zeros((128, 8), f32)
    for c in range(8):
        gA[0:64, c] = ga[c * 64:(c + 1) * 64]
        gA[64:128, c] = ga[(8 + c) * 64:(9 + c) * 64]
    gG = np.ascontiguousarray(np.asarray(g_gmlp_out, f32)[0].reshape(8, 128).T)
    badaT = np.ascontiguousarray(np.asarray(b_ada, f32)[0].reshape(96, 128).T)
    order = [c + 8 * half for c in range(8) for half in range(2)]
    tblP = np.ascontiguousarray(np.asarray(rel_bias_table, f32)[:, order])
    sinksP = np.ascontiguousarray(np.asarray(attn_sinks, f32)[0][order].reshape(1, 16))
    vgain = np.ascontiguousarray(np.asarray(gmlp_v_gain, f32)[0].reshape(1, 1024))
    bs = np.ascontiguousarray(np.asarray(gmlp_b_s, f32)[0].reshape(1, 1024))
    wsT = np.ascontiguousarray(np.asarray(gmlp_w_s, f32)[0].transpose(2, 0, 1))
    ident = np.eye(128, dtype=f32)
    tri = (np.arange(128)[:, None] <= np.arange(128)[None, :]).astype(f32)
    bi = np.arange(64)
    blkmask = ((bi[:, None] // 4 == bi[None, :] // 4) & (bi[:, None] % 4 <= bi[None, :] % 4)).astype(f32)
    oh = (_t5_bucket_np(np.arange(128))[None, :] == np.arange(32)[:, None]).astype(f32)

    shared = dict(ident=ident, tri=tri, blkmask=blkmask, oh=oh, tblP=tblP, sinksP=sinksP, gT=gT, gA=gA, gG=gG,
                  badaT=badaT, vgain=vgain, bs=bs, wsT=wsT, wada_u=wada_u, wfm_u=wfm_u, wtm_u=wtm_u,
                  wout_u=wout_u, wff1_u=wff1_u, wff2_u=wff2_u)

    in_maps = []
    for core in range(NCORES):
        bp, half = core // 2, core % 2
        xp = x_prompt[bp, half * TP:(half + 1) * TP]
        xs = x_sample[16 * core:16 * core + 16].reshape(64, D)
        halo = x_prompt[bp, TP - TH:TP] if half == 1 else np.zeros((TH, D), f32)
        xTc = np.ascontiguousarray(np.concatenate([xp, xs, halo], axis=0).T)
        cc = np.concatenate([c_prompt[bp:bp + 1], c_sample[16 * core:16 * core + 16]], axis=0)
        cTc = np.ascontiguousarray(cc.T.reshape(16, 128, 17).transpose(1, 0, 2).reshape(128, 16 * 17))
        ck = cache_k[0, 16 * core:16 * core + 16]
        ckT = np.ascontiguousarray(ck.transpose(2, 3, 0, 1).reshape(128, 16 * 128))
        cv2 = np.ascontiguousarray(cache_v[0, 16 * core:16 * core + 16].transpose(1, 0, 2, 3).reshape(128, 16 * 128))
        blk0 = np.full((128, 1), NEG if half == 0 else 0.0, f32)
        m = dict(shared)
        m.update(xT=xTc, cT=cTc, ckT=ckT, cv2=cv2, blk0=blk0)
        in_maps.append(m)

    if "nc" not in _PROGRAM:
        _PROGRAM["nc"] = build_program()
    res = run_bass_kernel_spmd(_PROGRAM["nc"], in_maps, core_ids=list(range(NCORES)))
    outs = res.results

    y_p = np.zeros((4, 2048, D), f32)
    y_s = np.zeros((128, 4, D), f32)
    nkp = np.zeros((1, 4, 128, 2, 64), f32)
    nvp = np.zeros((1, 4, 128, 2, 64), f32)
    nks = np.zeros((1, 128, 4, 2, 64), f32)
    nvs = np.zeros((1, 128, 4, 2, 64), f32)
    gvs = np.zeros((1, 128, 4, 8, 128), f32)
    for core in range(NCORES):
        bp, half = core // 2, core % 2
        o = outs[core]
        yTc = np.asarray(o["yT"])
        y_p[bp, half * TP:(half + 1) * TP] = yTc[:, 0:TP].T
        y_s[16 * core:16 * core + 16] = yTc[:, TP:TM].T.reshape(16, 4, D)
        kTo = np.asarray(o["kT_out"])
        vo = np.asarray(o["v_out"])
        if half == 1:
            nkp[0, bp] = kTo[:, 0:128].T.reshape(128, 2, 64)
            nvp[0, bp] = vo[0:128].reshape(128, 2, 64)
        nks[0, 16 * core:16 * core + 16] = kTo[:, 128:192].T.reshape(16, 4, 2, 64)
        nvs[0, 16 * core:16 * core + 16] = vo[128:192].reshape(16, 4, 2, 64)
        gvs[0, 16 * core:16 * core + 16] = np.asarray(o["gv_out"]).reshape(16, 4, 8, 128)
    return (y_p, y_s, nkp, nvp, nks, nvs, gvs)
```

```python
import bisect
import math
import numpy as np
import concourse.bass as bass
import concourse.mybir as mybir
from concourse.bass_utils import run_bass_kernel_spmd

F32 = mybir.dt.float32
BF16 = mybir.dt.bfloat16
AF = mybir.ActivationFunctionType
ALU = mybir.AluOpType
AX = mybir.AxisListType

NCORES = 8
D = 2048
TP, TS, TH = 1024, 64, 128
TM = TP + TS
TT = TM + TH
GROUPS = [(0, 512), (512, 512), (1024, 64)]
EPS = 1e-6
NEG = -30000.0
UNITW = 4096


class Buf:
    __slots__ = ("name", "last_write", "readers", "dsem", "dcount", "excl")

    def __init__(self, name, excl=False):
        self.name = name
        self.excl = excl
        self.last_write = None
        self.readers = []
        self.dsem = None
        self.dcount = 0


class _Op:
    __slots__ = ("fn", "waits", "inc", "is_dma", "dma_sem")

    def __init__(self, fn, is_dma=False, dma_sem=None):
        self.fn = fn
        self.waits = []
        self.inc = None
        self.is_dma = is_dma
        self.dma_sem = dma_sem


class _Eng:
    def __init__(self, name, sem):
        self.name = name
        self.sem = sem
        self.count = 0
        self.ops = []
        self.inc_seqs = []
        self.inc_vals = []
        self.waited = {}
        self.last_compute = -1


class FW:
    SAME_ENGINE_SYNC = True

    def __init__(self, nc):
        self.nc = nc
        self.engs = {}
        for n in ("pe", "act", "dve", "pool", "sp"):
            self.engs[n] = _Eng(n, nc.alloc_semaphore("s_" + n))
        self.nsem = 5

    def _resolve(self, t):
        if t[0] == "d":
            return t[1], t[2]
        E = self.engs[t[1]]
        seq = t[2]
        i = bisect.bisect_left(E.inc_seqs, seq)
        if i < len(E.inc_seqs):
            return E.sem, E.inc_vals[i]
        s = E.last_compute
        assert s >= seq
        E.count += 1
        E.ops[s].inc = E.count
        E.inc_seqs.append(s)
        E.inc_vals.append(E.count)
        return E.sem, E.count

    def _need(self, E, op, t, kind):
        if t is None:
            return
        if t[0] == "e" and t[1] == E.name:
            if E.name == "pe":
                return
            if kind == "rar" or not self.SAME_ENGINE_SYNC:
                return
            if kind == "waw" and getattr(self, "nowaw", False):
                return
        sem, val = self._resolve(t)
        key = id(sem)
        if E.waited.get(key, 0) >= val:
            return
        E.waited[key] = val
        op.waits.append((sem, val))

    def _deps(self, E, op, reads, writes):
        for b in reads:
            self._need(E, op, b.last_write, "raw")
            if b.excl:
                for r in b.readers:
                    self._need(E, op, r, "rar")
        for b in writes:
            self._need(E, op, b.last_write, "waw")
            for r in b.readers:
                self._need(E, op, r, "war")

    def _commit(self, t, reads, writes):
        for b in writes:
            b.last_write = t
            b.readers = []
        for b in reads:
            if t[0] == "e":
                b.readers = [r for r in b.readers if not (r[0] == "e" and r[1] == t[1])]
            else:
                b.readers = [r for r in b.readers if not (r[0] == "d" and r[1] is t[1])]
            b.readers.append(t)

    def op(self, eng, fn, reads=(), writes=(), nowaw=False):
        E = self.engs[eng]
        o = _Op(fn)
        self.nowaw = nowaw
        self._deps(E, o, reads, writes)
        self.nowaw = False
        E.ops.append(o)
        seq = len(E.ops) - 1
        E.last_compute = seq
        self._commit(("e", eng, seq), reads, writes)

    def dma(self, eng, fn, reads=(), writes=(), sembuf=None, par=False):
        E = self.engs[eng]
        sb = sembuf or (writes[0] if writes else reads[0])
        if sb.dsem is None:
            sb.dsem = self.nc.alloc_semaphore("d%d" % self.nsem)
            self.nsem += 1
        o = _Op(fn, True, sb.dsem)
        saved = []
        if par:
            for b in writes:
                lw = b.last_write
                if lw is not None and lw[0] == "d" and lw[1] is sb.dsem:
                    saved.append((b, lw))
                    b.last_write = None
        self._deps(E, o, reads, writes)
        for b, lw in saved:
            b.last_write = lw
        E.ops.append(o)
        sb.dcount += 16
        t = ("d", sb.dsem, sb.dcount)
        self._commit(t, reads, writes)
        return t

    def final_wait(self, eng, tickets):
        E = self.engs[eng]
        o = _Op(None)
        for t in tickets:
            self._need(E, o, t, "raw")
        E.ops.append(o)

    def emit(self):
        nc = self.nc
        hw = {"pe": "tensor", "act": "scalar", "dve": "vector", "pool": "gpsimd", "sp": "sync"}
        with nc.Block() as block:
            for n, E in self.engs.items():
                if not E.ops:
                    continue

                def body(e, E=E):
                    for o in E.ops:
                        for sem, val in o.waits:
                            e.wait_ge(sem, val)
                        if o.fn is None:
                            continue
                        ins = o.fn(e)
                        if o.is_dma:
                            ins.then_inc(o.dma_sem, 16)
                        elif o.inc is not None:
                            ins.then_inc(E.sem, 1)

                getattr(block, hw[n])(body)


class T:
    def __init__(self, name, ap, s, e, nbufs):
        self.name = name
        self.ap = ap
        self.s = s
        self.e = e
        self.bufs = [Buf("%s.%d" % (name, i)) for i in range(nbufs)]

    @property
    def b(self):
        return self.bufs[0]


class Arena:
    def __init__(self, big_ap, nwords):
        self.big = big_ap
        self.free_list = [(0, nwords)]
        self.retired = []
        self.peak = 0
        self.live = {}

    def alloc(self, name, shape, dtype, nbufs=1):
        n = 1
        for d_ in shape:
            n *= d_
        esz = 2 if dtype == BF16 else 4
        nw = (n * esz + 3) // 4
        nw = (nw + 7) // 8 * 8
        top = nw < 3000
        order = range(len(self.free_list) - 1, -1, -1) if top else range(len(self.free_list))
        for i in order:
            s, e = self.free_list[i]
            if e - s >= nw:
                break
        else:
            raise RuntimeError("arena out of SBUF for %s (%d words); free=%s live=%s" % (
                name, nw, self.free_list, sorted((v, k) for k, v in self.live.items())))
        if e - s == nw:
            self.free_list.pop(i)
        elif top:
            self.free_list[i] = (s, e - nw)
            s = e - nw
        else:
            self.free_list[i] = (s + nw, e)
        e = s + nw
        self.peak = max(self.peak, e)
        ap = self.big[:, s:e]
        if dtype == BF16:
            ap = ap.bitcast(BF16)
        ap = ap[:, 0:n]
        if len(shape) == 2:
            ap = ap.rearrange("p (a b) -> p a b", a=shape[0])
        elif len(shape) == 3:
            ap = ap.rearrange("p (a b c) -> p a b c", a=shape[0], b=shape[1])
        t = T(name, ap, s, e, nbufs)
        self.live[name] = (s, e)
        tick = []
        keep = []
        for (rs, re, tk) in self.retired:
            if rs < e and s < re:
                tick.extend(tk)
                if not (s <= rs and re <= e):
                    keep.append((rs, re, tk))
            else:
                keep.append((rs, re, tk))
        self.retired = keep
        for b in t.bufs:
            b.readers = list(tick)
        return t

    def free(self, t):
        self.live.pop(t.name, None)
        tk = []
        for b in t.bufs:
            if b.last_write is not None:
                tk.append(b.last_write)
            tk.extend(b.readers)
        seen = set()
        tk2 = []
        for x in tk:
            k = (x[0], id(x[1]) if x[0] == "d" else x[1], x[2])
            if k not in seen:
                seen.add(k)
                tk2.append(x)
        self.retired.append((t.s, t.e, tk2))
        fl = self.free_list + [(t.s, t.e)]
        fl.sort()
        merged = []
        for s, e in fl:
            if merged and merged[-1][1] == s:
                merged[-1] = (merged[-1][0], e)
            else:
                merged.append((s, e))
        self.free_list = merged


def build_program(stop=None):
    import os
    stop = int(os.environ.get("K_STOP", "99")) if stop is None else stop
    nc = bass.Bass("TRN2", target_bir_lowering=False)
    fw = FW(nc)

    def finish():
        fw.final_wait("sp", out_tickets)
        fw.emit()
        return nc

    def din(name, shape):
        return nc.dram_tensor(name, list(shape), F32, kind="ExternalInput").ap()

    def dout(name, shape):
        return nc.dram_tensor(name, list(shape), F32, kind="ExternalOutput").ap()

    xT = din("xT", [D, TT])
    cT_d = din("cT", [128, 16 * 17])
    ident_d = din("ident", [128, 128])
    tri_d = din("tri", [128, 128])
    blkmask_d = din("blkmask", [64, 64])
    oh_d = din("oh", [32, 128])
    blk0_d = din("blk0", [128, 1])
    tbl_d = din("tblP", [32, 16])
    sink_d = din("sinksP", [1, 16])
    gT_d = din("gT", [128, 64])
    gA_d = din("gA", [128, 8])
    gG_d = din("gG", [128, 8])
    badaT_d = din("badaT", [128, 96])
    vgain_d = din("vgain", [1, 1024])
    bs_d = din("bs", [1, 1024])
    wsT_d = din("wsT", [128, 8, 128])
    ckT_d = din("ckT", [128, 16 * 128])
    cv_d = din("cv2", [128, 16 * 128])
    wada_d = din("wada_u", [48, 128, UNITW])
    wfm_d = din("wfm_u", [9, 128, UNITW])
    wtm_d = din("wtm_u", [5, 128, UNITW])
    wout_d = din("wout_u", [8, 128, UNITW])
    wff1_d = din("wff1_u", [32, 128, UNITW])
    wff2_d = din("wff2_u", [32, 128, UNITW])

    yT = dout("yT", [D, TM])
    kT_out = dout("kT_out", [128, 192])
    v_out = dout("v_out", [192, 128])
    gv_out = dout("gv_out", [64, 1024])

    a_dr_t = nc.dram_tensor("a_scr", [16, 383], F32)
    a_dr = a_dr_t.ap()
    x1_dr = nc.dram_tensor("x1_scr", [D, TM], F32).ap()
    a_dr_buf = Buf("a_dr")
    x1_dr_bufs = [Buf("x1dr%d" % i) for i in range(16)]
    out_tickets = []

    NW = 52800
    big = nc.alloc_sbuf_tensor("big", [128, NW], F32)
    A = Arena(big.ap(), NW)
    psum = [nc.alloc_psum_tensor("ps%d" % i, [128, 512], F32) for i in range(8)]
    psb = [Buf("psb%d" % i, excl=True) for i in range(8)]
    rr = {"bank": 0, "slot": 0}

    def nextbank(allowed=None):
        allowed = allowed or range(8)
        while True:
            i = rr["bank"] % 8
            rr["bank"] += 1
            if i in allowed:
                return psum[i].ap(), psb[i]

    def mm(out, lhsT, rhs, start, stop, reads, writes, **kw):
        fw.op("pe", lambda e: e.matmul(out, lhsT, rhs, start=start, stop=stop, **kw), reads, writes)

    def tr(out, in_, ident, reads, writes):
        fw.op("pe", lambda e: e.transpose(out, in_, ident), reads, writes)

    def act(out, in_, func, reads, writes, bias=None, scale=None, accum=None, nowaw=False):
        kw = {}
        if bias is not None:
            kw["bias"] = bias
        if scale is not None:
            kw["scale"] = scale
        if accum is not None:
            kw["accum_out"] = accum
        fw.op("act", lambda e: e.activation(out, in_, func, **kw), reads, writes, nowaw=nowaw)

    def tt(eng, out, in0, in1, op, reads, writes, nowaw=False):
        fw.op(eng, lambda e: e.tensor_tensor(out, in0, in1, op), reads, writes, nowaw=nowaw)

    def ts(eng, out, in0, s1, s2, op0, op1, reads, writes):
        if op1 is None:
            fw.op(eng, lambda e: e.tensor_scalar(out, in0, s1, None, op0), reads, writes)
        else:
            fw.op(eng, lambda e: e.tensor_scalar(out, in0, s1, s2, op0, op1), reads, writes)

    def stt(out, in0, scalar, in1, op0, op1, reads, writes):
        fw.op("dve", lambda e: e.scalar_tensor_tensor(out, in0, scalar, in1, op0, op1), reads, writes)

    def cp(eng, out, in_, reads, writes):
        fw.op(eng, lambda e: e.tensor_copy(out, in_), reads, writes)

    def ld(out, in_, writes, reads=(), eng="sp", sembuf=None, par=False, **kw):
        return fw.dma(eng, lambda e: e.dma_start(out=out, in_=in_, **kw), reads, writes, sembuf=sembuf, par=par)

    def rsqrt_inplace(ap, buf, reads_extra=()):
        act(ap, ap, AF.Ln, [buf], [buf])
        act(ap, ap, AF.Exp, [buf], [buf], scale=-0.5)

    ring = [A.alloc("ring%d" % i, [UNITW], BF16) for i in range(2)]
    ring_extra = []

    def load_unit(dram_unit_ap):
        i = rr["slot"] % len(ring)
        rr["slot"] += 1
        t = ring[i]
        fw.dma("pool", lambda e: e.dma_start(out=t.ap, in_=dram_unit_ap, max_dma_last_dim=4096), (), [t.b])
        return t

    class Stream:
        def __init__(self, units):
            self.units = list(units)
            self.next = 0
            self.ready = []
            self.limit = len(self.units)

        def prefetch(self, n=1):
            while n > 0 and self.next < min(self.limit, len(self.units)):
                self.ready.append(load_unit(self.units[self.next]))
                self.next += 1
                n -= 1

        def get(self):
            if not self.ready:
                self.prefetch(1)
            return self.ready.pop(0)

    ident_f = A.alloc("ident_f", [128], F32)
    ident_b = A.alloc("ident_b", [128], BF16)
    ones_b = A.alloc("ones_b", [128], BF16)
    modT = A.alloc("modT", [96, 17], F32, nbufs=48)
    gT = A.alloc("gT", [64], F32)
    gA = A.alloc("gA", [8], F32)
    gG = A.alloc("gG", [8], F32)
    badaT = A.alloc("badaT", [96], F32)
    aM = A.alloc("aM", [16, 17], F32, nbufs=8)
    cm = A.alloc("cm", [16, 17], F32)
    a2 = A.alloc("a2", [16, 17], F32)
    cf = A.alloc("cf", [16, 17], F32)
    ld(ident_f.ap, ident_d, [ident_f.b])
    ld(gT.ap, gT_d, [gT.b])
    ld(gA.ap, gA_d, [gA.b])
    ld(gG.ap, gG_d, [gG.b])
    ld(badaT.ap, badaT_d, [badaT.b])
    cp("dve", ident_b.ap, ident_f.ap, [ident_f.b], [ident_b.b])
    fw.op("dve", lambda e: e.memset(ones_b.ap, 1.0), (), [ones_b.b])

    XT = A.alloc("XT", [16, TT], F32, nbufs=16)
    for kc in range(16):
        ld(XT.ap[:, kc, :], xT[kc * 128:(kc + 1) * 128, :], [XT.bufs[kc]])

    cT = A.alloc("cT", [16, 17], F32)
    siluT = A.alloc("siluT", [16, 17], BF16)
    ring_extra.extend(A.alloc("ringx%d" % i, [UNITW], BF16) for i in range(2))
    ring.extend(ring_extra)
    ld(cT.ap, cT_d.rearrange("p (a b) -> p a b", a=16), [cT.b])
    act(siluT.ap, cT.ap, AF.Silu, [cT.b], [siluT.b])

    ada_order = [u for i in range(8) for u in (i, 8 + i)] + list(range(16, 48))
    ada_stream = Stream([wada_d[u] for u in ada_order])
    ada_state = {"i": 0}

    def ada_unit(bank):
        u = ada_order[ada_state["i"]]
        ada_state["i"] += 1
        pt, pb = psum[bank].ap(), psb[bank]
        slot = ada_stream.get()
        w = slot.ap.rearrange("p (k j c) -> p k j c", k=16, j=2)
        for j in range(2):
            for kc in range(16):
                mm(pt[:, j * 17:(j + 1) * 17], w[:, kc, j, :], siluT.ap[:, kc, :],
                   kc == 0, kc == 15, [slot.b, siluT.b], [pb])
        ada_stream.prefetch(1)
        tt("dve", modT.ap[:, 2 * u:2 * u + 2, :], pt[:, 0:34].rearrange("p (a b) -> p a b", a=2),
           badaT.ap[:, 2 * u:2 * u + 2].unsqueeze(2).broadcast_to([128, 2, 17]), ALU.add,
           [pb, badaT.b], [modT.bufs[u]])
        if u == 23:
            tt("dve", cm.ap, seg_ap(2), gbc(1), ALU.mult, modT.bufs[16:24] + [gT.b], [cm.b])
        if u == 39:
            stt(a2.ap, seg_ap(4), 1.0, gbc(2), ALU.add, ALU.mult, modT.bufs[32:40] + [gT.b], [a2.b])
        if u == 47:
            tt("dve", cf.ap, seg_ap(5), gbc(3), ALU.mult, modT.bufs[40:48] + [gT.b], [cf.b])

    def seg_ap(seg):
        return modT.ap[:, seg * 16:(seg + 1) * 16, :]

    def gbc(i):
        return gT.ap[:, i * 16:(i + 1) * 16].unsqueeze(2).broadcast_to([128, 16, 17])

    def modulate_kc(dst, dst_buf, src, src_buf, rb, a_t, a_buf, sh_seg, kc, ncols_main, halo, idx, sample=True):
        tm = tmp2[idx % 2]
        w = ncols_main + (TH if halo else 0)
        shb = modT.bufs[sh_seg * 8 + kc // 2]
        tt("dve", tm.ap[:, 0:w], src[:, 0:w], rb.ap[:, 0:w], ALU.mult, [src_buf, rb.b], [tm.b])
        shp = modT.ap[:, sh_seg * 16 + kc, :]
        rd = [tm.b, a_buf, shb]
        act(dst.ap[:, kc, 0:TP], tm.ap[:, 0:TP], AF.Identity, rd, [dst_buf],
            bias=shp[:, 0:1], scale=a_t.ap[:, kc, 0:1])
        if halo:
            act(dst.ap[:, kc, TM:TT], tm.ap[:, TM:TT], AF.Identity, rd, [dst_buf],
                bias=shp[:, 0:1], scale=a_t.ap[:, kc, 0:1])
        if not sample:
            return
        tt("dve", tmps.ap.rearrange("p (b t) -> p b t", t=4), tm.ap[:, TP:TM].rearrange("p (b t) -> p b t", t=4),
           a_t.ap[:, kc, 1:17].unsqueeze(2).broadcast_to([128, 16, 4]), ALU.mult, [tm.b, a_buf], [tmps.b])
        tt("dve", dst.ap[:, kc, TP:TM].rearrange("p (b t) -> p b t", t=4), tmps.ap.rearrange("p (b t) -> p b t", t=4),
           shp[:, 1:17].unsqueeze(2).broadcast_to([128, 16, 4]), ALU.add, [tmps.b, shb], [dst_buf])

    HT = A.alloc("HT", [16, TT], BF16, nbufs=16)
    tmp2 = [A.alloc("tmp%d" % i, [TT], F32) for i in range(2)]
    tmps = A.alloc("tmps", [64], F32)
    ring_start = [A.alloc("ringy%d" % i, [UNITW], BF16) for i in range(2)]
    ring.extend(ring_start)
    ada_stream.limit = 16
    ada_stream.prefetch(5)
    def ada_pair_mod(i_):
        stt(aM.ap[:, 2 * i_:2 * i_ + 2, :], modT.ap[:, 16 + 2 * i_:18 + 2 * i_, :], 1.0,
            gT.ap[:, 2 * i_:2 * i_ + 2].unsqueeze(2).broadcast_to([128, 2, 17]), ALU.add, ALU.mult,
            [modT.bufs[8 + i_], gT.b], [aM.bufs[i_]])
        for kc in (2 * i_, 2 * i_ + 1):
            modulate_kc(HT, HT.bufs[kc], XT.ap[:, kc, :], XT.bufs[kc], RBC, aM, aM.bufs[i_], 0, kc, TM, True, kc)

    NPRE = 3
    for i_ in range(NPRE):
        ada_unit(6)
        ada_unit(7)
    def sumsq_banks():
        return [nextbank() for _ in range(3)]

    SSG = [(0, 512), (512, 512), (1024, 192)]
    RBC = A.alloc("RBC", [TT], F32)
    sq2 = [A.alloc("sq%d" % i, [TT], BF16) for i in range(2)]
    ssb = [(psum[i].ap(), psb[i]) for i in range(3)]
    for kc in range(16):
        sq = sq2[kc % 2]
        act(sq.ap, XT.ap[:, kc, :], AF.Square, [XT.bufs[kc]], [sq.b])
        for gi, (o, n) in enumerate(SSG):
            mm(ssb[gi][0][:, 0:n], ones_b.ap, sq.ap[:, o:o + n], kc == 0, kc == 15, [ones_b.b, sq.b], [ssb[gi][1]])
    for gi, (o, n) in enumerate(SSG):
        ts("dve", RBC.ap[:, o:o + n], ssb[gi][0][:, 0:n], 1.0 / D, EPS, ALU.mult, ALU.add, [ssb[gi][1]], [RBC.b])
    rsqrt_inplace(RBC.ap, RBC.b)

    for i_ in range(NPRE):
        ada_pair_mod(i_)
    for i_ in range(NPRE, 8):
        ada_unit(6)
        ada_unit(7)
        ada_pair_mod(i_)
    for t_ in ring_start:
        ring.remove(t_)
        A.free(t_)
    wi_stream = Stream([wfm_d[u] for u in range(9)] + [wtm_d[u] for u in range(5)])
    wi_stream.prefetch(3)
    wsT_f = A.alloc("wsT_f", [8, 128], F32)
    tri = A.alloc("tri", [128], F32)
    WsT = A.alloc("WsT", [8, 128], BF16)
    ld(wsT_f.ap, wsT_d, [wsT_f.b])
    ld(tri.ap, tri_d, [tri.b])
    tt("dve", WsT.ap, wsT_f.ap, tri.ap.unsqueeze(1).broadcast_to([128, 8, 128]), ALU.mult,
       [wsT_f.b, tri.b], [WsT.b])
    mrep = A.alloc("mrep", [8, 4], F32)
    blkm = A.alloc("blkm", [64], F32)
    Wblk = A.alloc("Wblk", [8, 64], BF16)
    ld(blkm.ap[0:64, :], blkmask_d, [blkm.b])
    for b_ in range(16):
        ld(mrep.ap[b_ * 4:(b_ + 1) * 4, :, :], wsT_d[0:4, :, 0:4], [mrep.b], par=True)
    tt("dve", Wblk.ap[0:64].rearrange("p h (b i) -> p h b i", b=16),
       mrep.ap[0:64].unsqueeze(2).broadcast_to([64, 8, 16, 4]),
       blkm.ap[0:64].rearrange("p (b i) -> p b i", b=16).unsqueeze(1).broadcast_to([64, 8, 16, 4]),
       ALU.mult, [mrep.b, blkm.b], [Wblk.b])
    bsr = A.alloc("bsr", [8, 128], F32)
    bs_hi = A.alloc("bs_hi", [8, 128], BF16)
    bs_lo = A.alloc("bs_lo", [8, 128], BF16)
    bs_t = A.alloc("bs_t", [8, 128], F32)
    bsS_hi = A.alloc("bsS_hi", [8, 64], BF16)
    bsS_lo = A.alloc("bsS_lo", [8, 64], BF16)
    ones_r = A.alloc("ones_r", [128], BF16)
    ld(bsr.ap[0:1], bs_d.rearrange("o (h i) -> o h i", h=8), [bsr.b])
    fw.op("dve", lambda e: e.memset(ones_r.ap[0:1, :], 1.0), (), [ones_r.b])
    cp("dve", bs_hi.ap[0:1], bsr.ap[0:1], [bsr.b], [bs_hi.b])
    cp("dve", bs_t.ap[0:1], bs_hi.ap[0:1], [bs_hi.b], [bs_t.b])
    tt("dve", bs_lo.ap[0:1], bsr.ap[0:1], bs_t.ap[0:1], ALU.subtract, [bsr.b, bs_t.b], [bs_lo.b])
    cp("dve", bsS_hi.ap[0:1].rearrange("p h (b i) -> p h b i", b=16),
       bs_hi.ap[0:1, :, 0:4].unsqueeze(2).broadcast_to([1, 8, 16, 4]), [bs_hi.b], [bsS_hi.b])
    cp("dve", bsS_lo.ap[0:1].rearrange("p h (b i) -> p h b i", b=16),
       bs_lo.ap[0:1, :, 0:4].unsqueeze(2).broadcast_to([1, 8, 16, 4]), [bs_lo.b], [bsS_lo.b])
    vgain = A.alloc("vgain", [1024], F32)
    ld(vgain.ap, vgain_d.partition_broadcast(128), [vgain.b])
    for t_ in (wsT_f, tri, mrep, blkm, bsr, bs_t):
        A.free(t_)

    A.free(XT)
    for t_ in (tmp2[0], tmp2[1], tmps):
        A.free(t_)
    A.free(sq2[0])
    A.free(sq2[1])
    KcT = A.alloc("KcT", [16, 128], BF16)
    Vc = A.alloc("Vc", [16, 128], BF16)
    fw.dma("pool", lambda e: e.dma_start(out=KcT.ap.rearrange("p a b -> p (a b)"), in_=ckT_d, max_dma_last_dim=4096), (), [KcT.b])
    fw.dma("pool", lambda e: e.dma_start(out=Vc.ap.rearrange("p a b -> p (a b)"), in_=cv_d, max_dma_last_dim=4096), (), [Vc.b])


    if stop == 1:
        return finish()
    tbl = A.alloc("tbl", [16], F32)
    oh = A.alloc("oh", [128], F32)
    a_sb = A.alloc("a_sb", [383], F32)
    sinkP = A.alloc("sinkP", [16], F32)
    nsinkP = A.alloc("nsinkP", [16], F32)
    sinkS = A.alloc("sinkS", [2], F32)
    nsinkS = A.alloc("nsinkS", [2], F32)
    blk0 = A.alloc("blk0", [1], F32)
    biasP = A.alloc("biasP", [16, 256], F32)
    biasS = A.alloc("biasS", [2, 132], F32)
    ld(tbl.ap[0:32, :], tbl_d, [tbl.b])
    ld(oh.ap[0:32, :], oh_d, [oh.b])
    ld(blk0.ap, blk0_d, [blk0.b])
    ld(sinkP.ap, sink_d.partition_broadcast(128), [sinkP.b])
    for g in range(8):
        src = bass.AP(tensor=sink_d.tensor, offset=2 * g, ap=[[0, 4], [1, 2]])
        ld(sinkS.ap[g * 4:(g + 1) * 4, :], src, [sinkS.b], par=True)
    ts("dve", nsinkP.ap, sinkP.ap, -1.0, None, ALU.mult, None, [sinkP.b], [nsinkP.b])
    ts("dve", nsinkS.ap[0:32, :], sinkS.ap[0:32, :], -1.0, None, ALU.mult, None, [sinkS.b], [nsinkS.b])
    pt, pb = nextbank()
    mm(pt[0:16, 0:128], tbl.ap[0:32, :], oh.ap[0:32, :], True, True, [tbl.b, oh.b], [pb])
    fw.op("dve", lambda e: e.memset(a_sb.ap[0:16, :], NEG), (), [a_sb.b])
    cp("dve", a_sb.ap[0:16, 127:255], pt[0:16, 0:128], [pb], [a_sb.b])
    ld(a_dr, a_sb.ap[0:16, :], [a_dr_buf], [a_sb.b])
    T1 = A.alloc("T1", [16, 256], F32)
    S1 = A.alloc("S1", [2, 128], F32)
    S1n = A.alloc("S1n", [2, 4], F32)
    for h_ in range(16):
        ld(T1.ap[:, h_, :], bass.AP(tensor=a_dr_t, offset=383 * h_, ap=[[1, 128], [1, 256]]), [T1.b], [a_dr_buf], par=True)
    for g in range(8):
        ld(S1.ap[g * 4:(g + 1) * 4, :, :],
           bass.AP(tensor=a_dr_t, offset=2 * g * 383 + 128, ap=[[1, 4], [383, 2], [1, 128]]), [S1.b], [a_dr_buf], par=True)
        ld(S1n.ap[g * 4:(g + 1) * 4, :, :],
           bass.AP(tensor=a_dr_t, offset=2 * g * 383 + 124, ap=[[1, 4], [383, 2], [1, 4]]), [S1n.b], [a_dr_buf], par=True)
    cp("dve", biasP.ap, T1.ap[:, :, ::-1], [T1.b], [biasP.b])
    cp("dve", biasS.ap[0:32, :, 0:128], S1.ap[0:32, :, ::-1], [S1.b], [biasS.b])
    cp("dve", biasS.ap[0:32, :, 128:132], S1n.ap[0:32, :, ::-1], [S1n.b], [biasS.b])
    for t_ in (T1, S1, S1n, a_sb, tbl, oh):
        A.free(t_)
    if stop == 3:
        return finish()
    QT = A.alloc("QT", [8, TM], BF16, nbufs=8)
    KT = A.alloc("KT", [TT], BF16)
    GUT = A.alloc("GUT", [8, TM], BF16, nbufs=8)
    KOUT = A.alloc("KOUT", [192], F32)
    fm_chunks = [("q", c) for c in range(8)] + [("k", 0)] + [("gu", h) for h in range(8)]
    for u in range(9):
        slot = wi_stream.get()
        wi_stream.prefetch(1)
        w = slot.ap.rearrange("p (k j c) -> p k j c", k=16, j=2)
        for j in range(2):
            ci = 2 * u + j
            if ci >= len(fm_chunks):
                break
            kind, idx = fm_chunks[ci]
            groups = GROUPS + ([(TM, TH)] if kind == "k" else [])
            for (o, n) in groups:
                pt, pb = nextbank()
                for kc in range(16):
                    mm(pt[:, 0:n], w[:, kc, j, :], HT.ap[:, kc, o:o + n], kc == 0, kc == 15,
                       [slot.b, HT.bufs[kc]], [pb])
                if kind == "q":
                    act(QT.ap[:, idx, o:o + n], pt[:, 0:n], AF.Identity, [pb], [QT.bufs[idx]], scale=0.125)
                elif kind == "k":
                    act(KT.ap[:, o:o + n], pt[:, 0:n], AF.Identity, [pb], [KT.b])
                    if o == 512:
                        cp("dve", KOUT.ap[:, 0:128], pt[:, 384:512], [pb], [KOUT.b])
                    if o == 1024:
                        cp("dve", KOUT.ap[:, 128:192], pt[:, 0:64], [pb], [KOUT.b])
                else:
                    act(GUT.ap[:, idx, o:o + n], pt[:, 0:n], AF.Gelu_apprx_tanh, [pb], [GUT.bufs[idx]])
    out_tickets.append(ld(kT_out, KOUT.ap, [Buf("kT_out")], [KOUT.b], sembuf=KOUT.b))

    if stop == 4:
        return finish()
    TILES = [(t_ * 128, 128) for t_ in range(8)] + [(TP, TS)]
    GV = A.alloc("GV", [9, 1024], BF16, nbufs=9)
    V = A.alloc("V", [10, 128], BF16, nbufs=10)
    GVOUT = A.alloc("GVOUT", [1024], F32)
    VOUT = A.alloc("VOUT", [2, 128], F32)
    gtmp = [A.alloc("gtmp%d" % i, [256], F32) for i in range(2)]
    sqt2 = [A.alloc("sqt%d" % i, [256], F32) for i in range(2)]
    gn2 = [A.alloc("gn%d" % i, [256], F32) for i in range(2)]
    ssv2 = [A.alloc("ssv%d" % i, [2], F32) for i in range(2)]
    mhalf = A.alloc("mhalf", [2], F32)
    fw.op("pool", lambda e: e.memset(mhalf.ap, -0.5), (), [mhalf.b])
    it = 0
    for u in range(4):
        slot = wi_stream.get()
        wi_stream.prefetch(1)
        w = slot.ap.rearrange("p (k c) -> p k c", k=16)
        for ti, (o, n) in enumerate(TILES):
            pt, pb = nextbank()
            for kc in range(16):
                mm(pt[0:n, 0:256], HT.ap[:, kc, o:o + n], w[:, kc, :], kc == 0, kc == 15,
                   [slot.b, HT.bufs[kc]], [pb])
            g_, sqt, gn, ssv = gtmp[it % 2], sqt2[it % 2], gn2[it % 2], ssv2[it % 2]
            it += 1
            act(g_.ap[0:n], pt[0:n, 0:256], AF.Gelu_apprx_tanh, [pb], [g_.b])
            for j_ in range(2):
                act(sqt.ap[0:n, j_ * 128:(j_ + 1) * 128], g_.ap[0:n, j_ * 128:(j_ + 1) * 128], AF.Square, [g_.b], [sqt.b, ssv.b],
                    accum=ssv.ap[0:n, j_:j_ + 1])
            ts("dve", ssv.ap[0:n], ssv.ap[0:n], 1.0 / 128, EPS, ALU.mult, ALU.add, [ssv.b], [ssv.b])
            tt("pool", ssv.ap[0:n], ssv.ap[0:n], mhalf.ap[0:n], ALU.pow, [ssv.b, mhalf.b], [ssv.b])
            tt("dve", gn.ap[0:n].rearrange("p (h c) -> p h c", h=2), g_.ap[0:n].rearrange("p (h c) -> p h c", h=2),
               ssv.ap[0:n].unsqueeze(2).broadcast_to([n, 2, 128]), ALU.mult, [g_.b, ssv.b], [gn.b])
            cols = slice(u * 256, (u + 1) * 256)
            tt("dve", GV.ap[0:n, ti, cols], gn.ap[0:n], vgain.ap[0:n, cols], ALU.mult, [gn.b, vgain.b], [GV.bufs[ti]])
            if ti == 8:
                tt("dve", GVOUT.ap[0:n, cols], gn.ap[0:n], vgain.ap[0:n, cols], ALU.mult, [gn.b, vgain.b], [GVOUT.b])
    out_tickets.append(ld(gv_out, GVOUT.ap[0:64, :], [Buf("gv_out")], [GVOUT.b], sembuf=GVOUT.b))
    slot = wi_stream.get()
    w = slot.ap.rearrange("p (k c) -> p k c", k=16)
    for ti, (o, n) in enumerate(TILES + [(TM, TH)]):
        pt, pb = nextbank()
        for kc in range(16):
            mm(pt[0:n, 0:128], HT.ap[:, kc, o:o + n], w[:, kc, 0:128], kc == 0, kc == 15,
               [slot.b, HT.bufs[kc]], [pb])
        act(V.ap[0:n, ti, :], pt[0:n, 0:128], AF.Identity, [pb], [V.bufs[ti]])
        if ti == 7:
            cp("dve", VOUT.ap[:, 0, :], pt[:, 0:128], [pb], [VOUT.b])
        if ti == 8:
            cp("dve", VOUT.ap[0:64, 1, :], pt[0:64, 0:128], [pb], [VOUT.b])
    vo_b = Buf("v_out")
    out_tickets.append(ld(v_out[0:128, :], VOUT.ap[:, 0, :], [vo_b], [VOUT.b], sembuf=VOUT.b))
    out_tickets.append(ld(v_out[128:192, :], VOUT.ap[0:64, 1, :], [vo_b], [VOUT.b], sembuf=VOUT.b))
    for t_ in (gtmp[0], gtmp[1], sqt2[0], sqt2[1], gn2[0], gn2[1], ssv2[0], ssv2[1], mhalf, vgain, RBC, KOUT, VOUT, GVOUT):
        A.free(t_)
    A.free(HT)

    if stop == 5:
        return finish()
    ada_stream.limit = 48
    ada_stream.prefetch(3)

    MT = A.alloc("MT", [16, TM], BF16, nbufs=16)
    RA = A.alloc("RA", [TM], F32)
    S2 = [A.alloc("S2_%d" % i, [4, 256], F32) for i in range(2)]
    Pb = [A.alloc("P_%d" % i, [4, 256], BF16) for i in range(2)]
    PTs = [A.alloc("PT_%d" % i, [8, 128], BF16) for i in range(2)]
    SQA = A.alloc("SQA", [256], BF16)
    st = A.alloc("st", [32], F32)
    def att_1a(qb, m, idx):
        qo = qb * 128
        sbanks = [(psum[(idx % 2) * 2 + i].ap(), psb[(idx % 2) * 2 + i]) for i in range(2)]
        s2 = S2[idx % 2]
        for s in range(4):
            c, half = 2 * m + s // 2, s % 2
            bank, bb = sbanks[half]
            cb = (s // 2) * 256
            hp = slice(half * 64, (half + 1) * 64)
            lh = QT.ap[hp, c, qo:qo + 128]
            if qb >= 1:
                mm(bank[:, cb:cb + 256], lh, KT.ap[hp, qo - 128:qo + 128], True, True, [QT.bufs[c], KT.b], [bb])
            else:
                mm(bank[:, cb:cb + 128], lh, KT.ap[hp, TM:TT], True, True, [QT.bufs[c], KT.b], [bb])
                mm(bank[:, cb + 128:cb + 256], lh, KT.ap[hp, 0:128], True, True, [QT.bufs[c], KT.b], [bb])
        for i in range(2):
            tt("dve", s2.ap[:, i:4:2, :], sbanks[i][0].rearrange("p (a b) -> p a b", a=2),
               biasP.ap[:, 4 * m + i:4 * m + 4:2, :], ALU.add, [sbanks[i][1], biasP.b], [s2.b], nowaw=True)
        if qb == 0:
            ts("dve", s2.ap[:, :, 0:128], s2.ap[:, :, 0:128], blk0.ap[:, 0:1], None, ALU.add, None, [s2.b, blk0.b], [s2.b])
        stb = st2[idx % 3]
        nmx = stb.ap[:, 0:4]
        fw.op("dve", lambda e, s2=s2, nmx=nmx: e.tensor_reduce(nmx, s2.ap, AX.X, ALU.max, negate=True), [s2.b], [stb.bufs[0]])
        tt("dve", nmx, nmx, nsinkP.ap[:, 4 * m:4 * m + 4], ALU.min, [stb.bufs[0], nsinkP.b], [stb.bufs[0]])

    def att_1b(qb, m, idx):
        s2, pb_, stb = S2[idx % 2], Pb[idx % 3], st2[idx % 3]
        for s in range(4):
            act(pb_.ap[:, s, :], s2.ap[:, s, :], AF.Exp, [s2.b, stb.bufs[0]], [pb_.b, stb.bufs[1]],
                bias=stb.ap[:, s:s + 1], accum=stb.ap[:, 4 + s:5 + s], nowaw=True)

    def att_1c(qb, m, idx):
        pb_, stb = Pb[idx % 3], st2[idx % 3]
        nmx, rs, es, den = (stb.ap[:, 0:4], stb.ap[:, 4:8], stb.ap[:, 8:12], stb.ap[:, 12:16])
        b0, b1, b2 = stb.bufs
        tt("pool", es, sinkP.ap[:, 4 * m:4 * m + 4], nmx, ALU.add, [sinkP.b, b0], [b2])
        act(es, es, AF.Exp, [b2], [b2])
        tt("pool", den, rs, es, ALU.add, [b1, b2], [b2])
        fw.op("dve", lambda e, den=den: e.reciprocal(den, den), [b2], [b2])
        for s_ in range(4):
            act(pb_.ap[:, s_, :], pb_.ap[:, s_, :], AF.Identity, [pb_.b, b2], [pb_.b], scale=den[:, s_:s_ + 1], nowaw=True)

    def att_stage2(qb, m, idx):
        qo = qb * 128
        pb_, ptt = Pb[idx % 3], PTs[idx % 2]
        ssA_t, ssA_b = psum[6].ap(), psb[6]
        ptb_t, ptb_b = psum[4].ap().bitcast(BF16), psb[4]
        for s in range(4):
            for kb in range(2):
                tr(ptb_t[:, (s * 2 + kb) * 128:(s * 2 + kb + 1) * 128], pb_.ap[:, s, kb * 128:(kb + 1) * 128],
                   ident_b.ap, [pb_.b, ident_b.b], [ptb_b])
        cp("dve", ptt.ap.rearrange("p a b -> p (a b)"), ptb_t, [ptb_b], [ptt.b])
        pv_t, pv_b = psum[5].ap(), psb[5]
        for s in range(4):
            c, half = 2 * m + s // 2, s % 2
            hp = slice(half * 64, (half + 1) * 64)
            for kb in range(2):
                kt_i = (9 if qb == 0 else qb - 1) if kb == 0 else qb
                mm(pv_t[hp, (s // 2) * 128:(s // 2 + 1) * 128], V.ap[:, kt_i, hp], ptt.ap[:, s * 2 + kb, :],
                   kb == 0, kb == 1, [V.bufs[kt_i], ptt.b], [pv_b])
        for j in range(2):
            c = 2 * m + j
            ts("dve", MT.ap[:, c, qo:qo + 128], pv_t[:, j * 128:(j + 1) * 128], gA.ap[:, c:c + 1], None, ALU.mult, None,
               [pv_b, gA.b], [MT.bufs[c]])
        sqa = SQA2[idx % 2]
        act(sqa.ap, pv_t[:, 0:256], AF.Square, [pv_b], [sqa.b])

        def fin():
            for j in range(2):
                mm(ssA_t[:, 0:128], ones_b.ap, sqa.ap[:, j * 128:(j + 1) * 128], m == 0 and j == 0, m == 3 and j == 1,
                   [ones_b.b, sqa.b], [ssA_b])
            if m == 3:
                ts("dve", RA.ap[:, qo:qo + 128], ssA_t[:, 0:128], 1.0 / 1024, EPS, ALU.mult, ALU.add, [ssA_b], [RA.b])
        return fin

    st2 = [A.alloc("st%d" % i, [16], F32, nbufs=3) for i in range(3)]
    mone = A.alloc("mone", [4], F32)
    fw.op("pool", lambda e: e.memset(mone.ap, -1.0), (), [mone.b])
    Pb.append(A.alloc("P_2", [4, 256], BF16))
    SQA2 = [SQA, A.alloc("SQAb", [256], BF16)]
    groups_ = [(qb, m) for qb in range(8) for m in range(4)]
    NG_ = len(groups_)
    pend = None
    for t in range(NG_ + 3):
        if t < NG_:
            att_1a(groups_[t][0], groups_[t][1], t)
        if 0 <= t - 1 < NG_:
            att_1b(groups_[t - 1][0], groups_[t - 1][1], t - 1)
        if 0 <= t - 2 < NG_:
            att_1c(groups_[t - 2][0], groups_[t - 2][1], t - 2)
        if 0 <= t - 3 < NG_:
            fin = att_stage2(groups_[t - 3][0], groups_[t - 3][1], t - 3)
            if pend is not None:
                pend()
            pend = fin
            if (t - 3) % 2 == 0:
                ada_unit(7)
    pend()
    for t_ in (mone, st2[0], st2[1], st2[2], SQA2[1], S2[0], S2[1], Pb[0], Pb[1], Pb[2], PTs[0], PTs[1], SQA):
        A.free(t_)

    if stop == 6:
        return finish()
    Vn = A.alloc("Vn", [16, 128], BF16)
    for b_ in range(16):
        ld(Vn.ap[0:4, b_, :], V.ap[b_ * 4:(b_ + 1) * 4, 8, :], [Vn.b], [V.bufs[8]], par=True)
    QS = A.alloc("QS", [16, 32], BF16)
    cp("dve", QS.ap.rearrange("p b (c t) -> p b c t", c=8), QT.ap[:, :, TP:TM].rearrange("p c (b t) -> p b c t", t=4),
       QT.bufs, [QS.b])
    SS2 = A.alloc("SS2", [4, 132], F32)
    Ps = A.alloc("Ps", [4, 132], BF16)
    PTS = A.alloc("PTS", [4, 32], BF16)
    PTN = A.alloc("PTN", [4, 32], BF16)
    os_t, os_b = psum[7].ap(), psb[7]
    SS2s = [SS2, A.alloc("SS2b", [4, 132], F32)]
    Pss = [Ps, A.alloc("Psb", [4, 132], BF16)]
    sts = [st, A.alloc("stb", [32], F32)]
    def samp_a(grp):
        SS2, Ps, st = SS2s[grp % 2], Pss[grp % 2], sts[grp % 2]
        sbanks = [(psum[(grp % 2) * 2 + i].ap(), psb[(grp % 2) * 2 + i]) for i in range(2)]
        for l in range(4):
            b_, kvh = 2 * grp + l // 2, l % 2
            bank, bb = sbanks[kvh]
            base = (l // 2) * 256
            hp = slice(kvh * 64, (kvh + 1) * 64)
            lh = QS.ap[hp, b_, :]
            mm(bank[0:32, base:base + 128], lh, KcT.ap[hp, b_, :], True, True, [QS.b, KcT.b], [bb])
            mm(bank[0:32, base + 128:base + 132], lh, KT.ap[hp, TP + b_ * 4:TP + b_ * 4 + 4], True, True,
               [QS.b, KT.b], [bb])
        for i in range(2):
            tt("dve", SS2.ap[0:32, i:4:2, :], sbanks[i][0][0:32, :].rearrange("p (a b) -> p a b", a=2)[:, :, 0:132],
               biasS.ap[0:32, i, :].unsqueeze(1).broadcast_to([32, 2, 132]), ALU.add, [sbanks[i][1], biasS.b], [SS2.b])
        nmx, rs, es, den = st.ap[0:32, 0:4], st.ap[0:32, 4:8], st.ap[0:32, 8:12], st.ap[0:32, 12:16]
        fw.op("dve", lambda e, nmx=nmx: e.tensor_reduce(nmx, SS2.ap[0:32], AX.X, ALU.max, negate=True), [SS2.b], [st.b])
        tt("dve", nmx.rearrange("p (a b) -> p a b", a=2), nmx.rearrange("p (a b) -> p a b", a=2),
           nsinkS.ap[0:32].unsqueeze(1).broadcast_to([32, 2, 2]), ALU.min, [st.b, nsinkS.b], [st.b])
        for l in range(4):
            act(Ps.ap[0:32, l, :], SS2.ap[0:32, l, :], AF.Exp, [SS2.b, st.b], [Ps.b, st.b],
                bias=st.ap[0:32, l:l + 1], accum=st.ap[0:32, 4 + l:5 + l])
        tt("dve", es.rearrange("p (a b) -> p a b", a=2), nmx.rearrange("p (a b) -> p a b", a=2),
           sinkS.ap[0:32].unsqueeze(1).broadcast_to([32, 2, 2]), ALU.add, [sinkS.b, st.b], [st.b])
        act(es, es, AF.Exp, [st.b], [st.b])
        tt("dve", den, rs, es, ALU.add, [st.b], [st.b])
        fw.op("dve", lambda e, den=den: e.reciprocal(den, den), [st.b], [st.b])
        tt("dve", Ps.ap[0:32], Ps.ap[0:32], den.unsqueeze(2).broadcast_to([32, 4, 132]), ALU.mult, [Ps.b, st.b], [Ps.b])
    def samp_b(grp):
        Ps = Pss[grp % 2]
        ptb_t, ptb_b = psum[4].ap().bitcast(BF16), psb[4]
        for l in range(4):
            tr(ptb_t[:, l * 32:(l + 1) * 32], Ps.ap[0:32, l, 0:128], ident_b.ap[0:32, 0:32], [Ps.b, ident_b.b], [ptb_b])
            tr(ptb_t[0:4, 128 + l * 32:128 + (l + 1) * 32], Ps.ap[0:32, l, 128:132], ident_b.ap[0:32, 0:32],
               [Ps.b, ident_b.b], [ptb_b])
        act(PTS.ap.rearrange("p a b -> p (a b)"), ptb_t[:, 0:128], AF.Identity, [ptb_b], [PTS.b])
        cp("dve", PTN.ap[0:4].rearrange("p a b -> p (a b)"), ptb_t[0:4, 128:256], [ptb_b], [PTN.b])
        for l in range(4):
            b_, kvh = 2 * grp + l // 2, l % 2
            hp = slice(kvh * 64, (kvh + 1) * 64)
            o_ap = os_t[hp, b_ * 32:(b_ + 1) * 32]
            mm(o_ap, Vc.ap[:, b_, hp], PTS.ap[:, l, :], True, False, [Vc.b, PTS.b], [os_b])
            mm(o_ap, Vn.ap[0:4, b_, hp], PTN.ap[0:4, l, :], False, True, [Vn.b, PTN.b], [os_b])
    samp_a(0)
    for grp in range(8):
        if grp + 1 < 8:
            samp_a(grp + 1)
        samp_b(grp)
        ada_unit(5)
    for t_ in (SS2s[1], Pss[1], sts[1]):
        A.free(t_)
    tt("dve", MT.ap[:, 0:8, TP:TM].rearrange("p c (b t) -> p b c t", t=4),
       os_t.rearrange("p (b c t) -> p b c t", b=16, c=8),
       gA.ap.unsqueeze(1).unsqueeze(3).broadcast_to([128, 16, 8, 4]), ALU.mult, [os_b, gA.b], MT.bufs[0:8])
    SQS = A.alloc("SQS", [512], BF16)
    act(SQS.ap.rearrange("p (c b t) -> p b c t", c=8, b=16), os_t.rearrange("p (b c t) -> p b c t", b=16, c=8),
        AF.Square, [os_b], [SQS.b])
    ssA_t, ssA_b = psum[6].ap(), psb[6]
    for c in range(8):
        mm(ssA_t[:, 0:64], ones_b.ap, SQS.ap[:, c * 64:(c + 1) * 64], c == 0, c == 7, [ones_b.b, SQS.b], [ssA_b])
    ts("dve", RA.ap[:, TP:TM], ssA_t[:, 0:64], 1.0 / 1024, EPS, ALU.mult, ALU.add, [ssA_b], [RA.b])
    rsqrt_inplace(RA.ap, RA.b)
    for t_ in (st, Vn, QS, SS2, Ps, PTS, PTN, SQS, KcT, Vc,
               biasP, biasS, QT, KT, V, sinkP, nsinkP, sinkS, nsinkS, blk0):
        A.free(t_)

    if stop == 7:
        return finish()
    RG = A.alloc("RG", [TM], F32)
    GO = [A.alloc("GO%d" % i, [4, 128], F32) for i in range(2)]
    SQG = [A.alloc("SQG%d" % i, [4, 128], BF16) for i in range(3)]
    it = 0
    pendgq = []
    for ti, (o, n) in enumerate(TILES):
        ssG_t, ssG_b = psum[6].ap(), psb[6]
        for hh in range(2):
            pt, pb = nextbank(allowed=(0, 1, 2, 3))
            for h4 in range(4):
                h = hh * 4 + h4
                out = pt[:, h4 * 128:h4 * 128 + n]
                if ti < 8:
                    rhs, rhi, rlo = WsT.ap[:, h, :], bs_hi.ap[0:1, h, :], bs_lo.ap[0:1, h, :]
                    rb_ = [WsT.b]
                else:
                    rhs, rhi, rlo = Wblk.ap[0:64, h, :], bsS_hi.ap[0:1, h, :], bsS_lo.ap[0:1, h, :]
                    rb_ = [Wblk.b]
                mm(out, GV.ap[0:n, ti, h * 128:(h + 1) * 128], rhs, True, False, [GV.bufs[ti]] + rb_, [pb])
                mm(out, ones_r.ap[0:1, :], rhi, False, False, [ones_r.b, bs_hi.b, bsS_hi.b], [pb])
                mm(out, ones_r.ap[0:1, :], rlo, False, True, [ones_r.b, bs_lo.b, bsS_lo.b], [pb])
            go, sqg = GO[it % 2], SQG[it % 3]
            it += 1
            hs = slice(hh * 4, hh * 4 + 4)
            tt("dve", go.ap[:, :, 0:n], pt.rearrange("p (a b) -> p a b", a=4)[:, :, 0:n], GUT.ap[:, hs, o:o + n],
               ALU.mult, [pb] + GUT.bufs[hh * 4:hh * 4 + 4], [go.b])
            tt("dve", MT.ap[:, 8 + hh * 4:12 + hh * 4, o:o + n], go.ap[:, :, 0:n],
               gG.ap[:, hs].unsqueeze(2).broadcast_to([128, 4, n]), ALU.mult, [go.b, gG.b], MT.bufs[8 + hh * 4:12 + hh * 4])
            act(sqg.ap[:, :, 0:n], go.ap[:, :, 0:n], AF.Square, [go.b], [sqg.b])
            if len(pendgq) >= 2:
                pendgq.pop(0)()

            def pendg(sqg=sqg, n=n, hh=hh, o=o):
                for h4 in range(4):
                    mm(ssG_t[:, 0:n], ones_b.ap, sqg.ap[:, h4, 0:n], hh == 0 and h4 == 0, hh == 1 and h4 == 3,
                       [ones_b.b, sqg.b], [ssG_b])
                if hh == 1:
                    ts("dve", RG.ap[:, o:o + n], ssG_t[:, 0:n], 1.0 / 1024, EPS, ALU.mult, ALU.add, [ssG_b], [RG.b])
            pendgq.append(pendg)
            if hh == 1 and ti < 8:
                ada_unit(5)
    while pendgq:
        pendgq.pop(0)()
    rsqrt_inplace(RG.ap, RG.b)
    for t_ in (cT, siluT, GO[0], GO[1], SQG[0], SQG[1], SQG[2], GUT, GV, WsT, Wblk, bs_hi, bs_lo, bsS_hi, bsS_lo, ones_r):
        A.free(t_)

    if stop == 8:
        return finish()
    OT = A.alloc("OT", [16, TM], F32, nbufs=16)
    ta = [A.alloc("ta%d" % i, [512], F32) for i in range(2)]
    tb = [A.alloc("tb%d" % i, [512], F32) for i in range(2)]
    sqo = [A.alloc("sqo%d" % i, [512], BF16) for i in range(3)]
    ssO = [(psum[5].ap(), psb[5]), (psum[6].ap(), psb[6]), (psum[7].ap(), psb[7])]
    it = 0
    wo_stream = Stream([wout_d[u] for u in range(8)])
    wo_stream.prefetch(3)
    pendq = []
    for u in range(8):
        slot = wo_stream.get()
        wo_stream.prefetch(1)
        w = slot.ap.rearrange("p (k j c) -> p k j c", k=16, j=2)
        for j in range(2):
            m = 2 * u + j
            for gi, (o, n) in enumerate(GROUPS):
                p1, b1 = nextbank(allowed=(0, 1, 2, 3, 4))
                p2, b2 = nextbank(allowed=(0, 1, 2, 3, 4))
                for kc in range(8):
                    mm(p1[:, 0:n], w[:, kc, j, :], MT.ap[:, kc, o:o + n], kc == 0, kc == 7, [slot.b, MT.bufs[kc]], [b1])
                for kc in range(8, 16):
                    mm(p2[:, 0:n], w[:, kc, j, :], MT.ap[:, kc, o:o + n], kc == 8, kc == 15, [slot.b, MT.bufs[kc]], [b2])
                a_, b__, q_ = ta[it % 2], tb[it % 2], sqo[it % 3]
                it += 1
                tt("dve", a_.ap[:, 0:n], p1[:, 0:n], RA.ap[:, o:o + n], ALU.mult, [b1, RA.b], [a_.b])
                tt("dve", b__.ap[:, 0:n], p2[:, 0:n], RG.ap[:, o:o + n], ALU.mult, [b2, RG.b], [b__.b])
                tt("pool", OT.ap[:, m, o:o + n], a_.ap[:, 0:n], b__.ap[:, 0:n], ALU.add, [a_.b, b__.b], [OT.bufs[m]])
                act(q_.ap[:, 0:n], OT.ap[:, m, o:o + n], AF.Square, [OT.bufs[m]], [q_.b])
                if len(pendq) >= 2:
                    pendq.pop(0)()

                def pend_(q_=q_, gi=gi, n=n, m=m):
                    mm(ssO[gi][0][:, 0:n], ones_b.ap, q_.ap[:, 0:n], m == 0, m == 15, [ones_b.b, q_.b], [ssO[gi][1]])
                pendq.append(pend_)
    while pendq:
        pendq.pop(0)()
    if os.environ.get("K_DBG"):
        dbg_mt = nc.dram_tensor("dbg_mt", [128, 16 * TM], BF16, kind="ExternalOutput").ap()
        dbg_ra = nc.dram_tensor("dbg_ra", [128, TM], F32, kind="ExternalOutput").ap()
        dbg_rg = nc.dram_tensor("dbg_rg", [128, TM], F32, kind="ExternalOutput").ap()
        dbg_cm = nc.dram_tensor("dbg_cm", [128, 16 * 17], F32, kind="ExternalOutput").ap()
        dbg_ot = nc.dram_tensor("dbg_ot", [128, 16 * TM], F32, kind="ExternalOutput").ap()
        db = Buf("dbg")
        out_tickets.append(ld(dbg_mt, MT.ap.rearrange("p a b -> p (a b)"), [db], MT.bufs, sembuf=db))
        out_tickets.append(ld(dbg_ra, RA.ap, [db], [RA.b], sembuf=db))
        out_tickets.append(ld(dbg_rg, RG.ap, [db], [RG.b], sembuf=db))
        out_tickets.append(ld(dbg_cm, cm.ap.rearrange("p a b -> p (a b)"), [db], [cm.b], sembuf=db))
        out_tickets.append(ld(dbg_ot, OT.ap.rearrange("p a b -> p (a b)"), [db], OT.bufs, sembuf=db))
    for t_ in (ta[0], ta[1], tb[0], tb[1], sqo[0], sqo[1], sqo[2], RA, RG):
        A.free(t_)
    A.free(MT)
    RB2 = A.alloc("RB2", [TM], F32)
    for gi, (o, n) in enumerate(GROUPS):
        ts("dve", RB2.ap[:, o:o + n], ssO[gi][0][:, 0:n], 1.0 / D, EPS, ALU.mult, ALU.add, [ssO[gi][1]], [RB2.b])
    rsqrt_inplace(RB2.ap, RB2.b)

    if stop == 9:
        return finish()
    assert ada_state["i"] == 48

    def bview(ap64):
        return ap64.rearrange("p k (b t) -> p k b t", t=4)

    def sample_cols(src, src_bufs, rb, c_t, xs_dram, xs_bufs, dstS, dstS_bufs):
        xS = A.alloc("xS", [16, 64], F32)
        tS = A.alloc("tS", [16, 64], F32)
        ld(xS.ap, xs_dram, [xS.b], xs_bufs)
        tt("dve", tS.ap, src.ap[:, :, TP:TM], rb.ap[:, TP:TM].unsqueeze(1).broadcast_to([128, 16, 64]), ALU.mult,
           list(src_bufs) + [rb.b], [tS.b])
        tt("dve", bview(tS.ap), bview(tS.ap), c_t.ap[:, :, 1:17].unsqueeze(3).broadcast_to([128, 16, 16, 4]), ALU.mult,
           [tS.b, c_t.b], [tS.b])
        tt("dve", dstS, tS.ap, xS.ap, ALU.add, [tS.b, xS.b], list(dstS_bufs))
        return xS, tS

    def residual(src, src_bufs, rb, c_t, x_src_fn, x_src_bufs_fn, dst_fn, dst_bufs_fn, after_fn):
        xc = [A.alloc("xc%d" % i, [TP], F32) for i in range(3)]
        tm2 = [A.alloc("rtm%d" % i, [TP], F32) for i in range(2)]
        for kc in range(16):
            x_, t_ = xc[kc % 3], tm2[kc % 2]
            ld(x_.ap, x_src_fn(kc), [x_.b], x_src_bufs_fn(kc))
            tt("dve", t_.ap, src.ap[:, kc, 0:TP], rb.ap[:, 0:TP], ALU.mult, [src_bufs[kc], rb.b], [t_.b])
            dst, dbufs = dst_fn(kc), dst_bufs_fn(kc)
            stt(dst[:, 0:TP], t_.ap, c_t.ap[:, kc, 0:1], x_.ap, ALU.mult, ALU.add,
                [t_.b, c_t.b, x_.b], dbufs)
            after_fn(kc)
        for t_ in xc + tm2:
            A.free(t_)

    sq2 = [A.alloc("sqx%d" % i, [TM], BF16) for i in range(2)]
    ss1 = sumsq_banks()

    def after_x1(kc):
        sq = sq2[kc % 2]
        act(sq.ap, OT.ap[:, kc, :], AF.Square, [OT.bufs[kc]], [sq.b])
        for gi, (o, n) in enumerate(GROUPS):
            mm(ss1[gi][0][:, 0:n], ones_b.ap, sq.ap[:, o:o + n], kc == 0, kc == 15, [ones_b.b, sq.b], [ss1[gi][1]])
        ld(x1_dr[kc * 128:(kc + 1) * 128, :], OT.ap[:, kc, :], [x1_dr_bufs[kc]], [OT.bufs[kc]], eng="act",
           sembuf=x1_dr_bufs[kc])

    xS_, tS_ = sample_cols(OT, OT.bufs, RB2, cm, xT[:, TP:TM].rearrange("(k p) t -> p k t", p=128), (),
                           OT.ap[:, :, TP:TM], OT.bufs)
    residual(OT, OT.bufs, RB2, cm, lambda kc: xT[kc * 128:(kc + 1) * 128, 0:TP], lambda kc: (),
             lambda kc: OT.ap[:, kc, :], lambda kc: [OT.bufs[kc]], after_x1)
    A.free(xS_)
    A.free(tS_)
    RB3 = A.alloc("RB3", [TM], F32)
    for gi, (o, n) in enumerate(GROUPS):
        ts("dve", RB3.ap[:, o:o + n], ss1[gi][0][:, 0:n], 1.0 / D, EPS, ALU.mult, ALU.add, [ss1[gi][1]], [RB3.b])
    rsqrt_inplace(RB3.ap, RB3.b)
    H2T = A.alloc("H2T", [16, TM], BF16, nbufs=16)
    tmp2 = [A.alloc("tmpB%d" % i, [TT], F32) for i in range(2)]
    tmps = A.alloc("tmpsB", [64], F32)
    tS2 = A.alloc("tS2", [16, 64], F32)
    tt("dve", tS2.ap, OT.ap[:, :, TP:TM], RB3.ap[:, TP:TM].unsqueeze(1).broadcast_to([128, 16, 64]), ALU.mult,
       OT.bufs + [RB3.b], [tS2.b])
    tt("dve", bview(tS2.ap), bview(tS2.ap), a2.ap[:, :, 1:17].unsqueeze(3).broadcast_to([128, 16, 16, 4]), ALU.mult,
       [tS2.b, a2.b], [tS2.b])
    tt("dve", bview(H2T.ap[:, :, TP:TM]), bview(tS2.ap), seg_ap(3)[:, :, 1:17].unsqueeze(3).broadcast_to([128, 16, 16, 4]),
       ALU.add, [tS2.b] + modT.bufs[24:32], H2T.bufs)
    A.free(tS2)
    for kc in range(16):
        modulate_kc(H2T, H2T.bufs[kc], OT.ap[:, kc, :], OT.bufs[kc], RB3, a2, a2.b, 3, kc, TP, False, kc, sample=False)
    for t_ in (tmp2[0], tmp2[1], tmps, RB2, RB3, sq2[0], sq2[1]):
        A.free(t_)
    A.free(OT)

    if stop == 10:
        return finish()
    for t_ in ring_extra:
        ring.remove(t_)
        A.free(t_)
    ACC = A.alloc("ACC", [16, TM], F32, nbufs=16)
    HID = A.alloc("HID", [16, TM], BF16, nbufs=16)
    W2G = A.alloc("W2G", [8, 2048], BF16, nbufs=4)
    rl = [A.alloc("rl%d" % i, [512], F32) for i in range(2)]
    state = {"it": 0}

    w1_stream = Stream([wff1_d[u] for u in range(32)])

    def ffn1(g):
        for uu in range(4):
            slot = w1_stream.get()
            w = slot.ap.rearrange("p (k j c) -> p k j c", k=16, j=2)
            for j in range(2):
                hs = (g * 8 + uu * 2 + j) % 16
                for (o, n) in GROUPS:
                    pt, pb = nextbank()
                    for kc in range(16):
                        mm(pt[:, 0:n], w[:, kc, j, :], H2T.ap[:, kc, o:o + n], kc == 0, kc == 15,
                           [slot.b, H2T.bufs[kc]], [pb])
                    r_ = rl[state["it"] % 2]
                    state["it"] += 1
                    act(r_.ap[:, 0:n], pt[:, 0:n], AF.Relu, [pb], [r_.b])
                    tt("dve", HID.ap[:, hs, o:o + n], r_.ap[:, 0:n], r_.ap[:, 0:n], ALU.mult, [r_.b], [HID.bufs[hs]])

    def load_w2(g):
        for uu in range(4):
            dst = w2cur["t"].ap[:, 2 * uu:2 * uu + 2, :].rearrange("p a b -> p (a b)")
            fw.dma("pool", lambda e, uu=uu, g=g, dst=dst: e.dma_start(
                out=dst, in_=wff2_d[g * 4 + uu], max_dma_last_dim=4096), (), [w2cur["t"].bufs[uu]])

    def ffn2(g):
        for m in range(16):
            for (o, n) in GROUPS:
                pt, pb = nextbank()
                for k8 in range(8):
                    hs = (g * 8 + k8) % 16
                    mm(pt[:, 0:n], w2cur["t"].ap[:, k8, m * 128:(m + 1) * 128], HID.ap[:, hs, o:o + n], k8 == 0, k8 == 7,
                       [w2cur["t"].bufs[k8 // 2], HID.bufs[hs]], [pb])
                if g == 0:
                    act(ACC.ap[:, m, o:o + n], pt[:, 0:n], AF.Identity, [pb], [ACC.bufs[m]])
                else:
                    tt("dve", ACC.ap[:, m, o:o + n], pt[:, 0:n], ACC.ap[:, m, o:o + n], ALU.add, [pb, ACC.bufs[m]], [ACC.bufs[m]])

    w1_stream.prefetch(2)
    ffn1(0)
    w2cur = {"t": W2G}
    for g in range(8):
        if g < 7:
            load_w2(g)
        if g + 1 < 8:
            ffn1(g + 1)
            w1_stream.prefetch(2)
            if g + 1 == 7:
                A.free(H2T)
                W2G7 = A.alloc("W2G7", [8, 2048], BF16, nbufs=4)
        if g == 7:
            w2cur["t"] = W2G7
            load_w2(7)
        ffn2(g)
        if g == 6:
            pass
    for t_ in (HID, W2G, W2G7, rl[0], rl[1]):
        A.free(t_)

    if stop == 11:
        return finish()
    sq2 = [A.alloc("sqf%d" % i, [TM], BF16) for i in range(2)]
    ssF = sumsq_banks()
    for m in range(16):
        sq = sq2[m % 2]
        act(sq.ap, ACC.ap[:, m, :], AF.Square, [ACC.bufs[m]], [sq.b])
        for gi, (o, n) in enumerate(GROUPS):
            mm(ssF[gi][0][:, 0:n], ones_b.ap, sq.ap[:, o:o + n], m == 0, m == 15, [ones_b.b, sq.b], [ssF[gi][1]])
    RB4 = A.alloc("RB4", [TM], F32)
    for gi, (o, n) in enumerate(GROUPS):
        ts("dve", RB4.ap[:, o:o + n], ssF[gi][0][:, 0:n], 1.0 / D, EPS, ALU.mult, ALU.add, [ssF[gi][1]], [RB4.b])
    rsqrt_inplace(RB4.ap, RB4.b)
    yo = [A.alloc("yo%d" % i, [TM], F32) for i in range(3)]
    y_b = Buf("yT")

    YS = A.alloc("YS", [16, 64], F32)
    xS_, tS_ = sample_cols(ACC, ACC.bufs, RB4, cf, x1_dr[:, TP:TM].rearrange("(k p) t -> p k t", p=128), x1_dr_bufs,
                           YS.ap, [YS.b])
    out_tickets.append(ld(yT[:, TP:TM].rearrange("(k p) t -> p k t", p=128), YS.ap, [y_b], [YS.b], sembuf=YS.b))

    def after_y(kc):
        out_tickets.append(ld(yT[kc * 128:(kc + 1) * 128, 0:TP], yo[kc % 3].ap[:, 0:TP], [y_b], [yo[kc % 3].b], eng="act",
                              sembuf=yo[kc % 3].b))

    residual(ACC, ACC.bufs, RB4, cf, lambda kc: x1_dr[kc * 128:(kc + 1) * 128, 0:TP], lambda kc: [x1_dr_bufs[kc]],
             lambda kc: yo[kc % 3].ap, lambda kc: [yo[kc % 3].b], after_y)

    return finish()


def _t5_bucket_np(n):
    n = np.maximum(n, 0)
    nf = np.maximum(n, 1).astype(np.float32)
    large = 16 + (np.log(nf / np.float32(16)) / np.float32(math.log(8.0)) * np.float32(16)).astype(np.int32)
    large = np.minimum(large, 31)
    return np.where(n < 16, n, large)


def _units_kjc(w, nu):
    return np.ascontiguousarray(w.reshape(16, 128, nu, 2, 128).transpose(2, 1, 0, 3, 4).reshape(nu, 128, UNITW))


_PROGRAM = {}


def kernel(x_prompt, x_sample, cache_k, cache_v, c_prompt, c_sample, rel_bias_table, w_ada, b_ada,
           g_pre_mix, w_in, attn_sinks, gmlp_v_gain, gmlp_w_s, gmlp_b_s, g_attn_out, g_gmlp_out,
           w_out, g_post_mix, g_pre_ff, w_ff1, w_ff2, g_post_ff):
    f32 = np.float32
    x_prompt = np.asarray(x_prompt, f32)
    x_sample = np.asarray(x_sample, f32)
    cache_k = np.asarray(cache_k, f32)
    cache_v = np.asarray(cache_v, f32)
    c_prompt = np.asarray(c_prompt, f32)
    c_sample = np.asarray(c_sample, f32)
    w_in0 = np.asarray(w_in, f32)[0]

    wada_u = _units_kjc(np.asarray(w_ada, f32)[0], 48)
    qcols = []
    for c in range(8):
        qcols += list(range(c * 64, (c + 1) * 64)) + list(range((8 + c) * 64, (9 + c) * 64))
    fm_cols = qcols + list(range(1024, 1152)) + list(range(1280, 2304))
    wfm = np.zeros((2048, 18 * 128), f32)
    wfm[:, :17 * 128] = w_in0[:, fm_cols]
    wfm_u = _units_kjc(wfm, 9)
    wtm = np.zeros((2048, 5 * 256), f32)
    wtm[:, 0:1024] = w_in0[:, 2304:3328]
    wtm[:, 1024:1152] = w_in0[:, 1152:1280]
    wtm_u = np.ascontiguousarray(wtm.reshape(16, 128, 5, 256).transpose(2, 1, 0, 3).reshape(5, 128, UNITW))
    perm = []
    for kc in range(16):
        for p in range(128):
            if kc < 8:
                perm.append(kc * 64 + p if p < 64 else (8 + kc) * 64 + (p - 64))
            else:
                perm.append(1024 + (kc - 8) * 128 + p)
    perm = np.array(perm)
    wout_u = _units_kjc(np.asarray(w_out, f32)[0][perm, :], 8)
    wff1_u = _units_kjc(np.asarray(w_ff1, f32)[0], 32)
    wff2_u = np.ascontiguousarray(np.asarray(w_ff2, f32)[0].reshape(32, 2, 128, 2048).transpose(0, 2, 1, 3).reshape(32, 128, UNITW))

    def fm16(g):
        return np.asarray(g, f32).reshape(16, 128).T

    gT = np.ascontiguousarray(np.concatenate([fm16(g_pre_mix[0]), fm16(g_post_mix[0]), fm16(g_pre_ff[0]), fm16(g_post_ff[0])], axis=1))
    ga = np.asarray(g_attn_out, f32)[0]
    gA = np.zeros((128, 8), f32)
    for c in range(8):
        gA[0:64, c] = ga[c * 64:(c + 1) * 64]
        gA[64:128, c] = ga[(8 + c) * 64:(9 + c) * 64]
    gG = np.ascontiguousarray(np.asarray(g_gmlp_out, f32)[0].reshape(8, 128).T)
    badaT = np.ascontiguousarray(np.asarray(b_ada, f32)[0].reshape(96, 128).T)
    order = [c + 8 * half for c in range(8) for half in range(2)]
    tblP = np.ascontiguousarray(np.asarray(rel_bias_table, f32)[:, order])
    sinksP = np.ascontiguousarray(np.asarray(attn_sinks, f32)[0][order].reshape(1, 16))
    vgain = np.ascontiguousarray(np.asarray(gmlp_v_gain, f32)[0].reshape(1, 1024))
    bs = np.ascontiguousarray(np.asarray(gmlp_b_s, f32)[0].reshape(1, 1024))
    wsT = np.ascontiguousarray(np.asarray(gmlp_w_s, f32)[0].transpose(2, 0, 1))
    ident = np.eye(128, dtype=f32)
    tri = (np.arange(128)[:, None] <= np.arange(128)[None, :]).astype(f32)
    bi = np.arange(64)
    blkmask = ((bi[:, None] // 4 == bi[None, :] // 4) & (bi[:, None] % 4 <= bi[None, :] % 4)).astype(f32)
    oh = (_t5_bucket_np(np.arange(128))[None, :] == np.arange(32)[:, None]).astype(f32)

    shared = dict(ident=ident, tri=tri, blkmask=blkmask, oh=oh, tblP=tblP, sinksP=sinksP, gT=gT, gA=gA, gG=gG,
                  badaT=badaT, vgain=vgain, bs=bs, wsT=wsT, wada_u=wada_u, wfm_u=wfm_u, wtm_u=wtm_u,
                  wout_u=wout_u, wff1_u=wff1_u, wff2_u=wff2_u)

    in_maps = []
    for core in range(NCORES):
        bp, half = core // 2, core % 2
        xp = x_prompt[bp, half * TP:(half + 1) * TP]
        xs = x_sample[16 * core:16 * core + 16].reshape(64, D)
        halo = x_prompt[bp, TP - TH:TP] if half == 1 else np.zeros((TH, D), f32)
        xTc = np.ascontiguousarray(np.concatenate([xp, xs, halo], axis=0).T)
        cc = np.concatenate([c_prompt[bp:bp + 1], c_sample[16 * core:16 * core + 16]], axis=0)
        cTc = np.ascontiguousarray(cc.T.reshape(16, 128, 17).transpose(1, 0, 2).reshape(128, 16 * 17))
        ck = cache_k[0, 16 * core:16 * core + 16]
        ckT = np.ascontiguousarray(ck.transpose(2, 3, 0, 1).reshape(128, 16 * 128))
        cv2 = np.ascontiguousarray(cache_v[0, 16 * core:16 * core + 16].transpose(1, 0, 2, 3).reshape(128, 16 * 128))
        blk0 = np.full((128, 1), NEG if half == 0 else 0.0, f32)
        m = dict(shared)
        m.update(xT=xTc, cT=cTc, ckT=ckT, cv2=cv2, blk0=blk0)
        in_maps.append(m)

    if "nc" not in _PROGRAM:
        _PROGRAM["nc"] = build_program()
    res = run_bass_kernel_spmd(_PROGRAM["nc"], in_maps, core_ids=list(range(NCORES)))
    outs = res.results

    y_p = np.zeros((4, 2048, D), f32)
    y_s = np.zeros((128, 4, D), f32)
    nkp = np.zeros((1, 4, 128, 2, 64), f32)
    nvp = np.zeros((1, 4, 128, 2, 64), f32)
    nks = np.zeros((1, 128, 4, 2, 64), f32)
    nvs = np.zeros((1, 128, 4, 2, 64), f32)
    gvs = np.zeros((1, 128, 4, 8, 128), f32)
    for core in range(NCORES):
        bp, half = core // 2, core % 2
        o = outs[core]
        yTc = np.asarray(o["yT"])
        y_p[bp, half * TP:(half + 1) * TP] = yTc[:, 0:TP].T
        y_s[16 * core:16 * core + 16] = yTc[:, TP:TM].T.reshape(16, 4, D)
        kTo = np.asarray(o["kT_out"])
        vo = np.asarray(o["v_out"])
        if half == 1:
            nkp[0, bp] = kTo[:, 0:128].T.reshape(128, 2, 64)
            nvp[0, bp] = vo[0:128].reshape(128, 2, 64)
        nks[0, 16 * core:16 * core + 16] = kTo[:, 128:192].T.reshape(16, 4, 2, 64)
        nvs[0, 16 * core:16 * core + 16] = vo[128:192].reshape(16, 4, 2, 64)
        gvs[0, 16 * core:16 * core + 16] = np.asarray(o["gv_out"]).reshape(16, 4, 8, 128)
    return (y_p, y_s, nkp, nvp, nks, nvs, gvs)
```

```python
import bisect
import math
import numpy as np
import concourse.bass as bass
import concourse.mybir as mybir
from concourse.bass_utils import run_bass_kernel_spmd

F32 = mybir.dt.float32
BF16 = mybir.dt.bfloat16
AF = mybir.ActivationFunctionType
ALU = mybir.AluOpType
AX = mybir.AxisListType

NCORES = 8
D = 2048
TP, TS, TH = 1024, 64, 128
TM = TP + TS
TT = TM + TH
GROUPS = [(0, 512), (512, 512), (1024, 64)]
EPS = 1e-6
NEG = -30000.0
UNITW = 4096


class Buf:
    __slots__ = ("name", "last_write", "readers", "dsem", "dcount", "excl")

    def __init__(self, name, excl=False):
        self.name = name
        self.excl = excl
        self.last_write = None
        self.readers = []
        self.dsem = None
        self.dcount = 0


class _Op:
    __slots__ = ("fn", "waits", "inc", "is_dma", "dma_sem")

    def __init__(self, fn, is_dma=False, dma_sem=None):
        self.fn = fn
        self.waits = []
        self.inc = None
        self.is_dma = is_dma
        self.dma_sem = dma_sem


class _Eng:
    def __init__(self, name, sem):
        self.name = name
        self.sem = sem
        self.count = 0
        self.ops = []
        self.inc_seqs = []
        self.inc_vals = []
        self.waited = {}
        self.last_compute = -1


class FW:
    SAME_ENGINE_SYNC = True

    def __init__(self, nc):
        self.nc = nc
        self.engs = {}
        for n in ("pe", "act", "dve", "pool", "sp"):
            self.engs[n] = _Eng(n, nc.alloc_semaphore("s_" + n))
        self.nsem = 5

    def _resolve(self, t):
        if t[0] == "d":
            return t[1], t[2]
        E = self.engs[t[1]]
        seq = t[2]
        i = bisect.bisect_left(E.inc_seqs, seq)
        if i < len(E.inc_seqs):
            return E.sem, E.inc_vals[i]
        s = E.last_compute
        assert s >= seq
        E.count += 1
        E.ops[s].inc = E.count
        E.inc_seqs.append(s)
        E.inc_vals.append(E.count)
        return E.sem, E.count

    def _need(self, E, op, t, kind):
        if t is None:
            return
        if t[0] == "e" and t[1] == E.name:
            if E.name == "pe":
                return
            if kind == "rar" or not self.SAME_ENGINE_SYNC:
                return
            if kind == "waw" and getattr(self, "nowaw", False):
                return
        sem, val = self._resolve(t)
        key = id(sem)
        if E.waited.get(key, 0) >= val:
            return
        E.waited[key] = val
        op.waits.append((sem, val))

    def _deps(self, E, op, reads, writes):
        for b in reads:
            self._need(E, op, b.last_write, "raw")
            if b.excl:
                for r in b.readers:
                    self._need(E, op, r, "rar")
        for b in writes:
            self._need(E, op, b.last_write, "waw")
            for r in b.readers:
                self._need(E, op, r, "war")

    def _commit(self, t, reads, writes):
        for b in writes:
            b.last_write = t
            b.readers = []
        for b in reads:
            if t[0] == "e":
                b.readers = [r for r in b.readers if not (r[0] == "e" and r[1] == t[1])]
            else:
                b.readers = [r for r in b.readers if not (r[0] == "d" and r[1] is t[1])]
            b.readers.append(t)

    def op(self, eng, fn, reads=(), writes=(), nowaw=False):
        E = self.engs[eng]
        o = _Op(fn)
        self.nowaw = nowaw
        self._deps(E, o, reads, writes)
        self.nowaw = False
        E.ops.append(o)
        seq = len(E.ops) - 1
        E.last_compute = seq
        self._commit(("e", eng, seq), reads, writes)

    def dma(self, eng, fn, reads=(), writes=(), sembuf=None, par=False):
        E = self.engs[eng]
        sb = sembuf or (writes[0] if writes else reads[0])
        if sb.dsem is None:
            sb.dsem = self.nc.alloc_semaphore("d%d" % self.nsem)
            self.nsem += 1
        o = _Op(fn, True, sb.dsem)
        saved = []
        if par:
            for b in writes:
                lw = b.last_write
                if lw is not None and lw[0] == "d" and lw[1] is sb.dsem:
                    saved.append((b, lw))
                    b.last_write = None
        self._deps(E, o, reads, writes)
        for b, lw in saved:
            b.last_write = lw
        E.ops.append(o)
        sb.dcount += 16
        t = ("d", sb.dsem, sb.dcount)
        self._commit(t, reads, writes)
        return t

    def final_wait(self, eng, tickets):
        E = self.engs[eng]
        o = _Op(None)
        for t in tickets:
            self._need(E, o, t, "raw")
        E.ops.append(o)

    def emit(self):
        nc = self.nc
        hw = {"pe": "tensor", "act": "scalar", "dve": "vector", "pool": "gpsimd", "sp": "sync"}
        with nc.Block() as block:
            for n, E in self.engs.items():
                if not E.ops:
                    continue

                def body(e, E=E):
                    for o in E.ops:
                        for sem, val in o.waits:
                            e.wait_ge(sem, val)
                        if o.fn is None:
                            continue
                        ins = o.fn(e)
                        if o.is_dma:
                            ins.then_inc(o.dma_sem, 16)
                        elif o.inc is not None:
                            ins.then_inc(E.sem, 1)

                getattr(block, hw[n])(body)


class T:
    def __init__(self, name, ap, s, e, nbufs):
        self.name = name
        self.ap = ap
        self.s = s
        self.e = e
        self.bufs = [Buf("%s.%d" % (name, i)) for i in range(nbufs)]

    @property
    def b(self):
        return self.bufs[0]


class Arena:
    def __init__(self, big_ap, nwords):
        self.big = big_ap
        self.free_list = [(0, nwords)]
        self.retired = []
        self.peak = 0
        self.live = {}

    def alloc(self, name, shape, dtype, nbufs=1):
        n = 1
        for d_ in shape:
            n *= d_
        esz = 2 if dtype == BF16 else 4
        nw = (n * esz + 3) // 4
        nw = (nw + 7) // 8 * 8
        top = nw < 3000
        order = range(len(self.free_list) - 1, -1, -1) if top else range(len(self.free_list))
        for i in order:
            s, e = self.free_list[i]
            if e - s >= nw:
                break
        else:
            raise RuntimeError("arena out of SBUF for %s (%d words); free=%s live=%s" % (
                name, nw, self.free_list, sorted((v, k) for k, v in self.live.items())))
        if e - s == nw:
            self.free_list.pop(i)
        elif top:
            self.free_list[i] = (s, e - nw)
            s = e - nw
        else:
            self.free_list[i] = (s + nw, e)
        e = s + nw
        self.peak = max(self.peak, e)
        ap = self.big[:, s:e]
        if dtype == BF16:
            ap = ap.bitcast(BF16)
        ap = ap[:, 0:n]
        if len(shape) == 2:
            ap = ap.rearrange("p (a b) -> p a b", a=shape[0])
        elif len(shape) == 3:
            ap = ap.rearrange("p (a b c) -> p a b c", a=shape[0], b=shape[1])
        t = T(name, ap, s, e, nbufs)
        self.live[name] = (s, e)
        tick = []
        keep = []
        for (rs, re, tk) in self.retired:
            if rs < e and s < re:
                tick.extend(tk)
                if not (s <= rs and re <= e):
                    keep.append((rs, re, tk))
            else:
                keep.append((rs, re, tk))
        self.retired = keep
        for b in t.bufs:
            b.readers = list(tick)
        return t

    def free(self, t):
        self.live.pop(t.name, None)
        tk = []
        for b in t.bufs:
            if b.last_write is not None:
                tk.append(b.last_write)
            tk.extend(b.readers)
        seen = set()
        tk2 = []
        for x in tk:
            k = (x[0], id(x[1]) if x[0] == "d" else x[1], x[2])
            if k not in seen:
                seen.add(k)
                tk2.append(x)
        self.retired.append((t.s, t.e, tk2))
        fl = self.free_list + [(t.s, t.e)]
        fl.sort()
        merged = []
        for s, e in fl:
            if merged and merged[-1][1] == s:
                merged[-1] = (merged[-1][0], e)
            else:
                merged.append((s, e))
        self.free_list = merged


def build_program(stop=None):
    import os
    stop = int(os.environ.get("K_STOP", "99")) if stop is None else stop
    nc = bass.Bass("TRN2", target_bir_lowering=False)
    fw = FW(nc)

    def finish():
        fw.final_wait("sp", out_tickets)
        fw.emit()
        return nc

    def din(name, shape):
        return nc.dram_tensor(name, list(shape), F32, kind="ExternalInput").ap()

    def dout(name, shape):
        return nc.dram_tensor(name, list(shape), F32, kind="ExternalOutput").ap()

    xT = din("xT", [D, TT])
    cT_d = din("cT", [128, 16 * 17])
    ident_d = din("ident", [128, 128])
    tri_d = din("tri", [128, 128])
    blkmask_d = din("blkmask", [64, 64])
    oh_d = din("oh", [32, 128])
    blk0_d = din("blk0", [128, 1])
    tbl_d = din("tblP", [32, 16])
    sink_d = din("sinksP", [1, 16])
    gT_d = din("gT", [128, 64])
    gA_d = din("gA", [128, 8])
    gG_d = din("gG", [128, 8])
    badaT_d = din("badaT", [128, 96])
    vgain_d = din("vgain", [1, 1024])
    bs_d = din("bs", [1, 1024])
    wsT_d = din("wsT", [128, 8, 128])
    ckT_d = din("ckT", [128, 16 * 128])
    cv_d = din("cv2", [128, 16 * 128])
    wada_d = din("wada_u", [48, 128, UNITW])
    wfm_d = din("wfm_u", [9, 128, UNITW])
    wtm_d = din("wtm_u", [5, 128, UNITW])
    wout_d = din("wout_u", [8, 128, UNITW])
    wff1_d = din("wff1_u", [32, 128, UNITW])
    wff2_d = din("wff2_u", [32, 128, UNITW])

    yT = dout("yT", [D, TM])
    kT_out = dout("kT_out", [128, 192])
    v_out = dout("v_out", [192, 128])
    gv_out = dout("gv_out", [64, 1024])

    a_dr_t = nc.dram_tensor("a_scr", [16, 383], F32)
    a_dr = a_dr_t.ap()
    x1_dr = nc.dram_tensor("x1_scr", [D, TM], F32).ap()
    a_dr_buf = Buf("a_dr")
    x1_dr_bufs = [Buf("x1dr%d" % i) for i in range(16)]
    out_tickets = []

    NW = 52800
    big = nc.alloc_sbuf_tensor("big", [128, NW], F32)
    A = Arena(big.ap(), NW)
    psum = [nc.alloc_psum_tensor("ps%d" % i, [128, 512], F32) for i in range(8)]
    psb = [Buf("psb%d" % i, excl=True) for i in range(8)]
    rr = {"bank": 0, "slot": 0}

    def nextbank(allowed=None):
        allowed = allowed or range(8)
        while True:
            i = rr["bank"] % 8
            rr["bank"] += 1
            if i in allowed:
                return psum[i].ap(), psb[i]

    def mm(out, lhsT, rhs, start, stop, reads, writes, **kw):
        fw.op("pe", lambda e: e.matmul(out, lhsT, rhs, start=start, stop=stop, **kw), reads, writes)

    def tr(out, in_, ident, reads, writes):
        fw.op("pe", lambda e: e.transpose(out, in_, ident), reads, writes)

    def act(out, in_, func, reads, writes, bias=None, scale=None, accum=None, nowaw=False):
        kw = {}
        if bias is not None:
            kw["bias"] = bias
        if scale is not None:
            kw["scale"] = scale
        if accum is not None:
            kw["accum_out"] = accum
        fw.op("act", lambda e: e.activation(out, in_, func, **kw), reads, writes, nowaw=nowaw)

    def tt(eng, out, in0, in1, op, reads, writes, nowaw=False):
        fw.op(eng, lambda e: e.tensor_tensor(out, in0, in1, op), reads, writes, nowaw=nowaw)

    def ts(eng, out, in0, s1, s2, op0, op1, reads, writes):
        if op1 is None:
            fw.op(eng, lambda e: e.tensor_scalar(out, in0, s1, None, op0), reads, writes)
        else:
            fw.op(eng, lambda e: e.tensor_scalar(out, in0, s1, s2, op0, op1), reads, writes)

    def stt(out, in0, scalar, in1, op0, op1, reads, writes):
        fw.op("dve", lambda e: e.scalar_tensor_tensor(out, in0, scalar, in1, op0, op1), reads, writes)

    def cp(eng, out, in_, reads, writes):
        fw.op(eng, lambda e: e.tensor_copy(out, in_), reads, writes)

    def ld(out, in_, writes, reads=(), eng="sp", sembuf=None, par=False, **kw):
        return fw.dma(eng, lambda e: e.dma_start(out=out, in_=in_, **kw), reads, writes, sembuf=sembuf, par=par)

    def rsqrt_inplace(ap, buf, reads_extra=()):
        act(ap, ap, AF.Ln, [buf], [buf])
        act(ap, ap, AF.Exp, [buf], [buf], scale=-0.5)

    ring = [A.alloc("ring%d" % i, [UNITW], BF16) for i in range(2)]
    ring_extra = []

    def load_unit(dram_unit_ap):
        i = rr["slot"] % len(ring)
        rr["slot"] += 1
        t = ring[i]
        fw.dma("pool", lambda e: e.dma_start(out=t.ap, in_=dram_unit_ap, max_dma_last_dim=4096), (), [t.b])
        return t

    class Stream:
        def __init__(self, units):
            self.units = list(units)
            self.next = 0
            self.ready = []
            self.limit = len(self.units)

        def prefetch(self, n=1):
            while n > 0 and self.next < min(self.limit, len(self.units)):
                self.ready.append(load_unit(self.units[self.next]))
                self.next += 1
                n -= 1

        def get(self):
            if not self.ready:
                self.prefetch(1)
            return self.ready.pop(0)

    ident_f = A.alloc("ident_f", [128], F32)
    ident_b = A.alloc("ident_b", [128], BF16)
    ones_b = A.alloc("ones_b", [128], BF16)
    modT = A.alloc("modT", [96, 17], F32, nbufs=48)
    gT = A.alloc("gT", [64], F32)
    gA = A.alloc("gA", [8], F32)
    gG = A.alloc("gG", [8], F32)
    badaT = A.alloc("badaT", [96], F32)
    aM = A.alloc("aM", [16, 17], F32, nbufs=8)
    cm = A.alloc("cm", [16, 17], F32)
    a2 = A.alloc("a2", [16, 17], F32)
    cf = A.alloc("cf", [16, 17], F32)
    ld(ident_f.ap, ident_d, [ident_f.b])
    ld(gT.ap, gT_d, [gT.b])
    ld(gA.ap, gA_d, [gA.b])
    ld(gG.ap, gG_d, [gG.b])
    ld(badaT.ap, badaT_d, [badaT.b])
    cp("dve", ident_b.ap, ident_f.ap, [ident_f.b], [ident_b.b])
    fw.op("dve", lambda e: e.memset(ones_b.ap, 1.0), (), [ones_b.b])

    XT = A.alloc("XT", [16, TT], F32, nbufs=16)
    for kc in range(16):
        ld(XT.ap[:, kc, :], xT[kc * 128:(kc + 1) * 128, :], [XT.bufs[kc]])

    cT = A.alloc("cT", [16, 17], F32)
    siluT = A.alloc("siluT", [16, 17], BF16)
    ring_extra.extend(A.alloc("ringx%d" % i, [UNITW], BF16) for i in range(2))
    ring.extend(ring_extra)
    ld(cT.ap, cT_d.rearrange("p (a b) -> p a b", a=16), [cT.b])
    act(siluT.ap, cT.ap, AF.Silu, [cT.b], [siluT.b])

    ada_order = [u for i in range(8) for u in (i, 8 + i)] + list(range(16, 48))
    ada_stream = Stream([wada_d[u] for u in ada_order])
    ada_state = {"i": 0}

    def ada_unit(bank):
        u = ada_order[ada_state["i"]]
        ada_state["i"] += 1
        pt, pb = psum[bank].ap(), psb[bank]
        slot = ada_stream.get()
        w = slot.ap.rearrange("p (k j c) -> p k j c", k=16, j=2)
        for j in range(2):
            for kc in range(16):
                mm(pt[:, j * 17:(j + 1) * 17], w[:, kc, j, :], siluT.ap[:, kc, :],
                   kc == 0, kc == 15, [slot.b, siluT.b], [pb])
        ada_stream.prefetch(1)
        tt("dve", modT.ap[:, 2 * u:2 * u + 2, :], pt[:, 0:34].rearrange("p (a b) -> p a b", a=2),
           badaT.ap[:, 2 * u:2 * u + 2].unsqueeze(2).broadcast_to([128, 2, 17]), ALU.add,
           [pb, badaT.b], [modT.bufs[u]])
        if u == 23:
            tt("dve", cm.ap, seg_ap(2), gbc(1), ALU.mult, modT.bufs[16:24] + [gT.b], [cm.b])
        if u == 39:
            stt(a2.ap, seg_ap(4), 1.0, gbc(2), ALU.add, ALU.mult, modT.bufs[32:40] + [gT.b], [a2.b])
        if u == 47:
            tt("dve", cf.ap, seg_ap(5), gbc(3), ALU.mult, modT.bufs[40:48] + [gT.b], [cf.b])

    def seg_ap(seg):
        return modT.ap[:, seg * 16:(seg + 1) * 16, :]

    def gbc(i):
        return gT.ap[:, i * 16:(i + 1) * 16].unsqueeze(2).broadcast_to([128, 16, 17])

    def modulate_kc(dst, dst_buf, src, src_buf, rb, a_t, a_buf, sh_seg, kc, ncols_main, halo, idx):
        tm = tmp2[idx % 2]
        w = ncols_main + (TH if halo else 0)
        shb = modT.bufs[sh_seg * 8 + kc // 2]
        tt("dve", tm.ap[:, 0:w], src[:, 0:w], rb.ap[:, 0:w], ALU.mult, [src_buf, rb.b], [tm.b])
        shp = modT.ap[:, sh_seg * 16 + kc, :]
        rd = [tm.b, a_buf, shb]
        act(dst.ap[:, kc, 0:TP], tm.ap[:, 0:TP], AF.Identity, rd, [dst_buf],
            bias=shp[:, 0:1], scale=a_t.ap[:, kc, 0:1])
        if halo:
            act(dst.ap[:, kc, TM:TT], tm.ap[:, TM:TT], AF.Identity, rd, [dst_buf],
                bias=shp[:, 0:1], scale=a_t.ap[:, kc, 0:1])
        tt("dve", tmps.ap.rearrange("p (b t) -> p b t", t=4), tm.ap[:, TP:TM].rearrange("p (b t) -> p b t", t=4),
           a_t.ap[:, kc, 1:17].unsqueeze(2).broadcast_to([128, 16, 4]), ALU.mult, [tm.b, a_buf], [tmps.b])
        tt("dve", dst.ap[:, kc, TP:TM].rearrange("p (b t) -> p b t", t=4), tmps.ap.rearrange("p (b t) -> p b t", t=4),
           shp[:, 1:17].unsqueeze(2).broadcast_to([128, 16, 4]), ALU.add, [tmps.b, shb], [dst_buf])

    HT = A.alloc("HT", [16, TT], BF16, nbufs=16)
    tmp2 = [A.alloc("tmp%d" % i, [TT], F32) for i in range(2)]
    tmps = A.alloc("tmps", [64], F32)
    ring_start = [A.alloc("ringy%d" % i, [UNITW], BF16) for i in range(2)]
    ring.extend(ring_start)
    ada_stream.limit = 16
    ada_stream.prefetch(5)
    def ada_pair_mod(i_):
        stt(aM.ap[:, 2 * i_:2 * i_ + 2, :], modT.ap[:, 16 + 2 * i_:18 + 2 * i_, :], 1.0,
            gT.ap[:, 2 * i_:2 * i_ + 2].unsqueeze(2).broadcast_to([128, 2, 17]), ALU.add, ALU.mult,
            [modT.bufs[8 + i_], gT.b], [aM.bufs[i_]])
        for kc in (2 * i_, 2 * i_ + 1):
            modulate_kc(HT, HT.bufs[kc], XT.ap[:, kc, :], XT.bufs[kc], RBC, aM, aM.bufs[i_], 0, kc, TM, True, kc)

    NPRE = 3
    for i_ in range(NPRE):
        ada_unit(6)
        ada_unit(7)
    def sumsq_banks():
        return [nextbank() for _ in range(3)]

    SSG = [(0, 512), (512, 512), (1024, 192)]
    RBC = A.alloc("RBC", [TT], F32)
    sq2 = [A.alloc("sq%d" % i, [TT], BF16) for i in range(2)]
    ssb = [(psum[i].ap(), psb[i]) for i in range(3)]
    for kc in range(16):
        sq = sq2[kc % 2]
        act(sq.ap, XT.ap[:, kc, :], AF.Square, [XT.bufs[kc]], [sq.b])
        for gi, (o, n) in enumerate(SSG):
            mm(ssb[gi][0][:, 0:n], ones_b.ap, sq.ap[:, o:o + n], kc == 0, kc == 15, [ones_b.b, sq.b], [ssb[gi][1]])
    for gi, (o, n) in enumerate(SSG):
        ts("dve", RBC.ap[:, o:o + n], ssb[gi][0][:, 0:n], 1.0 / D, EPS, ALU.mult, ALU.add, [ssb[gi][1]], [RBC.b])
    rsqrt_inplace(RBC.ap, RBC.b)

    for i_ in range(NPRE):
        ada_pair_mod(i_)
    for i_ in range(NPRE, 8):
        ada_unit(6)
        ada_unit(7)
        ada_pair_mod(i_)
    for t_ in ring_start:
        ring.remove(t_)
        A.free(t_)
    wi_stream = Stream([wfm_d[u] for u in range(9)] + [wtm_d[u] for u in range(5)])
    wi_stream.prefetch(3)
    wsT_f = A.alloc("wsT_f", [8, 128], F32)
    tri = A.alloc("tri", [128], F32)
    WsT = A.alloc("WsT", [8, 128], BF16)
    ld(wsT_f.ap, wsT_d, [wsT_f.b])
    ld(tri.ap, tri_d, [tri.b])
    tt("dve", WsT.ap, wsT_f.ap, tri.ap.unsqueeze(1).broadcast_to([128, 8, 128]), ALU.mult,
       [wsT_f.b, tri.b], [WsT.b])
    mrep = A.alloc("mrep", [8, 4], F32)
    blkm = A.alloc("blkm", [64], F32)
    Wblk = A.alloc("Wblk", [8, 64], BF16)
    ld(blkm.ap[0:64, :], blkmask_d, [blkm.b])
    for b_ in range(16):
        ld(mrep.ap[b_ * 4:(b_ + 1) * 4, :, :], wsT_d[0:4, :, 0:4], [mrep.b], par=True)
    tt("dve", Wblk.ap[0:64].rearrange("p h (b i) -> p h b i", b=16),
       mrep.ap[0:64].unsqueeze(2).broadcast_to([64, 8, 16, 4]),
       blkm.ap[0:64].rearrange("p (b i) -> p b i", b=16).unsqueeze(1).broadcast_to([64, 8, 16, 4]),
       ALU.mult, [mrep.b, blkm.b], [Wblk.b])
    bsr = A.alloc("bsr", [8, 128], F32)
    bs_hi = A.alloc("bs_hi", [8, 128], BF16)
    bs_lo = A.alloc("bs_lo", [8, 128], BF16)
    bs_t = A.alloc("bs_t", [8, 128], F32)
    bsS_hi = A.alloc("bsS_hi", [8, 64], BF16)
    bsS_lo = A.alloc("bsS_lo", [8, 64], BF16)
    ones_r = A.alloc("ones_r", [128], BF16)
    ld(bsr.ap[0:1], bs_d.rearrange("o (h i) -> o h i", h=8), [bsr.b])
    fw.op("dve", lambda e: e.memset(ones_r.ap[0:1, :], 1.0), (), [ones_r.b])
    cp("dve", bs_hi.ap[0:1], bsr.ap[0:1], [bsr.b], [bs_hi.b])
    cp("dve", bs_t.ap[0:1], bs_hi.ap[0:1], [bs_hi.b], [bs_t.b])
    tt("dve", bs_lo.ap[0:1], bsr.ap[0:1], bs_t.ap[0:1], ALU.subtract, [bsr.b, bs_t.b], [bs_lo.b])
    cp("dve", bsS_hi.ap[0:1].rearrange("p h (b i) -> p h b i", b=16),
       bs_hi.ap[0:1, :, 0:4].unsqueeze(2).broadcast_to([1, 8, 16, 4]), [bs_hi.b], [bsS_hi.b])
    cp("dve", bsS_lo.ap[0:1].rearrange("p h (b i) -> p h b i", b=16),
       bs_lo.ap[0:1, :, 0:4].unsqueeze(2).broadcast_to([1, 8, 16, 4]), [bs_lo.b], [bsS_lo.b])
    vgain = A.alloc("vgain", [1024], F32)
    ld(vgain.ap, vgain_d.partition_broadcast(128), [vgain.b])
    for t_ in (wsT_f, tri, mrep, blkm, bsr, bs_t):
        A.free(t_)

    A.free(XT)
    for t_ in (tmp2[0], tmp2[1], tmps):
        A.free(t_)
    A.free(sq2[0])
    A.free(sq2[1])
    KcT = A.alloc("KcT", [16, 128], BF16)
    Vc = A.alloc("Vc", [16, 128], BF16)
    fw.dma("pool", lambda e: e.dma_start(out=KcT.ap.rearrange("p a b -> p (a b)"), in_=ckT_d, max_dma_last_dim=4096), (), [KcT.b])
    fw.dma("pool", lambda e: e.dma_start(out=Vc.ap.rearrange("p a b -> p (a b)"), in_=cv_d, max_dma_last_dim=4096), (), [Vc.b])


    if stop == 1:
        return finish()
    tbl = A.alloc("tbl", [16], F32)
    oh = A.alloc("oh", [128], F32)
    a_sb = A.alloc("a_sb", [383], F32)
    sinkP = A.alloc("sinkP", [16], F32)
    nsinkP = A.alloc("nsinkP", [16], F32)
    sinkS = A.alloc("sinkS", [2], F32)
    nsinkS = A.alloc("nsinkS", [2], F32)
    blk0 = A.alloc("blk0", [1], F32)
    biasP = A.alloc("biasP", [16, 256], F32)
    biasS = A.alloc("biasS", [2, 132], F32)
    ld(tbl.ap[0:32, :], tbl_d, [tbl.b])
    ld(oh.ap[0:32, :], oh_d, [oh.b])
    ld(blk0.ap, blk0_d, [blk0.b])
    ld(sinkP.ap, sink_d.partition_broadcast(128), [sinkP.b])
    for g in range(8):
        src = bass.AP(tensor=sink_d.tensor, offset=2 * g, ap=[[0, 4], [1, 2]])
        ld(sinkS.ap[g * 4:(g + 1) * 4, :], src, [sinkS.b], par=True)
    ts("dve", nsinkP.ap, sinkP.ap, -1.0, None, ALU.mult, None, [sinkP.b], [nsinkP.b])
    ts("dve", nsinkS.ap[0:32, :], sinkS.ap[0:32, :], -1.0, None, ALU.mult, None, [sinkS.b], [nsinkS.b])
    pt, pb = nextbank()
    mm(pt[0:16, 0:128], tbl.ap[0:32, :], oh.ap[0:32, :], True, True, [tbl.b, oh.b], [pb])
    fw.op("dve", lambda e: e.memset(a_sb.ap[0:16, :], NEG), (), [a_sb.b])
    cp("dve", a_sb.ap[0:16, 127:255], pt[0:16, 0:128], [pb], [a_sb.b])
    ld(a_dr, a_sb.ap[0:16, :], [a_dr_buf], [a_sb.b])
    T1 = A.alloc("T1", [16, 256], F32)
    S1 = A.alloc("S1", [2, 128], F32)
    S1n = A.alloc("S1n", [2, 4], F32)
    for h_ in range(16):
        ld(T1.ap[:, h_, :], bass.AP(tensor=a_dr_t, offset=383 * h_, ap=[[1, 128], [1, 256]]), [T1.b], [a_dr_buf], par=True)
    for g in range(8):
        ld(S1.ap[g * 4:(g + 1) * 4, :, :],
           bass.AP(tensor=a_dr_t, offset=2 * g * 383 + 128, ap=[[1, 4], [383, 2], [1, 128]]), [S1.b], [a_dr_buf], par=True)
        ld(S1n.ap[g * 4:(g + 1) * 4, :, :],
           bass.AP(tensor=a_dr_t, offset=2 * g * 383 + 124, ap=[[1, 4], [383, 2], [1, 4]]), [S1n.b], [a_dr_buf], par=True)
    cp("dve", biasP.ap, T1.ap[:, :, ::-1], [T1.b], [biasP.b])
    cp("dve", biasS.ap[0:32, :, 0:128], S1.ap[0:32, :, ::-1], [S1.b], [biasS.b])
    cp("dve", biasS.ap[0:32, :, 128:132], S1n.ap[0:32, :, ::-1], [S1n.b], [biasS.b])
    for t_ in (T1, S1, S1n, a_sb, tbl, oh):
        A.free(t_)
    if stop == 3:
        return finish()
    QT = A.alloc("QT", [8, TM], BF16, nbufs=8)
    KT = A.alloc("KT", [TT], BF16)
    GUT = A.alloc("GUT", [8, TM], BF16, nbufs=8)
    KOUT = A.alloc("KOUT", [192], F32)
    fm_chunks = [("q", c) for c in range(8)] + [("k", 0)] + [("gu", h) for h in range(8)]
    for u in range(9):
        slot = wi_stream.get()
        wi_stream.prefetch(1)
        w = slot.ap.rearrange("p (k j c) -> p k j c", k=16, j=2)
        for j in range(2):
            ci = 2 * u + j
            if ci >= len(fm_chunks):
                break
            kind, idx = fm_chunks[ci]
            groups = GROUPS + ([(TM, TH)] if kind == "k" else [])
            for (o, n) in groups:
                pt, pb = nextbank()
                for kc in range(16):
                    mm(pt[:, 0:n], w[:, kc, j, :], HT.ap[:, kc, o:o + n], kc == 0, kc == 15,
                       [slot.b, HT.bufs[kc]], [pb])
                if kind == "q":
                    act(QT.ap[:, idx, o:o + n], pt[:, 0:n], AF.Identity, [pb], [QT.bufs[idx]], scale=0.125)
                elif kind == "k":
                    act(KT.ap[:, o:o + n], pt[:, 0:n], AF.Identity, [pb], [KT.b])
                    if o == 512:
                        cp("dve", KOUT.ap[:, 0:128], pt[:, 384:512], [pb], [KOUT.b])
                    if o == 1024:
                        cp("dve", KOUT.ap[:, 128:192], pt[:, 0:64], [pb], [KOUT.b])
                else:
                    act(GUT.ap[:, idx, o:o + n], pt[:, 0:n], AF.Gelu_apprx_tanh, [pb], [GUT.bufs[idx]])
    out_tickets.append(ld(kT_out, KOUT.ap, [Buf("kT_out")], [KOUT.b], sembuf=KOUT.b))

    if stop == 4:
        return finish()
    TILES = [(t_ * 128, 128) for t_ in range(8)] + [(TP, TS)]
    GV = A.alloc("GV", [9, 1024], BF16, nbufs=9)
    V = A.alloc("V", [10, 128], BF16, nbufs=10)
    GVOUT = A.alloc("GVOUT", [1024], F32)
    VOUT = A.alloc("VOUT", [2, 128], F32)
    gtmp = [A.alloc("gtmp%d" % i, [256], F32) for i in range(2)]
    sqt2 = [A.alloc("sqt%d" % i, [256], F32) for i in range(2)]
    gn2 = [A.alloc("gn%d" % i, [256], F32) for i in range(2)]
    ssv2 = [A.alloc("ssv%d" % i, [2], F32) for i in range(2)]
    mhalf = A.alloc("mhalf", [2], F32)
    fw.op("pool", lambda e: e.memset(mhalf.ap, -0.5), (), [mhalf.b])
    it = 0
    for u in range(4):
        slot = wi_stream.get()
        wi_stream.prefetch(1)
        w = slot.ap.rearrange("p (k c) -> p k c", k=16)
        for ti, (o, n) in enumerate(TILES):
            pt, pb = nextbank()
            for kc in range(16):
                mm(pt[0:n, 0:256], HT.ap[:, kc, o:o + n], w[:, kc, :], kc == 0, kc == 15,
                   [slot.b, HT.bufs[kc]], [pb])
            g_, sqt, gn, ssv = gtmp[it % 2], sqt2[it % 2], gn2[it % 2], ssv2[it % 2]
            it += 1
            act(g_.ap[0:n], pt[0:n, 0:256], AF.Gelu_apprx_tanh, [pb], [g_.b])
            for j_ in range(2):
                act(sqt.ap[0:n, j_ * 128:(j_ + 1) * 128], g_.ap[0:n, j_ * 128:(j_ + 1) * 128], AF.Square, [g_.b], [sqt.b, ssv.b],
                    accum=ssv.ap[0:n, j_:j_ + 1])
            ts("dve", ssv.ap[0:n], ssv.ap[0:n], 1.0 / 128, EPS, ALU.mult, ALU.add, [ssv.b], [ssv.b])
            tt("pool", ssv.ap[0:n], ssv.ap[0:n], mhalf.ap[0:n], ALU.pow, [ssv.b, mhalf.b], [ssv.b])
            tt("dve", gn.ap[0:n].rearrange("p (h c) -> p h c", h=2), g_.ap[0:n].rearrange("p (h c) -> p h c", h=2),
               ssv.ap[0:n].unsqueeze(2).broadcast_to([n, 2, 128]), ALU.mult, [g_.b, ssv.b], [gn.b])
            cols = slice(u * 256, (u + 1) * 256)
            tt("dve", GV.ap[0:n, ti, cols], gn.ap[0:n], vgain.ap[0:n, cols], ALU.mult, [gn.b, vgain.b], [GV.bufs[ti]])
            if ti == 8:
                tt("dve", GVOUT.ap[0:n, cols], gn.ap[0:n], vgain.ap[0:n, cols], ALU.mult, [gn.b, vgain.b], [GVOUT.b])
    out_tickets.append(ld(gv_out, GVOUT.ap[0:64, :], [Buf("gv_out")], [GVOUT.b], sembuf=GVOUT.b))
    slot = wi_stream.get()
    w = slot.ap.rearrange("p (k c) -> p k c", k=16)
    for ti, (o, n) in enumerate(TILES + [(TM, TH)]):
        pt, pb = nextbank()
        for kc in range(16):
            mm(pt[0:n, 0:128], HT.ap[:, kc, o:o + n], w[:, kc, 0:128], kc == 0, kc == 15,
               [slot.b, HT.bufs[kc]], [pb])
        act(V.ap[0:n, ti, :], pt[0:n, 0:128], AF.Identity, [pb], [V.bufs[ti]])
        if ti == 7:
            cp("dve", VOUT.ap[:, 0, :], pt[:, 0:128], [pb], [VOUT.b])
        if ti == 8:
            cp("dve", VOUT.ap[0:64, 1, :], pt[0:64, 0:128], [pb], [VOUT.b])
    vo_b = Buf("v_out")
    out_tickets.append(ld(v_out[0:128, :], VOUT.ap[:, 0, :], [vo_b], [VOUT.b], sembuf=VOUT.b))
    out_tickets.append(ld(v_out[128:192, :], VOUT.ap[0:64, 1, :], [vo_b], [VOUT.b], sembuf=VOUT.b))
    for t_ in (gtmp[0], gtmp[1], sqt2[0], sqt2[1], gn2[0], gn2[1], ssv2[0], ssv2[1], mhalf, vgain, RBC, KOUT, VOUT, GVOUT):
        A.free(t_)
    A.free(HT)

    if stop == 5:
        return finish()
    ada_stream.limit = 48
    ada_stream.prefetch(3)

    MT = A.alloc("MT", [16, TM], BF16, nbufs=16)
    RA = A.alloc("RA", [TM], F32)
    S2 = [A.alloc("S2_%d" % i, [4, 256], F32) for i in range(2)]
    Pb = [A.alloc("P_%d" % i, [4, 256], BF16) for i in range(2)]
    PTs = [A.alloc("PT_%d" % i, [8, 128], BF16) for i in range(2)]
    SQA = A.alloc("SQA", [256], BF16)
    st = A.alloc("st", [32], F32)
    def att_1a(qb, m, idx):
        qo = qb * 128
        sbanks = [(psum[(idx % 2) * 2 + i].ap(), psb[(idx % 2) * 2 + i]) for i in range(2)]
        s2 = S2[idx % 2]
        for s in range(4):
            c, half = 2 * m + s // 2, s % 2
            bank, bb = sbanks[half]
            cb = (s // 2) * 256
            hp = slice(half * 64, (half + 1) * 64)
            lh = QT.ap[hp, c, qo:qo + 128]
            if qb >= 1:
                mm(bank[:, cb:cb + 256], lh, KT.ap[hp, qo - 128:qo + 128], True, True, [QT.bufs[c], KT.b], [bb])
            else:
                mm(bank[:, cb:cb + 128], lh, KT.ap[hp, TM:TT], True, True, [QT.bufs[c], KT.b], [bb])
                mm(bank[:, cb + 128:cb + 256], lh, KT.ap[hp, 0:128], True, True, [QT.bufs[c], KT.b], [bb])
        for i in range(2):
            tt("dve", s2.ap[:, i:4:2, :], sbanks[i][0].rearrange("p (a b) -> p a b", a=2),
               biasP.ap[:, 4 * m + i:4 * m + 4:2, :], ALU.add, [sbanks[i][1], biasP.b], [s2.b], nowaw=True)
        if qb == 0:
            ts("dve", s2.ap[:, :, 0:128], s2.ap[:, :, 0:128], blk0.ap[:, 0:1], None, ALU.add, None, [s2.b, blk0.b], [s2.b])
        stb = st2[idx % 3]
        nmx = stb.ap[:, 0:4]
        fw.op("dve", lambda e, s2=s2, nmx=nmx: e.tensor_reduce(nmx, s2.ap, AX.X, ALU.max, negate=True), [s2.b], [stb.bufs[0]])
        tt("dve", nmx, nmx, nsinkP.ap[:, 4 * m:4 * m + 4], ALU.min, [stb.bufs[0], nsinkP.b], [stb.bufs[0]])

    def att_1b(qb, m, idx):
        s2, pb_, stb = S2[idx % 2], Pb[idx % 3], st2[idx % 3]
        for s in range(4):
            act(pb_.ap[:, s, :], s2.ap[:, s, :], AF.Exp, [s2.b, stb.bufs[0]], [pb_.b, stb.bufs[1]],
                bias=stb.ap[:, s:s + 1], accum=stb.ap[:, 4 + s:5 + s], nowaw=True)

    def att_1c(qb, m, idx):
        pb_, stb = Pb[idx % 3], st2[idx % 3]
        nmx, rs, es, den = (stb.ap[:, 0:4], stb.ap[:, 4:8], stb.ap[:, 8:12], stb.ap[:, 12:16])
        b0, b1, b2 = stb.bufs
        tt("pool", es, sinkP.ap[:, 4 * m:4 * m + 4], nmx, ALU.add, [sinkP.b, b0], [b2])
        act(es, es, AF.Exp, [b2], [b2])
        tt("pool", den, rs, es, ALU.add, [b1, b2], [b2])
        fw.op("dve", lambda e, den=den: e.reciprocal(den, den), [b2], [b2])
        for s_ in range(4):
            act(pb_.ap[:, s_, :], pb_.ap[:, s_, :], AF.Identity, [pb_.b, b2], [pb_.b], scale=den[:, s_:s_ + 1], nowaw=True)

    def att_stage2(qb, m, idx):
        qo = qb * 128
        pb_, ptt = Pb[idx % 3], PTs[idx % 2]
        ssA_t, ssA_b = psum[6].ap(), psb[6]
        ptb_t, ptb_b = psum[4].ap().bitcast(BF16), psb[4]
        for s in range(4):
            for kb in range(2):
                tr(ptb_t[:, (s * 2 + kb) * 128:(s * 2 + kb + 1) * 128], pb_.ap[:, s, kb * 128:(kb + 1) * 128],
                   ident_b.ap, [pb_.b, ident_b.b], [ptb_b])
        cp("dve", ptt.ap.rearrange("p a b -> p (a b)"), ptb_t, [ptb_b], [ptt.b])
        pv_t, pv_b = psum[5].ap(), psb[5]
        for s in range(4):
            c, half = 2 * m + s // 2, s % 2
            hp = slice(half * 64, (half + 1) * 64)
            for kb in range(2):
                kt_i = (9 if qb == 0 else qb - 1) if kb == 0 else qb
                mm(pv_t[hp, (s // 2) * 128:(s // 2 + 1) * 128], V.ap[:, kt_i, hp], ptt.ap[:, s * 2 + kb, :],
                   kb == 0, kb == 1, [V.bufs[kt_i], ptt.b], [pv_b])
        for j in range(2):
            c = 2 * m + j
            ts("dve", MT.ap[:, c, qo:qo + 128], pv_t[:, j * 128:(j + 1) * 128], gA.ap[:, c:c + 1], None, ALU.mult, None,
               [pv_b, gA.b], [MT.bufs[c]])
        sqa = SQA2[idx % 2]
        act(sqa.ap, pv_t[:, 0:256], AF.Square, [pv_b], [sqa.b])

        def fin():
            for j in range(2):
                mm(ssA_t[:, 0:128], ones_b.ap, sqa.ap[:, j * 128:(j + 1) * 128], m == 0 and j == 0, m == 3 and j == 1,
                   [ones_b.b, sqa.b], [ssA_b])
            if m == 3:
                ts("dve", RA.ap[:, qo:qo + 128], ssA_t[:, 0:128], 1.0 / 1024, EPS, ALU.mult, ALU.add, [ssA_b], [RA.b])
        return fin

    st2 = [A.alloc("st%d" % i, [16], F32, nbufs=3) for i in range(3)]
    mone = A.alloc("mone", [4], F32)
    fw.op("pool", lambda e: e.memset(mone.ap, -1.0), (), [mone.b])
    Pb.append(A.alloc("P_2", [4, 256], BF16))
    SQA2 = [SQA, A.alloc("SQAb", [256], BF16)]
    groups_ = [(qb, m) for qb in range(8) for m in range(4)]
    NG_ = len(groups_)
    pend = None
    for t in range(NG_ + 3):
        if t < NG_:
            att_1a(groups_[t][0], groups_[t][1], t)
        if 0 <= t - 1 < NG_:
            att_1b(groups_[t - 1][0], groups_[t - 1][1], t - 1)
        if 0 <= t - 2 < NG_:
            att_1c(groups_[t - 2][0], groups_[t - 2][1], t - 2)
        if 0 <= t - 3 < NG_:
            fin = att_stage2(groups_[t - 3][0], groups_[t - 3][1], t - 3)
            if pend is not None:
                pend()
            pend = fin
            if (t - 3) % 2 == 0:
                ada_unit(7)
    pend()
    for t_ in (mone, st2[0], st2[1], st2[2], SQA2[1], S2[0], S2[1], Pb[0], Pb[1], Pb[2], PTs[0], PTs[1], SQA):
        A.free(t_)

    if stop == 6:
        return finish()
    Vn = A.alloc("Vn", [16, 128], BF16)
    for b_ in range(16):
        ld(Vn.ap[0:4, b_, :], V.ap[b_ * 4:(b_ + 1) * 4, 8, :], [Vn.b], [V.bufs[8]], par=True)
    QS = A.alloc("QS", [16, 32], BF16)
    cp("dve", QS.ap.rearrange("p b (c t) -> p b c t", c=8), QT.ap[:, :, TP:TM].rearrange("p c (b t) -> p b c t", t=4),
       QT.bufs, [QS.b])
    SS2 = A.alloc("SS2", [4, 132], F32)
    Ps = A.alloc("Ps", [4, 132], BF16)
    PTS = A.alloc("PTS", [4, 32], BF16)
    PTN = A.alloc("PTN", [4, 32], BF16)
    os_t, os_b = psum[7].ap(), psb[7]
    SS2s = [SS2, A.alloc("SS2b", [4, 132], F32)]
    Pss = [Ps, A.alloc("Psb", [4, 132], BF16)]
    sts = [st, A.alloc("stb", [32], F32)]
    def samp_a(grp):
        SS2, Ps, st = SS2s[grp % 2], Pss[grp % 2], sts[grp % 2]
        sbanks = [(psum[(grp % 2) * 2 + i].ap(), psb[(grp % 2) * 2 + i]) for i in range(2)]
        for l in range(4):
            b_, kvh = 2 * grp + l // 2, l % 2
            bank, bb = sbanks[kvh]
            base = (l // 2) * 256
            hp = slice(kvh * 64, (kvh + 1) * 64)
            lh = QS.ap[hp, b_, :]
            mm(bank[0:32, base:base + 128], lh, KcT.ap[hp, b_, :], True, True, [QS.b, KcT.b], [bb])
            mm(bank[0:32, base + 128:base + 132], lh, KT.ap[hp, TP + b_ * 4:TP + b_ * 4 + 4], True, True,
               [QS.b, KT.b], [bb])
        for i in range(2):
            tt("dve", SS2.ap[0:32, i:4:2, :], sbanks[i][0][0:32, :].rearrange("p (a b) -> p a b", a=2)[:, :, 0:132],
               biasS.ap[0:32, i, :].unsqueeze(1).broadcast_to([32, 2, 132]), ALU.add, [sbanks[i][1], biasS.b], [SS2.b])
        nmx, rs, es, den = st.ap[0:32, 0:4], st.ap[0:32, 4:8], st.ap[0:32, 8:12], st.ap[0:32, 12:16]
        fw.op("dve", lambda e, nmx=nmx: e.tensor_reduce(nmx, SS2.ap[0:32], AX.X, ALU.max, negate=True), [SS2.b], [st.b])
        tt("dve", nmx.rearrange("p (a b) -> p a b", a=2), nmx.rearrange("p (a b) -> p a b", a=2),
           nsinkS.ap[0:32].unsqueeze(1).broadcast_to([32, 2, 2]), ALU.min, [st.b, nsinkS.b], [st.b])
        for l in range(4):
            act(Ps.ap[0:32, l, :], SS2.ap[0:32, l, :], AF.Exp, [SS2.b, st.b], [Ps.b, st.b],
                bias=st.ap[0:32, l:l + 1], accum=st.ap[0:32, 4 + l:5 + l])
        tt("dve", es.rearrange("p (a b) -> p a b", a=2), nmx.rearrange("p (a b) -> p a b", a=2),
           sinkS.ap[0:32].unsqueeze(1).broadcast_to([32, 2, 2]), ALU.add, [sinkS.b, st.b], [st.b])
        act(es, es, AF.Exp, [st.b], [st.b])
        tt("dve", den, rs, es, ALU.add, [st.b], [st.b])
        fw.op("dve", lambda e, den=den: e.reciprocal(den, den), [st.b], [st.b])
        tt("dve", Ps.ap[0:32], Ps.ap[0:32], den.unsqueeze(2).broadcast_to([32, 4, 132]), ALU.mult, [Ps.b, st.b], [Ps.b])
    def samp_b(grp):
        Ps = Pss[grp % 2]
        ptb_t, ptb_b = psum[4].ap().bitcast(BF16), psb[4]
        for l in range(4):
            tr(ptb_t[:, l * 32:(l + 1) * 32], Ps.ap[0:32, l, 0:128], ident_b.ap[0:32, 0:32], [Ps.b, ident_b.b], [ptb_b])
            tr(ptb_t[0:4, 128 + l * 32:128 + (l + 1) * 32], Ps.ap[0:32, l, 128:132], ident_b.ap[0:32, 0:32],
               [Ps.b, ident_b.b], [ptb_b])
        act(PTS.ap.rearrange("p a b -> p (a b)"), ptb_t[:, 0:128], AF.Identity, [ptb_b], [PTS.b])
        cp("dve", PTN.ap[0:4].rearrange("p a b -> p (a b)"), ptb_t[0:4, 128:256], [ptb_b], [PTN.b])
        for l in range(4):
            b_, kvh = 2 * grp + l // 2, l % 2
            hp = slice(kvh * 64, (kvh + 1) * 64)
            o_ap = os_t[hp, b_ * 32:(b_ + 1) * 32]
            mm(o_ap, Vc.ap[:, b_, hp], PTS.ap[:, l, :], True, False, [Vc.b, PTS.b], [os_b])
            mm(o_ap, Vn.ap[0:4, b_, hp], PTN.ap[0:4, l, :], False, True, [Vn.b, PTN.b], [os_b])
    samp_a(0)
    for grp in range(8):
        if grp + 1 < 8:
            samp_a(grp + 1)
        samp_b(grp)
        ada_unit(5)
    for t_ in (SS2s[1], Pss[1], sts[1]):
        A.free(t_)
    tt("dve", MT.ap[:, 0:8, TP:TM].rearrange("p c (b t) -> p b c t", t=4),
       os_t.rearrange("p (b c t) -> p b c t", b=16, c=8),
       gA.ap.unsqueeze(1).unsqueeze(3).broadcast_to([128, 16, 8, 4]), ALU.mult, [os_b, gA.b], MT.bufs[0:8])
    SQS = A.alloc("SQS", [512], BF16)
    act(SQS.ap.rearrange("p (c b t) -> p b c t", c=8, b=16), os_t.rearrange("p (b c t) -> p b c t", b=16, c=8),
        AF.Square, [os_b], [SQS.b])
    ssA_t, ssA_b = psum[6].ap(), psb[6]
    for c in range(8):
        mm(ssA_t[:, 0:64], ones_b.ap, SQS.ap[:, c * 64:(c + 1) * 64], c == 0, c == 7, [ones_b.b, SQS.b], [ssA_b])
    ts("dve", RA.ap[:, TP:TM], ssA_t[:, 0:64], 1.0 / 1024, EPS, ALU.mult, ALU.add, [ssA_b], [RA.b])
    rsqrt_inplace(RA.ap, RA.b)
    for t_ in (st, Vn, QS, SS2, Ps, PTS, PTN, SQS, KcT, Vc,
               biasP, biasS, QT, KT, V, sinkP, nsinkP, sinkS, nsinkS, blk0):
        A.free(t_)

    if stop == 7:
        return finish()
    RG = A.alloc("RG", [TM], F32)
    GO = [A.alloc("GO%d" % i, [4, 128], F32) for i in range(2)]
    SQG = [A.alloc("SQG%d" % i, [4, 128], BF16) for i in range(3)]
    it = 0
    pendgq = []
    for ti, (o, n) in enumerate(TILES):
        ssG_t, ssG_b = psum[6].ap(), psb[6]
        for hh in range(2):
            pt, pb = nextbank(allowed=(0, 1, 2, 3))
            for h4 in range(4):
                h = hh * 4 + h4
                out = pt[:, h4 * 128:h4 * 128 + n]
                if ti < 8:
                    rhs, rhi, rlo = WsT.ap[:, h, :], bs_hi.ap[0:1, h, :], bs_lo.ap[0:1, h, :]
                    rb_ = [WsT.b]
                else:
                    rhs, rhi, rlo = Wblk.ap[0:64, h, :], bsS_hi.ap[0:1, h, :], bsS_lo.ap[0:1, h, :]
                    rb_ = [Wblk.b]
                mm(out, GV.ap[0:n, ti, h * 128:(h + 1) * 128], rhs, True, False, [GV.bufs[ti]] + rb_, [pb])
                mm(out, ones_r.ap[0:1, :], rhi, False, False, [ones_r.b, bs_hi.b, bsS_hi.b], [pb])
                mm(out, ones_r.ap[0:1, :], rlo, False, True, [ones_r.b, bs_lo.b, bsS_lo.b], [pb])
            go, sqg = GO[it % 2], SQG[it % 3]
            it += 1
            hs = slice(hh * 4, hh * 4 + 4)
            tt("dve", go.ap[:, :, 0:n], pt.rearrange("p (a b) -> p a b", a=4)[:, :, 0:n], GUT.ap[:, hs, o:o + n],
               ALU.mult, [pb] + GUT.bufs[hh * 4:hh * 4 + 4], [go.b])
            tt("dve", MT.ap[:, 8 + hh * 4:12 + hh * 4, o:o + n], go.ap[:, :, 0:n],
               gG.ap[:, hs].unsqueeze(2).broadcast_to([128, 4, n]), ALU.mult, [go.b, gG.b], MT.bufs[8 + hh * 4:12 + hh * 4])
            act(sqg.ap[:, :, 0:n], go.ap[:, :, 0:n], AF.Square, [go.b], [sqg.b])
            if len(pendgq) >= 2:
                pendgq.pop(0)()

            def pendg(sqg=sqg, n=n, hh=hh, o=o):
                for h4 in range(4):
                    mm(ssG_t[:, 0:n], ones_b.ap, sqg.ap[:, h4, 0:n], hh == 0 and h4 == 0, hh == 1 and h4 == 3,
                       [ones_b.b, sqg.b], [ssG_b])
                if hh == 1:
                    ts("dve", RG.ap[:, o:o + n], ssG_t[:, 0:n], 1.0 / 1024, EPS, ALU.mult, ALU.add, [ssG_b], [RG.b])
            pendgq.append(pendg)
            if hh == 1 and ti < 8:
                ada_unit(5)
    while pendgq:
        pendgq.pop(0)()
    rsqrt_inplace(RG.ap, RG.b)
    for t_ in (cT, siluT, GO[0], GO[1], SQG[0], SQG[1], SQG[2], GUT, GV, WsT, Wblk, bs_hi, bs_lo, bsS_hi, bsS_lo, ones_r):
        A.free(t_)

    if stop == 8:
        return finish()
    OT = A.alloc("OT", [16, TM], F32, nbufs=16)
    ta = [A.alloc("ta%d" % i, [512], F32) for i in range(2)]
    tb = [A.alloc("tb%d" % i, [512], F32) for i in range(2)]
    sqo = [A.alloc("sqo%d" % i, [512], BF16) for i in range(3)]
    ssO = [(psum[5].ap(), psb[5]), (psum[6].ap(), psb[6]), (psum[7].ap(), psb[7])]
    it = 0
    wo_stream = Stream([wout_d[u] for u in range(8)])
    wo_stream.prefetch(3)
    pendq = []
    for u in range(8):
        slot = wo_stream.get()
        wo_stream.prefetch(1)
        w = slot.ap.rearrange("p (k j c) -> p k j c", k=16, j=2)
        for j in range(2):
            m = 2 * u + j
            for gi, (o, n) in enumerate(GROUPS):
                p1, b1 = nextbank(allowed=(0, 1, 2, 3, 4))
                p2, b2 = nextbank(allowed=(0, 1, 2, 3, 4))
                for kc in range(8):
                    mm(p1[:, 0:n], w[:, kc, j, :], MT.ap[:, kc, o:o + n], kc == 0, kc == 7, [slot.b, MT.bufs[kc]], [b1])
                for kc in range(8, 16):
                    mm(p2[:, 0:n], w[:, kc, j, :], MT.ap[:, kc, o:o + n], kc == 8, kc == 15, [slot.b, MT.bufs[kc]], [b2])
                a_, b__, q_ = ta[it % 2], tb[it % 2], sqo[it % 3]
                it += 1
                tt("dve", a_.ap[:, 0:n], p1[:, 0:n], RA.ap[:, o:o + n], ALU.mult, [b1, RA.b], [a_.b])
                tt("dve", b__.ap[:, 0:n], p2[:, 0:n], RG.ap[:, o:o + n], ALU.mult, [b2, RG.b], [b__.b])
                tt("pool", OT.ap[:, m, o:o + n], a_.ap[:, 0:n], b__.ap[:, 0:n], ALU.add, [a_.b, b__.b], [OT.bufs[m]])
                act(q_.ap[:, 0:n], OT.ap[:, m, o:o + n], AF.Square, [OT.bufs[m]], [q_.b])
                if len(pendq) >= 2:
                    pendq.pop(0)()

                def pend_(q_=q_, gi=gi, n=n, m=m):
                    mm(ssO[gi][0][:, 0:n], ones_b.ap, q_.ap[:, 0:n], m == 0, m == 15, [ones_b.b, q_.b], [ssO[gi][1]])
                pendq.append(pend_)
    while pendq:
        pendq.pop(0)()
    if os.environ.get("K_DBG"):
        dbg_mt = nc.dram_tensor("dbg_mt", [128, 16 * TM], BF16, kind="ExternalOutput").ap()
        dbg_ra = nc.dram_tensor("dbg_ra", [128, TM], F32, kind="ExternalOutput").ap()
        dbg_rg = nc.dram_tensor("dbg_rg", [128, TM], F32, kind="ExternalOutput").ap()
        dbg_cm = nc.dram_tensor("dbg_cm", [128, 16 * 17], F32, kind="ExternalOutput").ap()
        dbg_ot = nc.dram_tensor("dbg_ot", [128, 16 * TM], F32, kind="ExternalOutput").ap()
        db = Buf("dbg")
        out_tickets.append(ld(dbg_mt, MT.ap.rearrange("p a b -> p (a b)"), [db], MT.bufs, sembuf=db))
        out_tickets.append(ld(dbg_ra, RA.ap, [db], [RA.b], sembuf=db))
        out_tickets.append(ld(dbg_rg, RG.ap, [db], [RG.b], sembuf=db))
        out_tickets.append(ld(dbg_cm, cm.ap.rearrange("p a b -> p (a b)"), [db], [cm.b], sembuf=db))
        out_tickets.append(ld(dbg_ot, OT.ap.rearrange("p a b -> p (a b)"), [db], OT.bufs, sembuf=db))
    for t_ in (ta[0], ta[1], tb[0], tb[1], sqo[0], sqo[1], sqo[2], RA, RG):
        A.free(t_)
    A.free(MT)
    RB2 = A.alloc("RB2", [TM], F32)
    for gi, (o, n) in enumerate(GROUPS):
        ts("dve", RB2.ap[:, o:o + n], ssO[gi][0][:, 0:n], 1.0 / D, EPS, ALU.mult, ALU.add, [ssO[gi][1]], [RB2.b])
    rsqrt_inplace(RB2.ap, RB2.b)

    if stop == 9:
        return finish()
    assert ada_state["i"] == 48

    def residual(src, src_bufs, rb, c_t, x_src_fn, x_src_bufs_fn, dst_fn, dst_bufs_fn, after_fn):
        xc = [A.alloc("xc%d" % i, [TM], F32) for i in range(3)]
        tm2 = [A.alloc("rtm%d" % i, [TM], F32) for i in range(2)]
        tsm = A.alloc("rts", [64], F32)
        for kc in range(16):
            x_, t_ = xc[kc % 3], tm2[kc % 2]
            ld(x_.ap, x_src_fn(kc), [x_.b], x_src_bufs_fn(kc))
            tt("dve", t_.ap, src.ap[:, kc, :], rb.ap, ALU.mult, [src_bufs[kc], rb.b], [t_.b])
            dst, dbufs = dst_fn(kc), dst_bufs_fn(kc)
            stt(dst[:, 0:TP], t_.ap[:, 0:TP], c_t.ap[:, kc, 0:1], x_.ap[:, 0:TP], ALU.mult, ALU.add,
                [t_.b, c_t.b, x_.b], dbufs)
            tt("dve", tsm.ap.rearrange("p (b t) -> p b t", t=4), t_.ap[:, TP:TM].rearrange("p (b t) -> p b t", t=4),
               c_t.ap[:, kc, 1:17].unsqueeze(2).broadcast_to([128, 16, 4]), ALU.mult, [t_.b, c_t.b], [tsm.b])
            tt("dve", dst[:, TP:TM], tsm.ap, x_.ap[:, TP:TM], ALU.add, [tsm.b, x_.b], dbufs)
            after_fn(kc)
        for t_ in xc + tm2 + [tsm]:
            A.free(t_)

    sq2 = [A.alloc("sqx%d" % i, [TM], BF16) for i in range(2)]
    ss1 = sumsq_banks()

    def after_x1(kc):
        sq = sq2[kc % 2]
        act(sq.ap, OT.ap[:, kc, :], AF.Square, [OT.bufs[kc]], [sq.b])
        for gi, (o, n) in enumerate(GROUPS):
            mm(ss1[gi][0][:, 0:n], ones_b.ap, sq.ap[:, o:o + n], kc == 0, kc == 15, [ones_b.b, sq.b], [ss1[gi][1]])
        ld(x1_dr[kc * 128:(kc + 1) * 128, :], OT.ap[:, kc, :], [x1_dr_bufs[kc]], [OT.bufs[kc]], eng="act",
           sembuf=x1_dr_bufs[kc])

    residual(OT, OT.bufs, RB2, cm, lambda kc: xT[kc * 128:(kc + 1) * 128, 0:TM], lambda kc: (),
             lambda kc: OT.ap[:, kc, :], lambda kc: [OT.bufs[kc]], after_x1)
    RB3 = A.alloc("RB3", [TM], F32)
    for gi, (o, n) in enumerate(GROUPS):
        ts("dve", RB3.ap[:, o:o + n], ss1[gi][0][:, 0:n], 1.0 / D, EPS, ALU.mult, ALU.add, [ss1[gi][1]], [RB3.b])
    rsqrt_inplace(RB3.ap, RB3.b)
    H2T = A.alloc("H2T", [16, TM], BF16, nbufs=16)
    tmp2 = [A.alloc("tmpB%d" % i, [TT], F32) for i in range(2)]
    tmps = A.alloc("tmpsB", [64], F32)
    for kc in range(16):
        modulate_kc(H2T, H2T.bufs[kc], OT.ap[:, kc, :], OT.bufs[kc], RB3, a2, a2.b, 3, kc, TM, False, kc)
    for t_ in (tmp2[0], tmp2[1], tmps, RB2, RB3, sq2[0], sq2[1]):
        A.free(t_)
    A.free(OT)

    if stop == 10:
        return finish()
    for t_ in ring_extra:
        ring.remove(t_)
        A.free(t_)
    ACC = A.alloc("ACC", [16, TM], F32, nbufs=16)
    HID = A.alloc("HID", [16, TM], BF16, nbufs=16)
    W2G = A.alloc("W2G", [8, 2048], BF16, nbufs=4)
    rl = [A.alloc("rl%d" % i, [512], F32) for i in range(2)]
    state = {"it": 0}

    w1_stream = Stream([wff1_d[u] for u in range(32)])

    def ffn1(g, w2g=None):
        for uu in range(4):
            if w2g is not None and uu >= 2:
                slot = w1_stream.get()
                load_w2_unit(w2g, uu)
            else:
                if w2g is not None:
                    load_w2_unit(w2g, uu)
                slot = w1_stream.get()
            w = slot.ap.rearrange("p (k j c) -> p k j c", k=16, j=2)
            for j in range(2):
                hs = (g * 8 + uu * 2 + j) % 16
                for (o, n) in GROUPS:
                    pt, pb = nextbank()
                    for kc in range(16):
                        mm(pt[:, 0:n], w[:, kc, j, :], H2T.ap[:, kc, o:o + n], kc == 0, kc == 15,
                           [slot.b, H2T.bufs[kc]], [pb])
                    r_ = rl[state["it"] % 2]
                    state["it"] += 1
                    act(r_.ap[:, 0:n], pt[:, 0:n], AF.Relu, [pb], [r_.b])
                    tt("dve", HID.ap[:, hs, o:o + n], r_.ap[:, 0:n], r_.ap[:, 0:n], ALU.mult, [r_.b], [HID.bufs[hs]])

    def load_w2_unit(g, uu):
        dst = w2cur["t"].ap[:, 2 * uu:2 * uu + 2, :].rearrange("p a b -> p (a b)")
        fw.dma("pool", lambda e, uu=uu, g=g, dst=dst: e.dma_start(
            out=dst, in_=wff2_d[g * 4 + uu], max_dma_last_dim=4096), (), [w2cur["t"].bufs[uu]])

    def load_w2(g):
        for uu in range(4):
            load_w2_unit(g, uu)

    def ffn2(g):
        for m in range(16):
            for (o, n) in GROUPS:
                pt, pb = nextbank()
                for k8 in range(8):
                    hs = (g * 8 + k8) % 16
                    mm(pt[:, 0:n], w2cur["t"].ap[:, k8, m * 128:(m + 1) * 128], HID.ap[:, hs, o:o + n], k8 == 0, k8 == 7,
                       [w2cur["t"].bufs[k8 // 2], HID.bufs[hs]], [pb])
                if g == 0:
                    act(ACC.ap[:, m, o:o + n], pt[:, 0:n], AF.Identity, [pb], [ACC.bufs[m]])
                else:
                    tt("dve", ACC.ap[:, m, o:o + n], pt[:, 0:n], ACC.ap[:, m, o:o + n], ALU.add, [pb, ACC.bufs[m]], [ACC.bufs[m]])

    w1_stream.prefetch(2)
    ffn1(0)
    w2cur = {"t": W2G}
    for g in range(8):
        if g + 1 < 8:
            ffn1(g + 1, w2g=g)
            w1_stream.prefetch(2)
            if g + 1 == 7:
                A.free(H2T)
                W2G7 = A.alloc("W2G7", [8, 2048], BF16, nbufs=4)
        if g == 7:
            w2cur["t"] = W2G7
            load_w2(7)
        ffn2(g)
        if g == 6:
            pass
    for t_ in (HID, W2G, W2G7, rl[0], rl[1]):
        A.free(t_)

    if stop == 11:
        return finish()
    sq2 = [A.alloc("sqf%d" % i, [TM], BF16) for i in range(2)]
    ssF = sumsq_banks()
    for m in range(16):
        sq = sq2[m % 2]
        act(sq.ap, ACC.ap[:, m, :], AF.Square, [ACC.bufs[m]], [sq.b])
        for gi, (o, n) in enumerate(GROUPS):
            mm(ssF[gi][0][:, 0:n], ones_b.ap, sq.ap[:, o:o + n], m == 0, m == 15, [ones_b.b, sq.b], [ssF[gi][1]])
    RB4 = A.alloc("RB4", [TM], F32)
    for gi, (o, n) in enumerate(GROUPS):
        ts("dve", RB4.ap[:, o:o + n], ssF[gi][0][:, 0:n], 1.0 / D, EPS, ALU.mult, ALU.add, [ssF[gi][1]], [RB4.b])
    rsqrt_inplace(RB4.ap, RB4.b)
    yo = [A.alloc("yo%d" % i, [TM], F32) for i in range(3)]
    y_b = Buf("yT")

    def after_y(kc):
        out_tickets.append(ld(yT[kc * 128:(kc + 1) * 128, :], yo[kc % 3].ap, [y_b], [yo[kc % 3].b], eng="act",
                              sembuf=yo[kc % 3].b))

    residual(ACC, ACC.bufs, RB4, cf, lambda kc: x1_dr[kc * 128:(kc + 1) * 128, :], lambda kc: [x1_dr_bufs[kc]],
             lambda kc: yo[kc % 3].ap, lambda kc: [yo[kc % 3].b], after_y)

    return finish()


def _t5_bucket_np(n):
    n = np.maximum(n, 0)
    nf = np.maximum(n, 1).astype(np.float32)
    large = 16 + (np.log(nf / np.float32(16)) / np.float32(math.log(8.0)) * np.float32(16)).astype(np.int32)
    large = np.minimum(large, 31)
    return np.where(n < 16, n, large)


def _units_kjc(w, nu):
    return np.ascontiguousarray(w.reshape(16, 128, nu, 2, 128).transpose(2, 1, 0, 3, 4).reshape(nu, 128, UNITW))


_PROGRAM = {}


def kernel(x_prompt, x_sample, cache_k, cache_v, c_prompt, c_sample, rel_bias_table, w_ada, b_ada,
           g_pre_mix, w_in, attn_sinks, gmlp_v_gain, gmlp_w_s, gmlp_b_s, g_attn_out, g_gmlp_out,
           w_out, g_post_mix, g_pre_ff, w_ff1, w_ff2, g_post_ff):
    f32 = np.float32
    x_prompt = np.asarray(x_prompt, f32)
    x_sample = np.asarray(x_sample, f32)
    cache_k = np.asarray(cache_k, f32)
    cache_v = np.asarray(cache_v, f32)
    c_prompt = np.asarray(c_prompt, f32)
    c_sample = np.asarray(c_sample, f32)
    w_in0 = np.asarray(w_in, f32)[0]

    wada_u = _units_kjc(np.asarray(w_ada, f32)[0], 48)
    qcols = []
    for c in range(8):
        qcols += list(range(c * 64, (c + 1) * 64)) + list(range((8 + c) * 64, (9 + c) * 64))
    fm_cols = qcols + list(range(1024, 1152)) + list(range(1280, 2304))
    wfm = np.zeros((2048, 18 * 128), f32)
    wfm[:, :17 * 128] = w_in0[:, fm_cols]
    wfm_u = _units_kjc(wfm, 9)
    wtm = np.zeros((2048, 5 * 256), f32)
    wtm[:, 0:1024] = w_in0[:, 2304:3328]
    wtm[:, 1024:1152] = w_in0[:, 1152:1280]
    wtm_u = np.ascontiguousarray(wtm.reshape(16, 128, 5, 256).transpose(2, 1, 0, 3).reshape(5, 128, UNITW))
    perm = []
    for kc in range(16):
        for p in range(128):
            if kc < 8:
                perm.append(kc * 64 + p if p < 64 else (8 + kc) * 64 + (p - 64))
            else:
                perm.append(1024 + (kc - 8) * 128 + p)
    perm = np.array(perm)
    wout_u = _units_kjc(np.asarray(w_out, f32)[0][perm, :], 8)
    wff1_u = _units_kjc(np.asarray(w_ff1, f32)[0], 32)
    wff2_u = np.ascontiguousarray(np.asarray(w_ff2, f32)[0].reshape(32, 2, 128, 2048).transpose(0, 2, 1, 3).reshape(32, 128, UNITW))

    def fm16(g):
        return np.asarray(g, f32).reshape(16, 128).T

    gT = np.ascontiguousarray(np.concatenate([fm16(g_pre_mix[0]), fm16(g_post_mix[0]), fm16(g_pre_ff[0]), fm16(g_post_ff[0])], axis=1))
    ga = np.asarray(g_attn_out, f32)[0]
    gA = np.zeros((128, 8), f32)
    for c in range(8):
        gA[0:64, c] = ga[c * 64:(c + 1) * 64]
        gA[64:128, c] = ga[(8 + c) * 64:(9 + c) * 64]
    gG = np.ascontiguousarray(np.asarray(g_gmlp_out, f32)[0].reshape(8, 128).T)
    badaT = np.ascontiguousarray(np.asarray(b_ada, f32)[0].reshape(96, 128).T)
    order = [c + 8 * half for c in range(8) for half in range(2)]
    tblP = np.ascontiguousarray(np.asarray(rel_bias_table, f32)[:, order])
    sinksP = np.ascontiguousarray(np.asarray(attn_sinks, f32)[0][order].reshape(1, 16))
    vgain = np.ascontiguousarray(np.asarray(gmlp_v_gain, f32)[0].reshape(1, 1024))
    bs = np.ascontiguousarray(np.asarray(gmlp_b_s, f32)[0].reshape(1, 1024))
    wsT = np.ascontiguousarray(np.asarray(gmlp_w_s, f32)[0].transpose(2, 0, 1))
    ident = np.eye(128, dtype=f32)
    tri = (np.arange(128)[:, None] <= np.arange(128)[None, :]).astype(f32)
    bi = np.arange(64)
    blkmask = ((bi[:, None] // 4 == bi[None, :] // 4) & (bi[:, None] % 4 <= bi[None, :] % 4)).astype(f32)
    oh = (_t5_bucket_np(np.arange(128))[None, :] == np.arange(32)[:, None]).astype(f32)

    shared = dict(ident=ident, tri=tri, blkmask=blkmask, oh=oh, tblP=tblP, sinksP=sinksP, gT=gT, gA=gA, gG=gG,
                  badaT=badaT, vgain=vgain, bs=bs, wsT=wsT, wada_u=wada_u, wfm_u=wfm_u, wtm_u=wtm_u,
                  wout_u=wout_u, wff1_u=wff1_u, wff2_u=wff2_u)

    in_maps = []
    for core in range(NCORES):
        bp, half = core // 2, core % 2
        xp = x_prompt[bp, half * TP:(half + 1) * TP]
        xs = x_sample[16 * core:16 * core + 16].reshape(64, D)
        halo = x_prompt[bp, TP - TH:TP] if half == 1 else np.zeros((TH, D), f32)
        xTc = np.ascontiguousarray(np.concatenate([xp, xs, halo], axis=0).T)
        cc = np.concatenate([c_prompt[bp:bp + 1], c_sample[16 * core:16 * core + 16]], axis=0)
        cTc = np.ascontiguousarray(cc.T.reshape(16, 128, 17).transpose(1, 0, 2).reshape(128, 16 * 17))
        ck = cache_k[0, 16 * core:16 * core + 16]
        ckT = np.ascontiguousarray(ck.transpose(2, 3, 0, 1).reshape(128, 16 * 128))
        cv2 = np.ascontiguousarray(cache_v[0, 16 * core:16 * core + 16].transpose(1, 0, 2, 3).reshape(128, 16 * 128))
        blk0 = np.full((128, 1), NEG if half == 0 else 0.0, f32)
        m = dict(shared)
        m.update(xT=xTc, cT=cTc, ckT=ckT, cv2=cv2, blk0=blk0)
        in_maps.append(m)

    if "nc" not in _PROGRAM:
        _PROGRAM["nc"] = build_program()
    res = run_bass_kernel_spmd(_PROGRAM["nc"], in_maps, core_ids=list(range(NCORES)))
    outs = res.results

    y_p = np.zeros((4, 2048, D), f32)
    y_s = np.zeros((128, 4, D), f32)
    nkp = np.zeros((1, 4, 128, 2, 64), f32)
    nvp = np.zeros((1, 4, 128, 2, 64), f32)
    nks = np.zeros((1, 128, 4, 2, 64), f32)
    nvs = np.zeros((1, 128, 4, 2, 64), f32)
    gvs = np.zeros((1, 128, 4, 8, 128), f32)
    for core in range(NCORES):
        bp, half = core // 2, core % 2
        o = outs[core]
        yTc = np.asarray(o["yT"])
        y_p[bp, half * TP:(half + 1) * TP] = yTc[:, 0:TP].T
        y_s[16 * core:16 * core + 16] = yTc[:, TP:TM].T.reshape(16, 4, D)
        kTo = np.asarray(o["kT_out"])
        vo = np.asarray(o["v_out"])
        if half == 1:
            nkp[0, bp] = kTo[:, 0:128].T.reshape(128, 2, 64)
            nvp[0, bp] = vo[0:128].reshape(128, 2, 64)
        nks[0, 16 * core:16 * core + 16] = kTo[:, 128:192].T.reshape(16, 4, 2, 64)
        nvs[0, 16 * core:16 * core + 16] = vo[128:192].reshape(16, 4, 2, 64)
        gvs[0, 16 * core:16 * core + 16] = np.asarray(o["gv_out"]).reshape(16, 4, 8, 128)
    return (y_p, y_s, nkp, nvp, nks, nvs, gvs)
```
